# Optimizing a Trainium2 kernel written in Bass

```python
import math
import jax, jax.numpy as jnp
from jax import lax
import numpy as np


D_MODEL = 2048
BATCH = 32
SEQ = 256
DEPTH = 2
DEC_BATCH = 8
DEC_SEQ = 1024
PAST_LEN = 256

GRID_W = 64
HEAD_DIM = 128
A_HEADS = 4
A_KV_HEADS = 2
WINDOW = 128
BLK = 128
B_HEADS = 4
B_NOPE = 128
B_ROPE = 64
B_V = 128
B_RANK = 256
C_HEADS = 4
C_KV_HEADS = 2
D_HEADS = 4
D_HALF = 64
D_FF = 4 * D_MODEL
QBLK = 128
ROPE_BASE = 10000.0
EPS = 1e-6
MOD_CHUNKS = 6

IN_SIZES = (A_HEADS * HEAD_DIM, A_KV_HEADS * HEAD_DIM, A_KV_HEADS * HEAD_DIM,
            B_HEADS * B_NOPE, B_HEADS * B_ROPE, B_RANK, B_ROPE,
            C_HEADS * HEAD_DIM, C_KV_HEADS * HEAD_DIM, C_KV_HEADS * HEAD_DIM,
            D_HEADS * 2 * D_HALF, D_HEADS * 2 * D_HALF, D_HEADS * HEAD_DIM)
IN_WIDTH = sum(IN_SIZES)
IN_OFFSETS = tuple(sum(IN_SIZES[:i]) for i in range(1, len(IN_SIZES)))

kernel_name = 'hybrid_parallel_heads_diffusion_step'


def rmsnorm(x, g):
    xf = x.astype(jnp.float32)
    y = xf * lax.rsqrt(jnp.mean(xf * xf, axis=-1, keepdims=True) + EPS)
    return (y * g.astype(jnp.float32)).astype(x.dtype)


def _rope_1d(x, pos):
    half = x.shape[-1] // 2
    inv = ROPE_BASE ** (-jnp.arange(half, dtype=jnp.float32) / half)
    ang = pos.astype(jnp.float32)[:, None] * inv[None, :]
    cos = jnp.cos(ang).astype(x.dtype)
    sin = jnp.sin(ang).astype(x.dtype)
    x1, x2 = x[..., :half], x[..., half:]
    return jnp.concatenate([x1 * cos - x2 * sin, x1 * sin + x2 * cos], axis=-1)


def rope_2d(x):
    T = x.shape[-2]
    rows = T // GRID_W
    row = jnp.repeat(jnp.arange(rows), GRID_W)
    col = jnp.arange(rows * GRID_W) % GRID_W
    h = x.shape[-1] // 2
    return jnp.concatenate([_rope_1d(x[..., :h], row), _rope_1d(x[..., h:], col)], axis=-1)


def attend(q, k, v, scale, sink=None):
    B, Hq, T, dk = q.shape
    Hk = k.shape[1]
    G = Hq // Hk
    dv = v.shape[-1]
    nb = T // QBLK
    qb = q.reshape(B, Hk, G, nb, QBLK, dk).transpose(3, 0, 1, 2, 4, 5)

    def one_block(qi):
        s = jnp.einsum('bhgqd,bhsd->bhgqs', qi, k).astype(jnp.float32) * scale
        if sink is not None:
            sk = jnp.broadcast_to(sink.astype(jnp.float32).reshape(1, Hk, G, 1, 1), s.shape[:-1] + (1,))
            p = jax.nn.softmax(jnp.concatenate([s, sk], axis=-1), axis=-1)[..., :-1]
        else:
            p = jax.nn.softmax(s, axis=-1)
        return jnp.einsum('bhgqs,bhsd->bhgqd', p.astype(v.dtype), v)

    o = lax.map(one_block, qb)
    return o.transpose(1, 2, 3, 0, 4, 5).reshape(B, Hq, T, dv)


def banded_attend(q, k, v, k_ctx, v_ctx, sink, scale):
    B, Hq, T, dk = q.shape
    Hk = k.shape[1]
    G = Hq // Hk
    dv = v.shape[-1]
    nb = T // BLK
    S = k_ctx.shape[2]
    pad = ((0, 0), (0, 0), (BLK, BLK), (0, 0))
    kp = jnp.pad(k, pad).reshape(B, Hk, nb + 2, BLK, dk)
    vp = jnp.pad(v, pad).reshape(B, Hk, nb + 2, BLK, dv)
    kb = jnp.concatenate([kp[:, :, :-2], kp[:, :, 1:-1], kp[:, :, 2:]], axis=3)
    vb = jnp.concatenate([vp[:, :, :-2], vp[:, :, 1:-1], vp[:, :, 2:]], axis=3)
    qb = q.reshape(B, Hk, G, nb, BLK, dk)
    s_loc = jnp.einsum('bhgnqd,bhnkd->bhgnqk', qb, kb).astype(jnp.float32) * scale
    qpos = jnp.arange(nb)[:, None] * BLK + jnp.arange(BLK)[None, :]
    kpos = (jnp.arange(nb)[:, None] - 1) * BLK + jnp.arange(3 * BLK)[None, :]
    ok = (jnp.abs(qpos[:, :, None] - kpos[:, None, :]) <= WINDOW) & (kpos[:, None, :] >= 0) & (kpos[:, None, :] < T)
    s_loc = jnp.where(ok, s_loc, -jnp.inf)
    s_ctx = jnp.einsum('bhgnqd,bhsd->bhgnqs', qb, k_ctx).astype(jnp.float32) * scale
    sk = jnp.broadcast_to(sink.astype(jnp.float32).reshape(1, Hk, G, 1, 1, 1), s_loc.shape[:-1] + (1,))
    p = jax.nn.softmax(jnp.concatenate([s_ctx, s_loc, sk], axis=-1), axis=-1).astype(v.dtype)
    o = (jnp.einsum('bhgnqs,bhsd->bhgnqd', p[..., :S], v_ctx)
         + jnp.einsum('bhgnqk,bhnkd->bhgnqd', p[..., S:S + 3 * BLK], vb))
    return o.reshape(B, Hq, T, dv)


def mla_expand(c_lat, k_rope, w_uk, w_uv):
    k_nope = jnp.einsum('bsr,rhd->bhsd', c_lat, w_uk)
    k = jnp.concatenate([k_nope, jnp.broadcast_to(k_rope, k_nope.shape[:3] + (B_ROPE,))], axis=-1)
    v = jnp.einsum('bsr,rhd->bhsd', c_lat, w_uv)
    return k, v


def token_mixers(h, p, l, ctx):
    B, T, _ = h.shape
    latent = ctx is not None
    pos = rope_2d if latent else (lambda u: u)
    z = jnp.einsum('btd,de->bte', h, p['w_in'])
    (aq, ak, av, bqn, bqr, bckv, bkr, cq, ck, cv, dq, dk, dv) = jnp.split(z, IN_OFFSETS, axis=-1)

    def heads(u, n):
        return u.reshape(B, T, n, -1).transpose(0, 2, 1, 3)

    def merge(o):
        return o.transpose(0, 2, 1, 3).reshape(B, T, -1)

    qa = pos(heads(aq, A_HEADS))
    ka = pos(heads(ak, A_KV_HEADS))
    va = heads(av, A_KV_HEADS)
    if latent:
        o_a = banded_attend(qa, ka, va, ctx['a_k'], ctx['a_v'], p['a_sink'], HEAD_DIM ** -0.5)
    else:
        o_a = attend(qa, ka, va, HEAD_DIM ** -0.5, p['a_sink'])

    c_lat = rmsnorm(bckv, p['b_g_kv'])
    qb_ = jnp.concatenate([heads(bqn, B_HEADS), pos(heads(bqr, B_HEADS))], axis=-1)
    kb_, vb_ = mla_expand(c_lat, pos(bkr[:, None]), p['b_w_uk'], p['b_w_uv'])
    if latent:
        kb_c, vb_c = mla_expand(ctx['mla'][..., :B_RANK], ctx['mla'][:, None, :, B_RANK:], p['b_w_uk'], p['b_w_uv'])
        kb_ = jnp.concatenate([kb_c, kb_], axis=2)
        vb_ = jnp.concatenate([vb_c, vb_], axis=2)
    o_b = attend(qb_, kb_, vb_, (B_NOPE + B_ROPE) ** -0.5)

    qc = pos(rmsnorm(heads(cq, C_HEADS), p['c_gq']))
    kc = rmsnorm(heads(ck, C_KV_HEADS), p['c_gk'])
    vc = heads(cv, C_KV_HEADS)
    if latent:
        kc_s = jnp.concatenate([ctx['c_k'], pos(kc)], axis=2)
        vc_s = jnp.concatenate([ctx['c_v'], vc], axis=2)
    else:
        kc_s, vc_s = kc, vc
    o_c = attend(qc, kc_s, vc_s, HEAD_DIM ** -0.5)

    def pos2(u):
        return jnp.concatenate([pos(u[..., :D_HALF]), pos(u[..., D_HALF:])], axis=-1)
    qd = pos2(heads(dq, D_HEADS))
    kd = heads(dk, D_HEADS)
    vd = heads(dv, D_HEADS)
    if latent:
        kd_s = jnp.concatenate([ctx['d_k'], pos2(kd)], axis=2)
        vd_s = jnp.concatenate([ctx['d_v'], vd], axis=2)
    else:
        kd_s, vd_s = kd, vd
    lam_init = 0.8 - 0.6 * math.exp(-0.3 * l)
    lamp = p['d_lam'].astype(jnp.float32)
    lam = jnp.exp(jnp.sum(lamp[0] * lamp[1])) - jnp.exp(jnp.sum(lamp[2] * lamp[3])) + lam_init
    o1 = attend(qd[..., :D_HALF], kd_s[..., :D_HALF], vd_s, D_HALF ** -0.5)
    o2 = attend(qd[..., D_HALF:], kd_s[..., D_HALF:], vd_s, D_HALF ** -0.5)
    o_d = rmsnorm(o1 - lam.astype(o1.dtype) * o2, p['d_g_out']) * (1.0 - lam_init)

    o = jnp.concatenate([merge(o_a), merge(o_b), merge(o_c), merge(o_d)], axis=-1)
    out = jnp.einsum('bte,ed->btd', o, p['w_out'])
    if latent:
        return out, None
    return out, (ka, va, jnp.concatenate([c_lat, bkr], axis=-1), kc, vc, kd, vd)


def block(x, mod, p, l, ctx):
    sh1, sc1, g1, sh2, sc2, g2 = jnp.split(mod, MOD_CHUNKS, axis=-1)
    h = rmsnorm(x, p['g_attn_pre']) * (1 + sc1) + sh1
    o, ctx_out = token_mixers(h, p, l, ctx)
    x = x + g1 * rmsnorm(o, p['g_attn_post'])
    h = rmsnorm(x, p['g_mlp_pre']) * (1 + sc2) + sh2
    f = jnp.square(jax.nn.relu(h @ p['w_up'])) @ p['w_down']
    x = x + g2 * rmsnorm(f, p['g_mlp_post'])
    return x, ctx_out


def setup_inputs(seed: int = 0) -> dict:
    key = jax.random.key(seed)
    ks = jax.random.split(key, 29)
    D = D_MODEL

    def nrm(i, shape, scale=1.0):
        return jax.random.normal(ks[i], shape, jnp.float32) * scale

    def gain(i, shape):
        return 1.0 + 0.01 * jax.random.normal(ks[i], shape, jnp.float32)

    return {
        'x_prompt': nrm(0, (BATCH, SEQ, D)),
        'x_sample': nrm(1, (DEC_BATCH, DEC_SEQ, D)),
        'cache_a_k': nrm(2, (DEC_BATCH, DEPTH, A_KV_HEADS, PAST_LEN, HEAD_DIM)),
        'cache_a_v': nrm(3, (DEC_BATCH, DEPTH, A_KV_HEADS, PAST_LEN, HEAD_DIM)),
        'cache_mla': nrm(4, (DEC_BATCH, DEPTH, PAST_LEN, B_RANK + B_ROPE)),
        'cache_c_k': nrm(5, (DEC_BATCH, DEPTH, C_KV_HEADS, PAST_LEN, HEAD_DIM)),
        'cache_c_v': nrm(6, (DEC_BATCH, DEPTH, C_KV_HEADS, PAST_LEN, HEAD_DIM)),
        'cache_d_k': nrm(7, (DEC_BATCH, DEPTH, D_HEADS, PAST_LEN, 2 * D_HALF)),
        'cache_d_v': nrm(8, (DEC_BATCH, DEPTH, D_HEADS, PAST_LEN, HEAD_DIM)),
        'c': nrm(9, (DEC_BATCH, D)),
        'c_ctx': nrm(10, (D,)),
        'w_ada': nrm(11, (DEPTH, D, MOD_CHUNKS * D), 0.5 * D ** -0.5),
        'b_ada': nrm(12, (DEPTH, MOD_CHUNKS * D), 0.01),
        'g_attn_pre': gain(13, (DEPTH, D)),
        'g_attn_post': gain(14, (DEPTH, D)),
        'g_mlp_pre': gain(15, (DEPTH, D)),
        'g_mlp_post': gain(16, (DEPTH, D)),
        'w_in': nrm(17, (DEPTH, D, IN_WIDTH), D ** -0.5),
        'a_sink': nrm(18, (DEPTH, A_HEADS), 0.5),
        'b_g_kv': gain(19, (DEPTH, B_RANK)),
        'b_w_uk': nrm(20, (DEPTH, B_RANK, B_HEADS, B_NOPE), B_RANK ** -0.5),
        'b_w_uv': nrm(21, (DEPTH, B_RANK, B_HEADS, B_V), B_RANK ** -0.5),
        'c_gq': gain(22, (DEPTH, HEAD_DIM)),
        'c_gk': gain(23, (DEPTH, HEAD_DIM)),
        'd_lam': nrm(24, (DEPTH, 4, D_HALF), 0.1),
        'd_g_out': gain(25, (DEPTH, HEAD_DIM)),
        'w_out': nrm(26, (DEPTH, D, D), D ** -0.5),
        'w_up': nrm(27, (DEPTH, D, D_FF), D ** -0.5),
        'w_down': nrm(28, (DEPTH, D_FF, D), D_FF ** -0.5),
    }


def reference(x_prompt, x_sample, cache_a_k, cache_a_v, cache_mla, cache_c_k, cache_c_v, cache_d_k, cache_d_v,
              c, c_ctx, w_ada, b_ada, g_attn_pre, g_attn_post, g_mlp_pre, g_mlp_post, w_in, a_sink,
              b_g_kv, b_w_uk, b_w_uv, c_gq, c_gk, d_lam, d_g_out, w_out, w_up, w_down):
    def layer_params(l):
        return {'g_attn_pre': g_attn_pre[l], 'g_attn_post': g_attn_post[l],
                'g_mlp_pre': g_mlp_pre[l], 'g_mlp_post': g_mlp_post[l],
                'w_in': w_in[l], 'a_sink': a_sink[l], 'b_g_kv': b_g_kv[l],
                'b_w_uk': b_w_uk[l], 'b_w_uv': b_w_uv[l], 'c_gq': c_gq[l], 'c_gk': c_gk[l],
                'd_lam': d_lam[l], 'd_g_out': d_g_out[l], 'w_out': w_out[l],
                'w_up': w_up[l], 'w_down': w_down[l]}

    x = x_prompt
    states = [[] for _ in range(7)]
    for l in range(DEPTH):
        mod = (jax.nn.silu(c_ctx) @ w_ada[l] + b_ada[l])[None, None, :]
        x, st = block(x, mod, layer_params(l), l, None)
        for lst, t in zip(states, st):
            lst.append(t)
    y_prompt = x

    x = x_sample
    for l in range(DEPTH):
        mod = (jax.nn.silu(c) @ w_ada[l] + b_ada[l])[:, None, :]
        ctx = {'a_k': cache_a_k[:, l], 'a_v': cache_a_v[:, l], 'mla': cache_mla[:, l],
               'c_k': cache_c_k[:, l], 'c_v': cache_c_v[:, l],
               'd_k': cache_d_k[:, l], 'd_v': cache_d_v[:, l]}
        x, _ = block(x, mod, layer_params(l), l, ctx)
    y_sample = x

    return (y_prompt, y_sample,
            jnp.stack(states[0], axis=1), jnp.stack(states[1], axis=1), jnp.stack(states[2], axis=1),
            jnp.stack(states[3], axis=1), jnp.stack(states[4], axis=1),
            jnp.stack(states[5], axis=1), jnp.stack(states[6], axis=1))
```

```python
import math
from contextlib import ExitStack
import numpy as np
import concourse.bass as bass
import concourse.mybir as mybir
from concourse.bass_utils import run_bass_kernel_spmd

F32 = mybir.dt.float32
BF16 = mybir.dt.bfloat16
AF = mybir.ActivationFunctionType
ALU = mybir.AluOpType
AX = mybir.AxisListType

D = 2048
KC = 16
T = 1024
NT = 8
DEPTH = 2
DFF = 8192
INW = 4672
EPS = 1e-6
NCORES = 8
O_AQ, O_AKV, O_BQN, O_BQRC, O_BKR, O_CQ, O_CKV, O_DQ, O_DK, O_DV = 0, 512, 1024, 1536, 2048, 2112, 2624, 3136, 3648, 4160

PHASES = []
EPOCH = 12000
NDSEM = 16
NSW = 5
ENGS = ("pe", "act", "dve", "pool", "sp")
SAME_SYNC = {"pe": False, "act": True, "dve": True, "pool": True, "sp": False}


class Res:
    __slots__ = ("name", "w", "rs", "persist")

    def __init__(self, name, persist=False):
        self.name = name
        self.w = None
        self.rs = {}
        self.persist = persist


class Sched:
    def __init__(self):
        self.q = {e: [] for e in ENGS}
        self.cnt = {e: 0 for e in ENGS}
        self.nop = {e: 0 for e in ENGS}
        self.dma_uses = [0] * NDSEM
        self.dma_rr = 0
        self.dma_rr_sw = 0
        self.all_res = []
        self.pend = {e: {} for e in ENGS}
        self.out_events = []
        self.rank = {}
        self.nsig = {e: 0 for e in ENGS}

    def res(self, name, persist=False):
        r = Res(name, persist)
        self.all_res.append(r)
        return r

    @staticmethod
    def _need(d, ev):
        if ev is None:
            return
        if d.get(ev[0], 0) < ev[1]:
            d[ev[0]] = ev[1]

    def _deps(self, eng, reads, writes):
        allw, raw = {}, {}
        for r in reads:
            self._need(allw, r.w)
            self._need(raw, r.w)
        for r in writes:
            self._need(allw, r.w)
            for k, v in r.rs.items():
                self._need(allw, (k, v))
        for k, v in self.pend[eng].items():
            if allw.get(k, 0) < v:
                allw[k] = v
        self.pend[eng] = {}
        return allw, raw

    def _commit(self, ev, reads, writes):
        for r in reads:
            if r.rs.get(ev[0], 0) < ev[1]:
                r.rs[ev[0]] = ev[1]
        for r in writes:
            r.w = ev
            r.rs = {}

    def op(self, eng, fn, reads=(), writes=()):
        allw, raw = self._deps(eng, reads, writes)
        self.cnt[eng] += 1
        self.nop[eng] += 1
        ev = (("e", eng), self.nop[eng])
        self.q[eng].append([fn, allw, raw, ev])
        self._commit(ev, reads, writes)
        return ev

    def dma(self, eng, out, in_, reads=(), writes=(), is_out=False):
        allw, raw = self._deps(eng, reads, writes)
        if eng == "pool":
            j = self.dma_rr_sw
            self.dma_rr_sw = (j + 1) % NSW
        else:
            j = NSW + self.dma_rr
            self.dma_rr = (self.dma_rr + 1) % (NDSEM - NSW)
        prev = self.dma_uses[j]
        self.dma_uses[j] += 1
        key = ("d", j)
        if prev > 0 and allw.get(key, 0) < 16 * prev:
            allw[key] = 16 * prev
        ev = (key, 16 * (prev + 1))
        self.cnt[eng] += 1
        self.q[eng].append([lambda e: e.dma_start(out=out, in_=in_), allw, raw, ev])
        self._commit(ev, reads, writes)
        if is_out:
            self.out_events.append(ev)
        return ev

    def barrier(self):
        waits = {}
        for r in self.all_res:
            if r.persist:
                continue
            evs = list(r.rs.items())
            if r.w is not None:
                evs.append(r.w)
            for k, v in evs:
                if waits.get(k, 0) < v:
                    waits[k] = v
            r.w = None
            r.rs = {}
        for e in ENGS:
            for k, v in waits.items():
                if self.pend[e].get(k, 0) < v:
                    self.pend[e][k] = v

    def plan_signals(self):
        needed = {e: set() for e in ENGS}
        for eng in ENGS:
            seen = {}
            for ent in self.q[eng]:
                fn, allw, raw, ev = ent
                eff = {}
                for k, v in allw.items():
                    if k[0] == "e" and k[1] == eng and not SAME_SYNC[eng]:
                        continue
                    if seen.get(k, 0) >= v:
                        continue
                    seen[k] = v
                    eff[k] = v
                    if k[0] == "e":
                        needed[k[1]].add(v)
                ent.append(eff)
        for e in ENGS:
            for r, idx in enumerate(sorted(needed[e])):
                self.rank[(e, idx)] = r
            self.nsig[e] = len(needed[e])

    def emit(self, eng, e, sems):
        for fn, allw, raw, ev, eff in self.q[eng]:
            for k, v in eff.items():
                if k[0] == "e":
                    r = self.rank[(k[1], v)]
                    e.wait_ge(sems[("e", k[1], r // EPOCH)], r % EPOCH + 1)
                else:
                    e.wait_ge(sems[k], v)
            ins = fn(e)
            if ev[0][0] == "d":
                ins.then_inc(sems[ev[0]], 16)
            else:
                r = self.rank.get((eng, ev[1]))
                if r is not None:
                    ins.then_inc(sems[("e", eng, r // EPOCH)], 1)


def _rope_tables():
    t = np.arange(T)
    row = (t // 64).astype(np.float32)
    col = (t % 64).astype(np.float32)

    def tab(half):
        inv = (np.float32(10000.0) ** (-(np.arange(half, dtype=np.float32) / np.float32(half)))).astype(np.float32)
        ar = (row[:, None] * inv[None, :]).astype(np.float32)
        ac = (col[:, None] * inv[None, :]).astype(np.float32)
        c = np.stack([np.cos(ar), np.cos(ac)], axis=1).astype(np.float32)
        s = np.stack([np.sin(ar), np.sin(ac)], axis=1).astype(np.float32)
        c = c.reshape(NT, 128, 2, half).transpose(1, 0, 2, 3)
        s = s.reshape(NT, 128, 2, half).transpose(1, 0, 2, 3)
        return np.ascontiguousarray(c), np.ascontiguousarray(s)

    c32, s32 = tab(32)
    c16, s16 = tab(16)
    return c32, s32, c16, s16


def build():
    plan = _build(None)
    return _build(plan)


def _build(plan):
    nc = bass.Bass("TRN2", target_bir_lowering=False)
    del PHASES[:]
    S = Sched()

    def din(name, shape):
        return nc.dram_tensor(name, list(shape), F32, kind="ExternalInput").ap()

    def dout(name, shape):
        return nc.dram_tensor(name, list(shape), F32, kind="ExternalOutput").ap()

    x_in = [din("xp", [T, D]), din("xs", [T, D])]
    cak = din("cak", [2, 2, 256, 128]); cav = din("cav", [2, 2, 256, 128])
    cmla = din("cmla", [2, 256, 320])
    cck = din("cck", [2, 2, 256, 128]); ccv = din("ccv", [2, 2, 256, 128])
    cdk = din("cdk", [2, 4, 256, 128]); cdv = din("cdv", [2, 4, 256, 128])
    cT_d = din("cT", [128, KC, 2])
    gT_d = din("gT", [128, 2, 4, KC])
    bT_d = din("bT", [128, 2, 96])
    w_ada = din("w_ada", [2, D, 6 * D])
    w_in = din("w_in", [2, D, INW])
    w_out = din("w_out", [2, D, D])
    w_up = din("w_up", [2, D, DFF])
    w_down = din("w_down", [2, DFF, D])
    wuk_d = din("wuk", [2, 256, 512]); wuv_d = din("wuv", [2, 256, 512])
    small_d = din("small", [128, 2, 900])
    ident_d = din("ident", [128, 128])
    masks_d = din("masks", [128, 256])
    rc32_d = din("rc32", [128, NT, 2, 32]); rs32_d = din("rs32", [128, NT, 2, 32])
    rc16_d = din("rc16", [128, NT, 2, 16]); rs16_d = din("rs16", [128, NT, 2, 16])

    y_out = [dout("yp", [T, D]), dout("ys", [T, D])]
    nak = dout("nak", [4, 2, 2, 256, 128]); nav = dout("nav", [4, 2, 2, 256, 128])
    nmla = dout("nmla", [4, 2, 256, 320])
    nck = dout("nck", [4, 2, 2, 256, 128]); ncv = dout("ncv", [4, 2, 2, 256, 128])
    ndk = dout("ndk", [4, 2, 4, 256, 128]); ndv = dout("ndv", [4, 2, 4, 256, 128])
    xscr = [nc.dram_tensor("xscr%d" % i, [KC, 128, T], F32, kind="Internal").ap() for i in range(2)]
    XSC = [[S.res("xscr%d_%d" % (i, c)) for c in range(KC)] for i in range(2)]

    es = ExitStack()
    with es:
        def sb(name, shape, dt):
            return es.enter_context(nc.sbuf_tensor(name, list(shape), dt))

        def ps(name, shape, dt):
            return es.enter_context(nc.psum_tensor(name, list(shape), dt))

        es.enter_context(nc.allow_low_precision("bf16 matmul operands, fp32 accumulation"))
        WRt = sb("WR", [128, 3, 8192], BF16)
        Ht = sb("H", [128, KC, T], BF16)
        OTt = sb("OT", [128, KC, T], BF16)
        BIGt = sb("BIG", [128, KC, T], F32)
        RSt = sb("RS", [128, T], F32)
        identF = sb("identF", [128, 128], F32)
        identB = sb("identB", [128, 128], BF16)
        onesF = sb("onesF", [128, 128], F32)
        onesB = sb("onesB", [128, 128], BF16)
        maskB = sb("maskB", [128, 256], BF16)
        rc32 = sb("rc32s", [128, NT, 2, 32], F32); rs32 = sb("rs32s", [128, NT, 2, 32], F32)
        rc16 = sb("rc16s", [128, NT, 2, 16], F32); rs16 = sb("rs16s", [128, NT, 2, 16], F32)
        cTs = sb("cTs", [128, KC, 2], F32)
        sTs = sb("sTs", [128, KC, 2], BF16)
        gTs = sb("gTs", [128, 2, 4, KC], F32)
        bTs = sb("bTs", [128, 2, 96], F32)
        modT = sb("modT", [128, 2, 96, 2], F32)
        PAR = sb("PAR", [128, 2, 6, KC], F32)
        smallS = sb("smallS", [128, 900], F32)
        wukS = sb("wukS", [128, 2, 512], BF16); wuvS = sb("wuvS", [128, 2, 512], BF16)
        SC = sb("SC", [128, 64], F32)
        epsT = sb("epsT", [128, 1], F32)
        DGt = sb("DG", [128, 128], F32)
        LAMt = sb("LAMt", [128, 128], F32)

        STp = [ps("ST%d" % i, [128, 512], F32) for i in range(2)]
        Op = ps("Oacc", [128, 2, 512], F32)
        Zpair = ps("Zpair", [128, 2, 512], F32)
        Zp = [Zpair[:, 0, :], Zpair[:, 1, :]]
        TRp = ps("TRB", [128, 1024], BF16)
        Mp = ps("MISC", [128, 512], F32)
        R_ST = [S.res("ST%d" % i) for i in range(2)]
        R_O = S.res("O")
        R_Z = [S.res("Z%d" % i) for i in range(2)]
        R_TR = S.res("TR")
        R_M = S.res("M")

        BANKS6 = [(Zp[0], R_Z[0]), (Zp[1], R_Z[1]), (STp[0], R_ST[0]), (STp[1], R_ST[1]), (Mp, R_M)]
        BANKS3 = [(Zp[0], R_Z[0]), (Zp[1], R_Z[1]), (Mp, R_M)]

        def bank6():
            return BANKS6[nxt("bk6", len(BANKS6))]

        def bank3():
            return BANKS3[nxt("bk3", len(BANKS3))]

        R_WR = [S.res("WR%d" % i, persist=True) for i in range(3)]
        R_H = S.res("H"); R_OT = S.res("OT"); R_RS = S.res("RS")
        R_BIG = [S.res("BIG%d" % c) for c in range(KC)]
        R_const = S.res("const"); R_mod = S.res("mod"); R_par = S.res("par")
        R_lam = S.res("lam"); R_small = S.res("small"); R_wu = S.res("wu"); R_sc = S.res("sc"); R_dg = S.res("dg")

        OTf = OTt[:].bitcast(F32)
        OTflat = OTt[:].rearrange("p c t -> p (c t)").bitcast(F32)
        XC = [OTflat[:, i * 1024:(i + 1) * 1024] for i in range(2)]
        SQh = [OTflat[:, 2048 + i * 512: 2048 + (i + 1) * 512] for i in range(2)]
        SQbf = [SQh[i].bitcast(BF16) for i in range(2)]
        R_XC = [S.res("XC%d" % i) for i in range(2)]
        R_SQ = [S.res("SQ%d" % i) for i in range(2)]
        OTb = OTt[:].rearrange("p c t -> p (c t)")
        ACTT = [OTb[:, 6144 + i * 4096: 6144 + (i + 1) * 4096].rearrange("p (f t) -> p f t", f=4) for i in range(2)]
        R_ACTT = [S.res("ACTT%d" % i) for i in range(2)]
        RT = [OTflat[:, 7168 + i * 512: 7168 + (i + 1) * 512] for i in range(2)]
        R_RT = [S.res("RT%d" % i) for i in range(2)]

        BIGf = BIGt[:].rearrange("p c t -> p (c t)")
        BIGb = BIGf.bitcast(BF16)
        off = [0]

        def carve_b(n):
            a = BIGb[:, off[0]: off[0] + n]
            off[0] += n
            return a

        QA = carve_b(4096).rearrange("p (h t) -> p h t", h=4)
        QB = carve_b(2048).rearrange("p (h t) -> p h t", h=2)
        KA = carve_b(5120).rearrange("p (h t) -> p h t", h=4)
        KB = carve_b(1280)
        CL = carve_b(2560).rearrange("p (h t) -> p h t", h=2)
        VA = carve_b(5200).rearrange("p (k g d) -> p k g d", k=10, g=4)
        PT = [carve_b(512) for _ in range(3)]
        ZB = [carve_b(512) for _ in range(4)]
        OB = [carve_b(512).rearrange("p (j d) -> p j d", j=4) for _ in range(2)]
        CST = carve_b(1024).rearrange("p (k g d) -> p k g d", k=2, g=4)
        CM = carve_b(640).rearrange("p (k c) -> p k c", k=2)
        assert off[0] % 2 == 0
        foff = [off[0] // 2]

        def carve_f(n):
            a = BIGf[:, foff[0]: foff[0] + n]
            foff[0] += n
            return a

        ZS = [carve_f(512) for _ in range(3)]
        ZT = [carve_f(256) for _ in range(2)]
        O1S = carve_f(512).rearrange("p (j d) -> p j d", j=4)
        SQ2 = carve_f(512)
        assert foff[0] <= 16384, foff[0]
        R_QA = S.res("QA"); R_QB = S.res("QB"); R_KA = S.res("KA"); R_KB = S.res("KB"); R_CL = S.res("CL")
        R_VA = S.res("VA")
        R_PT = [S.res("PT%d" % i) for i in range(3)]
        R_ZB = [S.res("ZB%d" % i) for i in range(4)]
        R_OB = [S.res("OB%d" % i) for i in range(2)]
        R_CST = S.res("CST"); R_CM = S.res("CM")
        R_ZS = [S.res("ZS%d" % i) for i in range(3)]
        R_ZT = [S.res("ZT%d" % i) for i in range(2)]
        R_ZTP = [S.res("ZTP%d" % i) for i in range(2)]
        R_O1S = S.res("O1S"); R_SQ2 = S.res("SQ2")
        rr = {"zs": 0, "zb": 0, "z": 0, "pt": 0, "st": 0, "ob": 0, "xc": 0, "sq": 0, "bk6": 0, "bk3": 0, "oset": 0}

        def nxt(key, n):
            v = rr[key]
            rr[key] = (v + 1) % n
            return v

        def mm(out, lhsT, rhs, start, stop, reads, writes, sgc=False):
            if sgc:
                return S.op("pe", lambda e: e.matmul(out, lhsT, rhs, start=start, stop=stop, skip_group_check=True),
                            reads, writes)
            return S.op("pe", lambda e: e.matmul(out, lhsT, rhs, start=start, stop=stop), reads, writes)

        def tr(out, in_, ident, reads, writes):
            return S.op("pe", lambda e: e.transpose(out, in_, ident), reads, writes)

        def act(out, in_, func, reads, writes, bias=None, scale=None, accum=None):
            kw = {}
            if bias is not None:
                kw["bias"] = bias
            if scale is not None:
                kw["scale"] = scale
            if accum is not None:
                kw["accum_out"] = accum
            return S.op("act", lambda e: e.activation(out, in_, func, **kw), reads, writes)

        def tt(eng, out, a, b, op, reads, writes):
            return S.op(eng, lambda e: e.tensor_tensor(out, a, b, op), reads, writes)

        def ts(eng, out, a, s1, op0, reads, writes, s2=None, op1=None):
            if s2 is None:
                return S.op(eng, lambda e: e.tensor_scalar(out, a, s1, None, op0), reads, writes)
            return S.op(eng, lambda e: e.tensor_scalar(out, a, s1, s2, op0, op1), reads, writes)

        def stt(eng, out, a, s, b, op0, op1, reads, writes):
            return S.op(eng, lambda e: e.scalar_tensor_tensor(out, a, s, b, op0, op1), reads, writes)

        def cp(eng, out, in_, reads, writes):
            if eng == "act":
                return S.op("act", lambda e: e.copy(out, in_), reads, writes)
            return S.op(eng, lambda e: e.tensor_copy(out, in_), reads, writes)

        def recip(out, in_, reads, writes):
            return S.op("dve", lambda e: e.reciprocal(out, in_), reads, writes)

        def memset(eng, ap, v, writes):
            return S.op(eng, lambda e: e.memset(ap, v), (), writes)

        def w_desc(key):
            kind = key[0]
            if kind == "ada":
                _, l, g = key
                return (w_ada[l][:, g * 512:(g + 1) * 512].rearrange("(k p) c -> p k c", p=128), 8192, (16, 512))
            if kind == "in":
                _, l, pa, c0 = key
                n = 64 if c0 == O_BKR else 512
                return (w_in[l][:, c0:c0 + n].rearrange("(k p) c -> p k c", p=128), 16 * n, (16, n))
            if kind == "out":
                _, l, pa, cg = key
                return (w_out[l][:, cg * 512:(cg + 1) * 512].rearrange("(k p) c -> p k c", p=128), 8192, (16, 512))
            if kind == "up":
                _, l, pa, fg = key
                return (w_up[l][:, fg * 512:(fg + 1) * 512].rearrange("(k p) c -> p k c", p=128), 8192, (16, 512))
            _, l, pa, fg = key
            return (w_down[l][fg * 512:(fg + 1) * 512, :].rearrange("(f p) d -> p f d", p=128), 8192, (4, D))

        wkeys = list(plan) if plan is not None else []
        wseq = [w_desc(k) for k in wkeys]
        wstate = {"next_issue": 0, "next_get": 0}

        def w_issue_upto(n):
            while wstate["next_issue"] < min(n, len(wseq)):
                i = wstate["next_issue"]
                dram_ap, nel, shp = wseq[i]
                slot = i % 3
                dst = WRt[:, slot, 0:nel].rearrange("p (a b) -> p a b", a=shp[0])
                S.dma("pool", dst, dram_ap, (), (R_WR[slot],))
                wstate["next_issue"] += 1

        def w_get(key):
            i = wstate["next_get"]
            wstate["next_get"] += 1
            if plan is None:
                wkeys.append(key)
                dram_ap, nel, shp = w_desc(key)
            else:
                assert wkeys[i] == key, (i, wkeys[i], key)
                w_issue_upto(i + 3)
                dram_ap, nel, shp = wseq[i]
            slot = i % 3
            return WRt[:, slot, 0:nel].rearrange("p (a b) -> p a b", a=shp[0]), R_WR[slot]

        S.dma("sp", identF[:], ident_d, (), (R_const,))
        S.dma("pool", identB[:], ident_d, (), (R_const,))
        S.dma("pool", maskB[:], masks_d, (), (R_const,))
        S.dma("sp", rc32[:], rc32_d, (), (R_const,)); S.dma("sp", rs32[:], rs32_d, (), (R_const,))
        S.dma("sp", rc16[:], rc16_d, (), (R_const,)); S.dma("sp", rs16[:], rs16_d, (), (R_const,))
        S.dma("sp", cTs[:], cT_d, (), (R_const,))
        S.dma("sp", gTs[:], gT_d, (), (R_const,))
        S.dma("sp", bTs[:], bT_d, (), (R_const,))
        memset("dve", onesF[:], 1.0, (R_const,))
        memset("dve", onesB[:], 1.0, (R_const,))
        memset("dve", epsT[:], EPS, (R_const,))
        act(sTs[:], cTs[:], AF.Silu, (R_const,), (R_const,))
        if plan is not None:
            w_issue_upto(3)

        def stats_rstd(get_chunk, width, scale_inv):
            for c in range(KC):
                src, rds = get_chunk(c)
                i = c % 2
                act(SQbf[i], src, AF.Square, rds, (R_SQ[i],))
                for th in range(2):
                    mm(Zp[th][:], onesB[:], SQbf[i][:, th * 512:(th + 1) * 512], c == 0, c == KC - 1,
                       (R_SQ[i], R_const), (R_Z[th],))
            for th in range(2):
                act(RSt[:, th * 512:(th + 1) * 512], Zp[th][:], AF.Sqrt, (R_Z[th], R_const), (R_RS,),
                    bias=epsT[:, 0:1], scale=scale_inv)
            recip(RSt[:], RSt[:], (R_RS,), (R_RS,))

        def load_xc(pa, c):
            i = nxt("xc", 2)
            S.dma("sp", XC[i], xscr[pa][c], (XSC[pa][c],), (R_XC[i],))
            return i

        def apply_mod(pa, gi, si, c, src, rds):
            for th in range(2):
                i = nxt("sq", 2)
                hs = slice(th * 512, (th + 1) * 512)
                tt("dve", SQh[i], src[:, hs], RSt[:, hs], ALU.mult, tuple(rds) + (R_RS,), (R_SQ[i],))
                act(Ht[:, c, hs], SQh[i], AF.Identity, (R_SQ[i], R_par), (R_H,),
                    bias=PAR[:, pa, si, c:c + 1], scale=PAR[:, pa, gi, c:c + 1])

        def norm_from_scratch(pa, gi, si):
            def gc(c):
                i = load_xc(pa, c)
                return XC[i], (R_XC[i],)
            stats_rstd(gc, T, 1.0 / D)
            for c in range(KC):
                i = load_xc(pa, c)
                apply_mod(pa, gi, si, c, XC[i], (R_XC[i],))

        def post_residual(pa, ggi, to_scratch):
            def gc(c):
                return BIGt[:, c, :], (R_BIG[c],)
            stats_rstd(gc, T, 1.0 / D)
            for c in range(KC):
                i = load_xc(pa, c)
                for th in range(2):
                    q = nxt("sq", 2)
                    hs = slice(th * 512, (th + 1) * 512)
                    tt("dve", SQh[q], BIGt[:, c, hs], RSt[:, hs], ALU.mult, (R_BIG[c], R_RS), (R_SQ[q],))
                    stt("dve", BIGt[:, c, hs], SQh[q], PAR[:, pa, ggi, c:c + 1], XC[i][:, hs], ALU.mult, ALU.add,
                        (R_SQ[q], R_par, R_XC[i]), (R_BIG[c],))
                if to_scratch:
                    S.dma("sp", xscr[pa][c], BIGt[:, c, :], (R_BIG[c],), (XSC[pa][c],))

        def norm_from_big(pa, gi, si):
            def gc(c):
                return BIGt[:, c, :], (R_BIG[c],)
            stats_rstd(gc, T, 1.0 / D)
            for c in range(KC):
                apply_mod(pa, gi, si, c, BIGt[:, c, :], (R_BIG[c],))

        def phase_xt(pa):
            XIN = [BIGf[:, i * 2048:(i + 1) * 2048] for i in range(4)]
            XTS = [BIGf[:, 8192 + i * 2048: 8192 + (i + 1) * 2048].rearrange("p (c t) -> p c t", c=KC) for i in range(4)]
            R_XIN = [S.res("XIN%d" % i) for i in range(4)]
            R_XTS = [S.res("XTS%d" % i) for i in range(4)]
            for t_ in range(NT):
                b = t_ % 4
                S.dma("sp", XIN[b], x_in[pa][t_ * 128:(t_ + 1) * 128, :], (), (R_XIN[b],))
                for cg in range(4):
                    z = nxt("z", 2)
                    for j in range(4):
                        c = cg * 4 + j
                        tr(Zp[z][:, j * 128:(j + 1) * 128], XIN[b][:, c * 128:(c + 1) * 128], identF[:],
                           (R_XIN[b], R_const), (R_Z[z],))
                    cp("dve" if cg % 2 == 0 else "act", XTS[b][:, cg * 4:(cg + 1) * 4, :],
                       Zp[z][:].rearrange("p (j t) -> p j t", j=4), (R_Z[z],), (R_XTS[b],))
                S.dma("sp", xscr[pa][:, :, t_ * 128:(t_ + 1) * 128].rearrange("c p t -> p c t"), XTS[b],
                      (R_XTS[b],), tuple(XSC[pa]))
            S.barrier()

        def ada_group(l, g):
            wv, rw = w_get(("ada", l, g))
            for j in range(4):
                for k in range(KC):
                    mm(Mp[:, j * 2:(j + 1) * 2], wv[:, k, j * 128:(j + 1) * 128], sTs[:, k, :], k == 0, k == KC - 1,
                       (rw, R_const), (R_M,))
            for j in range(4):
                m = g * 4 + j
                ts("dve", modT[:, l, m, :], Mp[:, j * 2:(j + 1) * 2], bTs[:, l, m:m + 1], ALU.add,
                   (R_M, R_const), (R_mod,))

        ada_pending = []

        def ada_pump(n=1):
            for _ in range(n):
                if ada_pending:
                    la, g = ada_pending.pop(0)
                    ada_group(la, g)

        def ada_slots(l, pa, pi):
            if l == 0 and pa == 0:
                return [(0, 8 + 2 * pi), (0, 9 + 2 * pi)] if pi < 8 else []
            if l == 0 and pa == 1:
                if pi < 4:
                    return [(1, 3 * pi + i) for i in range(3)]
                return [(1, 12 + 2 * (pi - 4) + i) for i in range(2)]
            return []

        def par_rows(l, rows):
            for pa in range(2):
                def mch(i):
                    return modT[:, l, i * 16:(i + 1) * 16, pa]
                rds = (R_mod, R_const)
                if 0 in rows:
                    stt("dve", PAR[:, pa, 0, :], mch(1), 1.0, gTs[:, l, 0, :], ALU.add, ALU.mult, rds, (R_par,))
                if 1 in rows:
                    cp("dve", PAR[:, pa, 1, :], mch(0), rds, (R_par,))
                if 2 in rows:
                    tt("dve", PAR[:, pa, 2, :], mch(2), gTs[:, l, 1, :], ALU.mult, rds, (R_par,))
                if 3 in rows:
                    stt("dve", PAR[:, pa, 3, :], mch(4), 1.0, gTs[:, l, 2, :], ALU.add, ALU.mult, rds, (R_par,))
                if 4 in rows:
                    cp("dve", PAR[:, pa, 4, :], mch(3), rds, (R_par,))
                if 5 in rows:
                    tt("dve", PAR[:, pa, 5, :], mch(5), gTs[:, l, 3, :], ALU.mult, rds, (R_par,))

        def phase_ada(l):
            if l == 0:
                for g in range(8):
                    ada_group(0, g)
                par_rows(0, (0, 1))
            else:
                par_rows(1, (0, 1, 2, 3, 4, 5))
            S.dma("sp", smallS[:], small_d[:, l, :], (), (R_small,))
            S.dma("pool", wukS[:], wuk_d[l].rearrange("(c p) n -> p c n", p=128), (), (R_wu,))
            S.dma("pool", wuvS[:], wuv_d[l].rearrange("(c p) n -> p c n", p=128), (), (R_wu,))
            act(SC[:, 0:4], smallS[:, 0:4], AF.Exp, (R_small,), (R_sc,))
            lam_init = 0.8 - 0.6 * math.exp(-0.3 * l)
            dl = smallS[:, 644:900]
            tt("dve", LAMt[:, 0:64], dl[:, 0:64], dl[:, 64:128], ALU.mult, (R_small,), (R_lam,))
            tt("dve", LAMt[:, 64:128], dl[:, 128:192], dl[:, 192:256], ALU.mult, (R_small,), (R_lam,))
            S.op("dve", lambda e: e.tensor_reduce(SC[:, 8:10], LAMt[:].rearrange("p (a b) -> p a b", a=2), AX.X, ALU.add),
                 (R_lam,), (R_sc,))
            act(SC[:, 8:10], SC[:, 8:10], AF.Exp, (R_sc,), (R_sc,))
            tt("dve", SC[:, 10:11], SC[:, 9:10], SC[:, 8:9], ALU.subtract, (R_sc,), (R_sc,))
            ts("dve", SC[:, 10:11], SC[:, 10:11], -lam_init, ALU.add, (R_sc,), (R_sc,))
            ts("dve", DGt[:], smallS[:, 516:644], 1.0 - lam_init, ALU.mult, (R_small,), (R_dg,))
            S.barrier()

        Opv = Op[:].rearrange("p b (j c) -> p (b j) c", j=2)
        Zpv = Zpair[:].rearrange("p b (j c) -> p (b j) c", j=2)
        OSETS = [(Opv, (R_O,)), (Zpv, (R_Z[0], R_Z[1]))]

        def phase_attn(l, pa):
            isS = (pa == 1)
            koff = 2 if isS else 0
            nkt = 10 if isS else 8
            if isS:
                seqs = [(list(range(8)), list(range(10)))]
            else:
                seqs = [([2 * s_, 2 * s_ + 1], [2 * s_, 2 * s_ + 1]) for s_ in range(4)]
            memset("dve", VA[:, :, :, 128:129], 1.0, (R_VA,))
            if l == 0 and pa == 0:
                ada_pending.extend((0, g) for g in range(8, 24))
            if l == 0 and pa == 1:
                ada_pending.extend((1, g) for g in range(24))

            def proj(c0, n, handler):
                wv, rw = w_get(("in", l, pa, c0))
                pend2 = []
                for t_ in range(NT):
                    zb_, rz_ = bank3()
                    for k in range(KC):
                        mm(zb_[:, 0:n], Ht[:, k, t_ * 128:(t_ + 1) * 128], wv[:, k, :], k == 0, k == KC - 1,
                           (R_H, rw), (rz_,))
                    st2 = []
                    handler(t_, zb_, rz_, st2)
                    for f_ in pend2:
                        f_()
                    pend2 = st2
                for f_ in pend2:
                    f_()
                ada_pump(1)

            pidx = [0]

            def zs_load(zp, rz, n):
                zi = nxt("zs", 3)
                cp("act", ZS[zi][:, 0:n], zp[:, 0:n], (rz,), (R_ZS[zi],))
                return zi

            def rope(t_, src, dst, W, hw, rsrc, wdst):
                G = W // (4 * hw)
                ctab = (rc32 if hw == 32 else rc16)[:, t_]
                stab = (rs32 if hw == 32 else rs16)[:, t_]
                cb = ctab.unsqueeze(1).broadcast_to([128, G, 2, hw])
                sb_ = stab.unsqueeze(1).broadcast_to([128, G, 2, hw])
                s5 = src.rearrange("p (g b h f) -> p g b h f", g=G, b=2, h=2)
                d5 = dst.rearrange("p (g b h f) -> p g b h f", g=G, b=2, h=2)
                x1 = s5[:, :, :, 0, :]
                x2 = s5[:, :, :, 1, :]
                en, ta, tb, ra, rb_ = "dve", ZT[0], ZT[1], R_ZT[0], R_ZT[1]
                t1 = ta[:, 0:W // 2].rearrange("p (g b f) -> p g b f", g=G, b=2)
                t2 = tb[:, 0:W // 2].rearrange("p (g b f) -> p g b f", g=G, b=2)
                rd = tuple(rsrc) + (R_const,)
                tt(en, t1, x1, cb, ALU.mult, rd, (ra,))
                tt(en, t2, x2, sb_, ALU.mult, rd, (rb_,))
                tt(en, d5[:, :, :, 0, :], t1, t2, ALU.subtract, (ra, rb_), wdst)
                tt(en, t1, x2, cb, ALU.mult, rd, (ra,))
                tt(en, t2, x1, sb_, ALU.mult, rd, (rb_,))
                tt(en, d5[:, :, :, 1, :], t1, t2, ALU.add, (ra, rb_), wdst)

            cpe = [0]

            def cpalt():
                cpe[0] ^= 1
                return "act" if cpe[0] else "dve"

            def tr_to(zb, n, dst3, wres):
                for j in range(n):
                    tr(TRp[:, j * 128:(j + 1) * 128], ZB[zb][:, j * 128:(j + 1) * 128], identB[:],
                       (R_ZB[zb], R_const), (R_TR,))
                cp(cpalt(), dst3, TRp[:, 0:n * 128].rearrange("p (j t) -> p j t", j=n), (R_TR,), (wres,))

            def kq_path(t_, zi, c0, w, hw, dst3, wres, st2):
                zb = nxt("zb", 4)
                src = ZS[zi][:, c0:c0 + w]
                if isS and hw is not None:
                    rope(t_, src, ZB[zb][:, 0:w], w, hw, (R_ZS[zi],), (R_ZB[zb],))
                else:
                    cp("dve", ZB[zb][:, 0:w], src, (R_ZS[zi],), (R_ZB[zb],))
                st2.append(lambda: tr_to(zb, w // 128, dst3, wres))

            def v_path(t_, zi, c0, G):
                cp("dve", VA[:, koff + t_, 0:G, 0:128], ZS[zi][:, c0:c0 + G * 128].rearrange("p (g d) -> p g d", g=G),
                   (R_ZS[zi],), (R_VA,))

            def out_heads(t_, zi, c0, G, dram):
                s_ = t_ // 2
                r0 = (t_ % 2) * 128
                S.dma("sp", dram[s_, l, :, r0:r0 + 128, :].rearrange("g t d -> t g d"),
                      ZS[zi][:, c0:c0 + G * 128].rearrange("p (g d) -> p g d", g=G), (R_ZS[zi],), (), is_out=True)

            def tok(t_):
                return slice(t_ * 128, (t_ + 1) * 128)

            def ktok(t_):
                return slice((koff + t_) * 128, (koff + t_ + 1) * 128)

            def ctx_k_heads(dram_k, G):
                for kt in range(2):
                    S.dma("pool", CST[:, kt, 0:G, :], dram_k[l][:, kt * 128:(kt + 1) * 128, :].rearrange("g p d -> p g d"),
                          (), (R_CST,))
                for kt in range(2):
                    for g in range(G):
                        tr(TRp[:, (kt * G + g) * 128:(kt * G + g + 1) * 128], CST[:, kt, g, :], identB[:],
                           (R_CST, R_const), (R_TR,))
                cp(cpalt(), KA[:, 0:G, 0:256].rearrange("p g (k t) -> p k g t", k=2),
                   TRp[:, 0:2 * G * 128].rearrange("p (k g t) -> p k g t", k=2, g=G), (R_TR,), (R_KA,))

            def ctx_v_heads(dram_v, G):
                for kt in range(2):
                    S.dma("pool", VA[:, kt, 0:G, 0:128], dram_v[l][:, kt * 128:(kt + 1) * 128, :].rearrange("g p d -> p g d"),
                          (), (R_VA,))

            fin_pend = [None]

            def run_attn(hlist, parts, vap, scale, fin, banded=False):
                for (qt_list, kt_list) in seqs:
                    if banded:
                        chunks = [[q] for q in qt_list]
                    else:
                        chunks = [qt_list[i:i + 4] for i in range(0, len(qt_list), 4)]
                    for hh in hlist:
                        for ch in chunks:
                            q0 = ch[0] * 128
                            w = len(ch) * 128
                            if banded:
                                i = ch[0]
                                kts = [(0, None), (1, None)]
                                for j in (i - 1, i, i + 1):
                                    if 0 <= j < 8:
                                        kts.append((2 + j, 0 if j == i - 1 else (1 if j == i + 1 else None)))
                            else:
                                kts = [(k, None) for k in kt_list]
                            Ov, Ores = OSETS[nxt("oset", 2)]

                            def pv(n_, kt, p_, nk=len(kts), nch=len(ch), hh=hh, Ov=Ov, Ores=Ores):
                                for j in range(nch):
                                    mm(Ov[:, j, 0:129], PT[p_][:, j * 128:(j + 1) * 128], vap(hh, kt),
                                       n_ == 0 and j % 2 == 0, n_ == nk - 1, (R_PT[p_], R_VA), Ores, sgc=True)
                            pend = None
                            for n_, (kt, mk) in enumerate(kts):
                                s_ = nxt("st", 2)
                                pl = parts(hh, kt, q0, w)
                                for pi, (lh, rh) in enumerate(pl):
                                    mm(STp[s_][:, 0:w], lh, rh, pi == 0, pi == len(pl) - 1,
                                       (R_KA, R_KB, R_QA, R_QB), (R_ST[s_],))
                                p_ = nxt("pt", 3)
                                act(PT[p_][:, 0:w], STp[s_][:, 0:w], AF.Exp, (R_ST[s_],), (R_PT[p_],), scale=scale)
                                if mk is not None:
                                    tt("dve", PT[p_][:, 0:128], PT[p_][:, 0:128], maskB[:, mk * 128:(mk + 1) * 128],
                                       ALU.mult, (R_PT[p_], R_const), (R_PT[p_],))
                                if n_ == 1 and fin_pend[0] is not None:
                                    fin_pend[0]()
                                    fin_pend[0] = None
                                if pend is not None:
                                    pv(*pend)
                                pend = (n_, kt, p_)
                            pv(*pend)
                            if fin_pend[0] is not None:
                                fin_pend[0]()
                            fin_pend[0] = fin(hh, ch, Ov, Ores)
                        ada_pump(1)
                if fin_pend[0] is not None:
                    fin_pend[0]()
                    fin_pend[0] = None

            def fin_std(e_of, sink):
                def f(hh, ch, Ov, Ores):
                    n = len(ch)
                    q0 = ch[0] * 128
                    w = n * 128
                    ob = nxt("ob", 2)
                    if sink:
                        ts("dve", SC[:, 32:32 + n], Ov[:, 0:n, 128], SC[:, hh:hh + 1], ALU.add, Ores + (R_sc,), (R_sc,))
                    else:
                        cp("dve", SC[:, 32:32 + n], Ov[:, 0:n, 128], Ores, (R_sc,))
                    recip(SC[:, 32:32 + n], SC[:, 32:32 + n], (R_sc,), (R_sc,))
                    tt("dve", OB[ob][:, 0:n, :], Ov[:, 0:n, 0:128],
                       SC[:, 32:32 + n].unsqueeze(2).broadcast_to([128, n, 128]), ALU.mult, Ores + (R_sc,), (R_OB[ob],))
                    def s2():
                        for j in range(n):
                            tr(TRp[:, j * 128:(j + 1) * 128], OB[ob][:, j, :], identB[:], (R_OB[ob], R_const), (R_TR,))
                        cp(cpalt(), OTt[:, e_of(hh), q0:q0 + w], TRp[:, 0:w], (R_TR,), (R_OT,))
                    return s2
                return f

            def hA1(t_, zp, rz, st2):
                zi = zs_load(zp, rz, 512)
                kq_path(t_, zi, 0, 512, 32, QA[:, 0:4, tok(t_)], R_QA, st2)

            def hA2(t_, zp, rz, st2):
                zi = zs_load(zp, rz, 512)
                if not isS:
                    out_heads(t_, zi, 0, 2, nak)
                    out_heads(t_, zi, 256, 2, nav)
                kq_path(t_, zi, 0, 256, 32, KA[:, 0:2, ktok(t_)], R_KA, st2)
                v_path(t_, zi, 256, 2)

            mark('A_proj_%d%d' % (l, pa))
            proj(O_AQ, 512, hA1)
            proj(O_AKV, 512, hA2)
            if isS:
                ctx_k_heads(cak, 2)
                ctx_v_heads(cav, 2)
            mark('A_attn_%d%d' % (l, pa))
            run_attn(list(range(4)),
                     lambda h, kt, q0, w: [(KA[:, h // 2, kt * 128:(kt + 1) * 128], QA[:, h, q0:q0 + w])],
                     lambda h, kt: VA[:, kt, h // 2, 0:129], 128.0 ** -0.5, fin_std(lambda h: h, True), banded=isS)

            def hB1(t_, zp, rz, st2):
                zi = zs_load(zp, rz, 512)
                kq_path(t_, zi, 0, 512, None, QA[:, 0:4, tok(t_)], R_QA, st2)

            def rms_rows(srcv, G, width, ssl, gain_bc, rsrc, wres):
                act(SQ2[:, 0:G * width].rearrange("p (g d) -> p g d", g=G), srcv, AF.Square, rsrc, (R_SQ2,))
                S.op("dve", lambda e: e.tensor_reduce(SC[:, ssl], SQ2[:, 0:G * width].rearrange("p (g d) -> p g d", g=G),
                                                      AX.X, ALU.add), (R_SQ2,), (R_sc,))
                act(SC[:, ssl], SC[:, ssl], AF.Sqrt, (R_sc, R_const), (R_sc,), bias=epsT[:, 0:1], scale=1.0 / width)
                recip(SC[:, ssl], SC[:, ssl], (R_sc,), (R_sc,))
                tt("dve", srcv, srcv, SC[:, ssl].unsqueeze(2).broadcast_to([128, G, width]), ALU.mult,
                   tuple(rsrc) + (R_sc,), wres)
                tt("dve", srcv, srcv, gain_bc, ALU.mult, tuple(rsrc) + (R_small,), wres)

            def hB2(t_, zp, rz, st2):
                zi = zs_load(zp, rz, 512)
                kq_path(t_, zi, 0, 256, 16, QB[:, 0:2, tok(t_)], R_QB, st2)
                cv = ZS[zi][:, 256:512].rearrange("p (g d) -> p g d", g=1)
                rms_rows(cv, 1, 256, slice(20, 21), smallS[:, 4:260].rearrange("p (g d) -> p g d", g=1),
                         (R_ZS[zi],), (R_ZS[zi],))
                if not isS:
                    s_ = t_ // 2
                    r0 = (t_ % 2) * 128
                    S.dma("sp", nmla[s_, l, r0:r0 + 128, 0:256], ZS[zi][:, 256:512], (R_ZS[zi],), (), is_out=True)
                zb = nxt("zb", 4)
                cp("dve", ZB[zb][:, 0:256], ZS[zi][:, 256:512], (R_ZS[zi],), (R_ZB[zb],))
                st2.append(lambda: tr_to(zb, 2, CL[:, 0:2, ktok(t_)], R_CL))

            def hB3(t_, zp, rz, st2):
                zi = zs_load(zp, rz, 64)
                if not isS:
                    s_ = t_ // 2
                    r0 = (t_ % 2) * 128
                    S.dma("sp", nmla[s_, l, r0:r0 + 128, 256:320], ZS[zi][:, 0:64], (R_ZS[zi],), (), is_out=True)
                zb = nxt("zb", 4)
                if isS:
                    rope(t_, ZS[zi][:, 0:64], ZB[zb][:, 0:64], 64, 16, (R_ZS[zi],), (R_ZB[zb],))
                else:
                    cp("dve", ZB[zb][:, 0:64], ZS[zi][:, 0:64], (R_ZS[zi],), (R_ZB[zb],))
                cp("dve", ZB[zb][:, 64:128], ZB[zb][:, 0:64], (R_ZB[zb],), (R_ZB[zb],))

                def s2():
                    tr(TRp[:, 0:128], ZB[zb][:, 0:128], identB[:], (R_ZB[zb], R_const), (R_TR,))
                    cp(cpalt(), KB[:, ktok(t_)], TRp[:, 0:128], (R_TR,), (R_KB,))
                st2.append(s2)

            mark('B_proj_%d%d' % (l, pa))
            proj(O_BQN, 512, hB1)
            proj(O_BQRC, 512, hB2)
            proj(O_BKR, 64, hB3)
            if isS:
                S.dma("pool", CM[:], cmla[l].rearrange("(k p) c -> p k c", p=128), (), (R_CM,))
                for kt in range(2):
                    for rc in range(2):
                        tr(TRp[:, (kt * 2 + rc) * 128:(kt * 2 + rc + 1) * 128], CM[:, kt, rc * 128:(rc + 1) * 128],
                           identB[:], (R_CM, R_const), (R_TR,))
                cp(cpalt(), CL[:, 0:2, 0:256].rearrange("p r (k t) -> p k r t", k=2),
                   TRp[:, 0:512].rearrange("p (k r t) -> p k r t", k=2, r=2), (R_TR,), (R_CL,))
                for kt in range(2):
                    zb = nxt("zb", 4)
                    cp("dve", ZB[zb][:, 0:64], CM[:, kt, 256:320], (R_CM,), (R_ZB[zb],))
                    cp("dve", ZB[zb][:, 64:128], CM[:, kt, 256:320], (R_CM,), (R_ZB[zb],))
                    tr(TRp[:, 0:128], ZB[zb][:, 0:128], identB[:], (R_ZB[zb], R_const), (R_TR,))
                    cp(cpalt(), KB[:, kt * 128:(kt + 1) * 128], TRp[:, 0:128], (R_TR,), (R_KB,))
            mark('B_expand_%d%d' % (l, pa))
            nkeys = nkt * 128
            for h in range(4):
                k0 = 0
                while k0 < nkeys:
                    w = min(512, nkeys - k0)
                    z = nxt("z", 2)
                    for rc in range(2):
                        mm(Zp[z][:, 0:w], wukS[:, rc, h * 128:(h + 1) * 128], CL[:, rc, k0:k0 + w], rc == 0, rc == 1,
                           (R_wu, R_CL), (R_Z[z],))
                    cp(cpalt(), KA[:, h, k0:k0 + w], Zp[z][:, 0:w], (R_Z[z],), (R_KA,))
                    k0 += w
            for kt in range(nkt):
                z = nxt("z", 2)
                for rc in range(2):
                    mm(Zp[z][:], CL[:, rc, kt * 128:(kt + 1) * 128], wuvS[:, rc, :], rc == 0, rc == 1,
                       (R_wu, R_CL), (R_Z[z],))
                cp(cpalt(), VA[:, kt, 0:4, 0:128], Zp[z][:].rearrange("p (g d) -> p g d", g=4), (R_Z[z],), (R_VA,))

            def partsB(h, kt, q0, w):
                hh_ = h % 2
                return [(KA[:, h, kt * 128:(kt + 1) * 128], QA[:, h, q0:q0 + w]),
                        (KB[64 * hh_:64 * hh_ + 64, kt * 128:(kt + 1) * 128], QB[64 * hh_:64 * hh_ + 64, h // 2, q0:q0 + w])]

            mark('B_attn_%d%d' % (l, pa))
            run_attn(list(range(4)), partsB, lambda h, kt: VA[:, kt, h, 0:129], 192.0 ** -0.5,
                     fin_std(lambda h: 4 + h, False))

            def hC1(t_, zp, rz, st2):
                zi = zs_load(zp, rz, 512)
                rms_rows(ZS[zi][:, 0:512].rearrange("p (g d) -> p g d", g=4), 4, 128, slice(24, 28),
                         smallS[:, 260:388].unsqueeze(1).broadcast_to([128, 4, 128]), (R_ZS[zi],), (R_ZS[zi],))
                kq_path(t_, zi, 0, 512, 32, QA[:, 0:4, tok(t_)], R_QA, st2)

            def hC2(t_, zp, rz, st2):
                zi = zs_load(zp, rz, 512)
                rms_rows(ZS[zi][:, 0:256].rearrange("p (g d) -> p g d", g=2), 2, 128, slice(24, 26),
                         smallS[:, 388:516].unsqueeze(1).broadcast_to([128, 2, 128]), (R_ZS[zi],), (R_ZS[zi],))
                if not isS:
                    out_heads(t_, zi, 0, 2, nck)
                    out_heads(t_, zi, 256, 2, ncv)
                kq_path(t_, zi, 0, 256, 32, KA[:, 0:2, ktok(t_)], R_KA, st2)
                v_path(t_, zi, 256, 2)

            mark('C_proj_%d%d' % (l, pa))
            proj(O_CQ, 512, hC1)
            proj(O_CKV, 512, hC2)
            if isS:
                ctx_k_heads(cck, 2)
                ctx_v_heads(ccv, 2)
            mark('C_attn_%d%d' % (l, pa))
            run_attn(list(range(4)),
                     lambda h, kt, q0, w: [(KA[:, h // 2, kt * 128:(kt + 1) * 128], QA[:, h, q0:q0 + w])],
                     lambda h, kt: VA[:, kt, h // 2, 0:129], 128.0 ** -0.5, fin_std(lambda h: 8 + h, False))

            def hD1(t_, zp, rz, st2):
                zi = zs_load(zp, rz, 512)
                kq_path(t_, zi, 0, 512, 16, QA[:, 0:4, tok(t_)], R_QA, st2)

            def hD2(t_, zp, rz, st2):
                zi = zs_load(zp, rz, 512)
                if not isS:
                    out_heads(t_, zi, 0, 4, ndk)
                kq_path(t_, zi, 0, 512, 16, KA[:, 0:4, ktok(t_)], R_KA, st2)

            def hD3(t_, zp, rz, st2):
                zi = zs_load(zp, rz, 512)
                if not isS:
                    out_heads(t_, zi, 0, 4, ndv)
                v_path(t_, zi, 0, 4)

            mark('D_proj_%d%d' % (l, pa))
            proj(O_DQ, 512, hD1)
            proj(O_DK, 512, hD2)
            proj(O_DV, 512, hD3)
            if isS:
                ctx_k_heads(cdk, 4)
                ctx_v_heads(cdv, 4)

            def partsD(hm, kt, q0, w):
                h, m = hm
                return [(KA[64 * m:64 * m + 64, h, kt * 128:(kt + 1) * 128], QA[64 * m:64 * m + 64, h, q0:q0 + w])]

            def finD(hm, ch, Ov, Ores):
                h, m = hm
                n = len(ch)
                q0 = ch[0] * 128
                w = n * 128
                cp("dve", SC[:, 32:32 + n], Ov[:, 0:n, 128], Ores, (R_sc,))
                recip(SC[:, 32:32 + n], SC[:, 32:32 + n], (R_sc,), (R_sc,))
                rb = SC[:, 32:32 + n].unsqueeze(2).broadcast_to([128, n, 128])
                if m == 0:
                    tt("dve", O1S[:, 0:n, :], Ov[:, 0:n, 0:128], rb, ALU.mult, Ores + (R_sc,), (R_O1S,))
                    return None
                dv_ = SQ2[:, 0:w].rearrange("p (j d) -> p j d", j=n)
                tt("dve", dv_, Ov[:, 0:n, 0:128], rb, ALU.mult, Ores + (R_sc,), (R_SQ2,))
                stt("dve", dv_, dv_, SC[:, 10:11], O1S[:, 0:n, :], ALU.mult, ALU.add, (R_SQ2, R_sc, R_O1S), (R_SQ2,))
                sqv = ZS[0][:, 0:w].rearrange("p (j d) -> p j d", j=n)
                act(sqv, dv_, AF.Square, (R_SQ2,), (R_ZS[0],))
                S.op("dve", lambda e: e.tensor_reduce(SC[:, 40:40 + n], sqv, AX.X, ALU.add), (R_ZS[0],), (R_sc,))
                act(SC[:, 40:40 + n], SC[:, 40:40 + n], AF.Sqrt, (R_sc, R_const), (R_sc,), bias=epsT[:, 0:1], scale=1.0 / 128)
                recip(SC[:, 40:40 + n], SC[:, 40:40 + n], (R_sc,), (R_sc,))
                tt("dve", dv_, dv_, SC[:, 40:40 + n].unsqueeze(2).broadcast_to([128, n, 128]), ALU.mult,
                   (R_SQ2, R_sc), (R_SQ2,))
                ob = nxt("ob", 2)
                tt("dve", OB[ob][:, 0:n, :], dv_, DGt[:].unsqueeze(1).broadcast_to([128, n, 128]), ALU.mult,
                   (R_SQ2, R_dg), (R_OB[ob],))
                def s2():
                    for j in range(n):
                        tr(TRp[:, j * 128:(j + 1) * 128], OB[ob][:, j, :], identB[:], (R_OB[ob], R_const), (R_TR,))
                    cp(cpalt(), OTt[:, 12 + h, q0:q0 + w], TRp[:, 0:w], (R_TR,), (R_OT,))
                return s2

            def run_attn_D():
                for (qt_list, kt_list) in seqs:
                    chunks = [qt_list[i:i + 4] for i in range(0, len(qt_list), 4)]
                    for h in range(4):
                        for ch in chunks:
                            for m in range(2):
                                q0 = ch[0] * 128
                                w = len(ch) * 128
                                Ov, Ores = OSETS[nxt("oset", 2)]

                                def pv(n_, kt, p_, nk=len(kt_list), nch=len(ch), h=h, Ov=Ov, Ores=Ores):
                                    for j in range(nch):
                                        mm(Ov[:, j, 0:129], PT[p_][:, j * 128:(j + 1) * 128], VA[:, kt, h, 0:129],
                                           n_ == 0 and j % 2 == 0, n_ == nk - 1, (R_PT[p_], R_VA), Ores, sgc=True)
                                pend = None
                                for n_, kt in enumerate(kt_list):
                                    s_ = nxt("st", 2)
                                    (lh, rh), = partsD((h, m), kt, q0, w)
                                    mm(STp[s_][:, 0:w], lh, rh, True, True, (R_KA, R_QA), (R_ST[s_],))
                                    p_ = nxt("pt", 3)
                                    act(PT[p_][:, 0:w], STp[s_][:, 0:w], AF.Exp, (R_ST[s_],), (R_PT[p_],), scale=64.0 ** -0.5)
                                    if n_ == 1 and fin_pend[0] is not None:
                                        fin_pend[0]()
                                        fin_pend[0] = None
                                    if pend is not None:
                                        pv(*pend)
                                    pend = (n_, kt, p_)
                                pv(*pend)
                                if fin_pend[0] is not None:
                                    fin_pend[0]()
                                fin_pend[0] = finD((h, m), ch, Ov, Ores)
                        ada_pump(1)
                if fin_pend[0] is not None:
                    fin_pend[0]()
                    fin_pend[0] = None

            mark('D_attn_%d%d' % (l, pa))
            run_attn_D()
            ada_pump(len(ada_pending))
            S.barrier()

        def phase_wout(l, pa):
            ce = 0
            for cg in range(4):
                wv, rw = w_get(("out", l, pa, cg))
                for j in range(4):
                    dc = cg * 4 + j
                    for th in range(2):
                        zb_, rz_ = bank6()
                        for e_ in range(KC):
                            mm(zb_[:], wv[:, e_, j * 128:(j + 1) * 128], OTt[:, e_, th * 512:(th + 1) * 512],
                               e_ == 0, e_ == KC - 1, (rw, R_OT), (rz_,))
                        ce ^= 1
                        cp("act" if ce else "dve", BIGt[:, dc, th * 512:(th + 1) * 512], zb_[:], (rz_,), (R_BIG[dc],))
            S.barrier()

        def phase_mlp(l, pa):
            rt = 0
            for fg in range(16):
                ab = fg % 2
                wu, ru = w_get(("up", l, pa, fg))
                for fc in range(4):
                    for th in range(2):
                        zb_, rz_ = bank6()
                        for k in range(KC):
                            mm(zb_[:], wu[:, k, fc * 128:(fc + 1) * 128], Ht[:, k, th * 512:(th + 1) * 512],
                               k == 0, k == KC - 1, (ru, R_H), (rz_,))
                        rt ^= 1
                        act(RT[rt], zb_[:], AF.Relu, (rz_,), (R_RT[rt],))
                        tt("dve", ACTT[ab][:, fc, th * 512:(th + 1) * 512], RT[rt], RT[rt], ALU.mult,
                           (R_RT[rt],), (R_ACTT[ab],))
                wd, rd = w_get(("down", l, pa, fg))
                for dc in range(KC):
                    for th in range(2):
                        zb_, rz_ = bank6()
                        for fc in range(4):
                            mm(zb_[:], wd[:, fc, dc * 128:(dc + 1) * 128], ACTT[ab][:, fc, th * 512:(th + 1) * 512],
                               fc == 0, fc == 3, (rd, R_ACTT[ab]), (rz_,))
                        dst = BIGt[:, dc, th * 512:(th + 1) * 512]
                        if fg == 0:
                            cp("dve", dst, zb_[:], (rz_,), (R_BIG[dc],))
                        else:
                            tt("dve", dst, dst, zb_[:], ALU.add, (rz_, R_BIG[dc]), (R_BIG[dc],))

        def phase_final(pa):
            rt = 0
            for t_ in range(NT):
                for cg in range(4):
                    z = nxt("z", 2)
                    for j in range(4):
                        c = cg * 4 + j
                        tr(Zp[z][:, j * 128:(j + 1) * 128], BIGt[:, c, t_ * 128:(t_ + 1) * 128], identF[:],
                           (R_BIG[c], R_const), (R_Z[z],))
                    rt ^= 1
                    cp("act" if rt else "dve", RT[rt], Zp[z][:], (R_Z[z],), (R_RT[rt],))
                    S.dma("sp", y_out[pa][t_ * 128:(t_ + 1) * 128, cg * 512:(cg + 1) * 512], RT[rt], (R_RT[rt],), (),
                          is_out=True)

        def mark(name):
            PHASES.append((name, dict(S.cnt)))
        mark('xt')
        phase_xt(0)
        phase_xt(1)
        for l in range(DEPTH):
            mark('ada%d' % l)
            phase_ada(l)
            for pa in range(2):
                mark('norm1_%d%d' % (l, pa))
                norm_from_scratch(pa, 0, 1)
                S.barrier()
                mark('attn_%d%d' % (l, pa))
                phase_attn(l, pa)
                if l == 0 and pa == 0:
                    par_rows(0, (2, 3, 4, 5))
                mark('wout_%d%d' % (l, pa))
                phase_wout(l, pa)
                mark('post1_%d%d' % (l, pa))
                post_residual(pa, 2, True)
                norm_from_big(pa, 3, 4)
                mark('mlp_%d%d' % (l, pa))
                phase_mlp(l, pa)
                mark('post2_%d%d' % (l, pa))
                post_residual(pa, 5, l == 0)
                if l == DEPTH - 1:
                    phase_final(pa)
        mark('end')
        if plan is None:
            return wkeys
        assert wstate["next_get"] == len(wseq), (wstate, len(wseq))

        S.plan_signals()
        nep = {e: (S.nsig[e] + EPOCH - 1) // EPOCH for e in ENGS}
        sems = {}
        for e in ENGS:
            for ep in range(max(1, nep[e])):
                sems[("e", e, ep)] = es.enter_context(nc.semaphore("s_%s_%d" % (e, ep)))
        for j in range(NDSEM):
            sems[("d", j)] = es.enter_context(nc.semaphore("d_%d" % j))
        block = es.enter_context(nc.Block())

        @block.tensor
        def _(e):
            S.emit("pe", e, sems)

        @block.scalar
        def _(e):
            S.emit("act", e, sems)

        @block.vector
        def _(e):
            S.emit("dve", e, sems)

        @block.gpsimd
        def _(e):
            S.emit("pool", e, sems)

        @block.sync
        def _(e):
            S.emit("sp", e, sems)
            for j in range(NDSEM):
                if S.dma_uses[j] > 0:
                    e.wait_ge(sems[("d", j)], 16 * S.dma_uses[j])
    return nc


_CACHE = {}


def kernel(x_prompt, x_sample, cache_a_k, cache_a_v, cache_mla, cache_c_k, cache_c_v, cache_d_k, cache_d_v,
           c, c_ctx, w_ada, b_ada, g_attn_pre, g_attn_post, g_mlp_pre, g_mlp_post, w_in, a_sink,
           b_g_kv, b_w_uk, b_w_uv, c_gq, c_gk, d_lam, d_g_out, w_out, w_up, w_down):
    f = lambda a: np.ascontiguousarray(np.asarray(a, dtype=np.float32))
    x_prompt, x_sample = f(x_prompt), f(x_sample)
    if "nc" not in _CACHE:
        _CACHE["nc"] = build()
    nc = _CACHE["nc"]
    c32, s32, c16, s16 = _rope_tables()
    ident = np.eye(128, dtype=np.float32)
    masks = np.concatenate([np.tril(np.ones((128, 128), np.float32)), np.triu(np.ones((128, 128), np.float32))], axis=1)
    gT = np.stack([f(g_attn_pre), f(g_attn_post), f(g_mlp_pre), f(g_mlp_post)], axis=1)
    gT = np.ascontiguousarray(gT.reshape(2, 4, KC, 128).transpose(3, 0, 1, 2))
    bT = np.ascontiguousarray(f(b_ada).reshape(2, 96, 128).transpose(2, 0, 1))
    small = np.concatenate([f(a_sink), f(b_g_kv), f(c_gq), f(c_gk), f(d_g_out), f(d_lam).reshape(2, 256)], axis=1)
    small = np.ascontiguousarray(np.broadcast_to(small[None], (128, 2, 900)))
    shared = {
        "w_ada": f(w_ada), "w_in": f(w_in), "w_out": f(w_out), "w_up": f(w_up), "w_down": f(w_down),
        "wuk": f(b_w_uk).reshape(2, 256, 512), "wuv": f(b_w_uv).reshape(2, 256, 512),
        "gT": gT, "bT": bT, "small": small, "ident": ident, "masks": masks,
        "rc32": c32, "rs32": s32, "rc16": c16, "rs16": s16,
    }
    cc, cctx = f(c), f(c_ctx)
    in_maps = []
    for b in range(NCORES):
        cT = np.stack([cctx, cc[b]], axis=-1).reshape(KC, 128, 2).transpose(1, 0, 2)
        m = dict(shared)
        m.update({
            "xp": x_prompt[4 * b:4 * b + 4].reshape(T, D), "xs": x_sample[b],
            "cak": f(cache_a_k[b]), "cav": f(cache_a_v[b]), "cmla": f(cache_mla[b]),
            "cck": f(cache_c_k[b]), "ccv": f(cache_c_v[b]), "cdk": f(cache_d_k[b]), "cdv": f(cache_d_v[b]),
            "cT": np.ascontiguousarray(cT),
        })
        in_maps.append(m)
    res = run_bass_kernel_spmd(nc, in_maps, core_ids=list(range(NCORES)))
    R = res.results
    cat = lambda k: np.concatenate([np.asarray(r[k], dtype=np.float32) for r in R], axis=0)
    y_prompt = cat("yp").reshape(32, 256, D)
    y_sample = np.stack([np.asarray(r["ys"], dtype=np.float32) for r in R], axis=0)
    return (y_prompt, y_sample, cat("nak"), cat("nav"), cat("nmla"), cat("nck"), cat("ncv"), cat("ndk"), cat("ndv"))
```

```python
import math
from contextlib import ExitStack
import numpy as np
import concourse.bass as bass
import concourse.mybir as mybir
from concourse.bass_utils import run_bass_kernel_spmd

F32 = mybir.dt.float32
BF16 = mybir.dt.bfloat16
AF = mybir.ActivationFunctionType
ALU = mybir.AluOpType
AX = mybir.AxisListType

D = 2048
KC = 16
T = 1024
NT = 8
DEPTH = 2
DFF = 8192
INW = 4672
EPS = 1e-6
NCORES = 8
O_AQ, O_AKV, O_BQN, O_BQRC, O_BKR, O_CQ, O_CKV, O_DQ, O_DK, O_DV = 0, 512, 1024, 1536, 2048, 2112, 2624, 3136, 3648, 4160

PHASES = []
EPOCH = 12000
NDSEM = 16
NSW = 5
ENGS = ("pe", "act", "dve", "pool", "sp")
SAME_SYNC = {"pe": False, "act": True, "dve": True, "pool": True, "sp": False}


class Res:
    __slots__ = ("name", "w", "rs", "persist")

    def __init__(self, name, persist=False):
        self.name = name
        self.w = None
        self.rs = {}
        self.persist = persist


class Sched:
    def __init__(self):
        self.q = {e: [] for e in ENGS}
        self.cnt = {e: 0 for e in ENGS}
        self.nop = {e: 0 for e in ENGS}
        self.dma_uses = [0] * NDSEM
        self.dma_rr = 0
        self.dma_rr_sw = 0
        self.all_res = []
        self.pend = {e: {} for e in ENGS}
        self.out_events = []
        self.rank = {}
        self.nsig = {e: 0 for e in ENGS}

    def res(self, name, persist=False):
        r = Res(name, persist)
        self.all_res.append(r)
        return r

    @staticmethod
    def _need(d, ev):
        if ev is None:
            return
        if d.get(ev[0], 0) < ev[1]:
            d[ev[0]] = ev[1]

    def _deps(self, eng, reads, writes):
        allw, raw = {}, {}
        for r in reads:
            self._need(allw, r.w)
            self._need(raw, r.w)
        for r in writes:
            self._need(allw, r.w)
            for k, v in r.rs.items():
                self._need(allw, (k, v))
        for k, v in self.pend[eng].items():
            if allw.get(k, 0) < v:
                allw[k] = v
        self.pend[eng] = {}
        return allw, raw

    def _commit(self, ev, reads, writes):
        for r in reads:
            if r.rs.get(ev[0], 0) < ev[1]:
                r.rs[ev[0]] = ev[1]
        for r in writes:
            r.w = ev
            r.rs = {}

    def op(self, eng, fn, reads=(), writes=()):
        allw, raw = self._deps(eng, reads, writes)
        self.cnt[eng] += 1
        self.nop[eng] += 1
        ev = (("e", eng), self.nop[eng])
        self.q[eng].append([fn, allw, raw, ev])
        self._commit(ev, reads, writes)
        return ev

    def dma(self, eng, out, in_, reads=(), writes=(), is_out=False):
        allw, raw = self._deps(eng, reads, writes)
        if eng == "pool":
            j = self.dma_rr_sw
            self.dma_rr_sw = (j + 1) % NSW
        else:
            j = NSW + self.dma_rr
            self.dma_rr = (self.dma_rr + 1) % (NDSEM - NSW)
        prev = self.dma_uses[j]
        self.dma_uses[j] += 1
        key = ("d", j)
        if prev > 0 and allw.get(key, 0) < 16 * prev:
            allw[key] = 16 * prev
        ev = (key, 16 * (prev + 1))
        self.cnt[eng] += 1
        self.q[eng].append([lambda e: e.dma_start(out=out, in_=in_), allw, raw, ev])
        self._commit(ev, reads, writes)
        if is_out:
            self.out_events.append(ev)
        return ev

    def barrier(self):
        waits = {}
        for r in self.all_res:
            if r.persist:
                continue
            evs = list(r.rs.items())
            if r.w is not None:
                evs.append(r.w)
            for k, v in evs:
                if waits.get(k, 0) < v:
                    waits[k] = v
            r.w = None
            r.rs = {}
        for e in ENGS:
            for k, v in waits.items():
                if self.pend[e].get(k, 0) < v:
                    self.pend[e][k] = v

    def plan_signals(self):
        needed = {e: set() for e in ENGS}
        for eng in ENGS:
            seen = {}
            for ent in self.q[eng]:
                fn, allw, raw, ev = ent
                eff = {}
                for k, v in allw.items():
                    if k[0] == "e" and k[1] == eng and not SAME_SYNC[eng]:
                        continue
                    if seen.get(k, 0) >= v:
                        continue
                    seen[k] = v
                    eff[k] = v
                    if k[0] == "e":
                        needed[k[1]].add(v)
                ent.append(eff)
        for e in ENGS:
            for r, idx in enumerate(sorted(needed[e])):
                self.rank[(e, idx)] = r
            self.nsig[e] = len(needed[e])

    def emit(self, eng, e, sems):
        for fn, allw, raw, ev, eff in self.q[eng]:
            for k, v in eff.items():
                if k[0] == "e":
                    r = self.rank[(k[1], v)]
                    e.wait_ge(sems[("e", k[1], r // EPOCH)], r % EPOCH + 1)
                else:
                    e.wait_ge(sems[k], v)
            ins = fn(e)
            if ev[0][0] == "d":
                ins.then_inc(sems[ev[0]], 16)
            else:
                r = self.rank.get((eng, ev[1]))
                if r is not None:
                    ins.then_inc(sems[("e", eng, r // EPOCH)], 1)


def _rope_tables():
    t = np.arange(T)
    row = (t // 64).astype(np.float32)
    col = (t % 64).astype(np.float32)

    def tab(half):
        inv = (np.float32(10000.0) ** (-(np.arange(half, dtype=np.float32) / np.float32(half)))).astype(np.float32)
        ar = (row[:, None] * inv[None, :]).astype(np.float32)
        ac = (col[:, None] * inv[None, :]).astype(np.float32)
        c = np.stack([np.cos(ar), np.cos(ac)], axis=1).astype(np.float32)
        s = np.stack([np.sin(ar), np.sin(ac)], axis=1).astype(np.float32)
        c = c.reshape(NT, 128, 2, half).transpose(1, 0, 2, 3)
        s = s.reshape(NT, 128, 2, half).transpose(1, 0, 2, 3)
        return np.ascontiguousarray(c), np.ascontiguousarray(s)

    c32, s32 = tab(32)
    c16, s16 = tab(16)
    return c32, s32, c16, s16


def build():
    plan = _build(None)
    return _build(plan)


def _build(plan):
    nc = bass.Bass("TRN2", target_bir_lowering=False)
    del PHASES[:]
    S = Sched()

    def din(name, shape):
        return nc.dram_tensor(name, list(shape), F32, kind="ExternalInput").ap()

    def dout(name, shape):
        return nc.dram_tensor(name, list(shape), F32, kind="ExternalOutput").ap()

    x_in = [din("xp", [T, D]), din("xs", [T, D])]
    cak = din("cak", [2, 2, 256, 128]); cav = din("cav", [2, 2, 256, 128])
    cmla = din("cmla", [2, 256, 320])
    cck = din("cck", [2, 2, 256, 128]); ccv = din("ccv", [2, 2, 256, 128])
    cdk = din("cdk", [2, 4, 256, 128]); cdv = din("cdv", [2, 4, 256, 128])
    cT_d = din("cT", [128, KC, 2])
    gT_d = din("gT", [128, 2, 4, KC])
    bT_d = din("bT", [128, 2, 96])
    w_ada = din("w_ada", [2, D, 6 * D])
    w_in = din("w_in", [2, D, INW])
    w_out = din("w_out", [2, D, D])
    w_up = din("w_up", [2, D, DFF])
    w_down = din("w_down", [2, DFF, D])
    wuk_d = din("wuk", [2, 256, 512]); wuv_d = din("wuv", [2, 256, 512])
    small_d = din("small", [128, 2, 900])
    ident_d = din("ident", [128, 128])
    masks_d = din("masks", [128, 256])
    rc32_d = din("rc32", [128, NT, 2, 32]); rs32_d = din("rs32", [128, NT, 2, 32])
    rc16_d = din("rc16", [128, NT, 2, 16]); rs16_d = din("rs16", [128, NT, 2, 16])

    y_out = [dout("yp", [T, D]), dout("ys", [T, D])]
    nak = dout("nak", [4, 2, 2, 256, 128]); nav = dout("nav", [4, 2, 2, 256, 128])
    nmla = dout("nmla", [4, 2, 256, 320])
    nck = dout("nck", [4, 2, 2, 256, 128]); ncv = dout("ncv", [4, 2, 2, 256, 128])
    ndk = dout("ndk", [4, 2, 4, 256, 128]); ndv = dout("ndv", [4, 2, 4, 256, 128])
    xscr = [nc.dram_tensor("xscr%d" % i, [KC, 128, T], F32, kind="Internal").ap() for i in range(2)]
    XSC = [[S.res("xscr%d_%d" % (i, c)) for c in range(KC)] for i in range(2)]

    es = ExitStack()
    with es:
        def sb(name, shape, dt):
            return es.enter_context(nc.sbuf_tensor(name, list(shape), dt))

        def ps(name, shape, dt):
            return es.enter_context(nc.psum_tensor(name, list(shape), dt))

        es.enter_context(nc.allow_low_precision("bf16 matmul operands, fp32 accumulation"))
        WRt = sb("WR", [128, 3, 8192], BF16)
        Ht = sb("H", [128, KC, T], BF16)
        OTt = sb("OT", [128, KC, T], BF16)
        BIGt = sb("BIG", [128, KC, T], F32)
        RSt = sb("RS", [128, T], F32)
        identF = sb("identF", [128, 128], F32)
        identB = sb("identB", [128, 128], BF16)
        onesF = sb("onesF", [128, 128], F32)
        onesB = sb("onesB", [128, 128], BF16)
        maskB = sb("maskB", [128, 256], BF16)
        rc32 = sb("rc32s", [128, NT, 2, 32], F32); rs32 = sb("rs32s", [128, NT, 2, 32], F32)
        rc16 = sb("rc16s", [128, NT, 2, 16], F32); rs16 = sb("rs16s", [128, NT, 2, 16], F32)
        cTs = sb("cTs", [128, KC, 2], F32)
        sTs = sb("sTs", [128, KC, 2], BF16)
        gTs = sb("gTs", [128, 2, 4, KC], F32)
        bTs = sb("bTs", [128, 2, 96], F32)
        modT = sb("modT", [128, 2, 96, 2], F32)
        PAR = sb("PAR", [128, 2, 6, KC], F32)
        smallS = sb("smallS", [128, 900], F32)
        wukS = sb("wukS", [128, 2, 512], BF16); wuvS = sb("wuvS", [128, 2, 512], BF16)
        SC = sb("SC", [128, 64], F32)
        epsT = sb("epsT", [128, 1], F32)
        DGt = sb("DG", [128, 128], F32)
        LAMt = sb("LAMt", [128, 128], F32)

        STp = [ps("ST%d" % i, [128, 512], F32) for i in range(2)]
        Op = ps("Oacc", [128, 2, 512], F32)
        Zp = [ps("Z%d" % i, [128, 512], F32) for i in range(2)]
        TRp = ps("TRB", [128, 1024], BF16)
        Mp = ps("MISC", [128, 512], F32)
        R_ST = [S.res("ST%d" % i) for i in range(2)]
        R_O = S.res("O")
        R_Z = [S.res("Z%d" % i) for i in range(2)]
        R_TR = S.res("TR")
        R_M = S.res("M")

        BANKS6 = [(Zp[0], R_Z[0]), (Zp[1], R_Z[1]), (STp[0], R_ST[0]), (STp[1], R_ST[1]), (Mp, R_M)]
        BANKS3 = [(Zp[0], R_Z[0]), (Zp[1], R_Z[1]), (Mp, R_M)]

        def bank6():
            return BANKS6[nxt("bk6", len(BANKS6))]

        def bank3():
            return BANKS3[nxt("bk3", len(BANKS3))]

        R_WR = [S.res("WR%d" % i, persist=True) for i in range(3)]
        R_H = S.res("H"); R_OT = S.res("OT"); R_RS = S.res("RS")
        R_BIG = [S.res("BIG%d" % c) for c in range(KC)]
        R_const = S.res("const"); R_mod = S.res("mod"); R_par = S.res("par")
        R_lam = S.res("lam"); R_small = S.res("small"); R_wu = S.res("wu"); R_sc = S.res("sc"); R_dg = S.res("dg")

        OTf = OTt[:].bitcast(F32)
        OTflat = OTt[:].rearrange("p c t -> p (c t)").bitcast(F32)
        XC = [OTflat[:, i * 1024:(i + 1) * 1024] for i in range(2)]
        SQh = [OTflat[:, 2048 + i * 512: 2048 + (i + 1) * 512] for i in range(2)]
        SQbf = [SQh[i].bitcast(BF16) for i in range(2)]
        R_XC = [S.res("XC%d" % i) for i in range(2)]
        R_SQ = [S.res("SQ%d" % i) for i in range(2)]
        OTb = OTt[:].rearrange("p c t -> p (c t)")
        ACTT = [OTb[:, 6144 + i * 4096: 6144 + (i + 1) * 4096].rearrange("p (f t) -> p f t", f=4) for i in range(2)]
        R_ACTT = [S.res("ACTT%d" % i) for i in range(2)]
        RT = [OTflat[:, 7168 + i * 512: 7168 + (i + 1) * 512] for i in range(2)]
        R_RT = [S.res("RT%d" % i) for i in range(2)]

        BIGf = BIGt[:].rearrange("p c t -> p (c t)")
        BIGb = BIGf.bitcast(BF16)
        off = [0]

        def carve_b(n):
            a = BIGb[:, off[0]: off[0] + n]
            off[0] += n
            return a

        QA = carve_b(4096).rearrange("p (h t) -> p h t", h=4)
        QB = carve_b(2048).rearrange("p (h t) -> p h t", h=2)
        KA = carve_b(5120).rearrange("p (h t) -> p h t", h=4)
        KB = carve_b(1280)
        CL = carve_b(2560).rearrange("p (h t) -> p h t", h=2)
        VA = carve_b(5200).rearrange("p (k g d) -> p k g d", k=10, g=4)
        PT = [carve_b(512) for _ in range(3)]
        ZB = [carve_b(512) for _ in range(4)]
        OB = [carve_b(512).rearrange("p (j d) -> p j d", j=4) for _ in range(2)]
        CST = carve_b(1024).rearrange("p (k g d) -> p k g d", k=2, g=4)
        CM = carve_b(640).rearrange("p (k c) -> p k c", k=2)
        assert off[0] % 2 == 0
        foff = [off[0] // 2]

        def carve_f(n):
            a = BIGf[:, foff[0]: foff[0] + n]
            foff[0] += n
            return a

        ZS = [carve_f(512) for _ in range(3)]
        ZT = [carve_f(256) for _ in range(2)]
        O1S = carve_f(512).rearrange("p (j d) -> p j d", j=4)
        SQ2 = carve_f(512)
        assert foff[0] <= 16384, foff[0]
        R_QA = S.res("QA"); R_QB = S.res("QB"); R_KA = S.res("KA"); R_KB = S.res("KB"); R_CL = S.res("CL")
        R_VA = S.res("VA")
        R_PT = [S.res("PT%d" % i) for i in range(3)]
        R_ZB = [S.res("ZB%d" % i) for i in range(4)]
        R_OB = [S.res("OB%d" % i) for i in range(2)]
        R_CST = S.res("CST"); R_CM = S.res("CM")
        R_ZS = [S.res("ZS%d" % i) for i in range(3)]
        R_ZT = [S.res("ZT%d" % i) for i in range(2)]
        R_ZTP = [S.res("ZTP%d" % i) for i in range(2)]
        R_O1S = S.res("O1S"); R_SQ2 = S.res("SQ2")
        rr = {"zs": 0, "zb": 0, "z": 0, "pt": 0, "st": 0, "ob": 0, "xc": 0, "sq": 0, "bk6": 0, "bk3": 0}

        def nxt(key, n):
            v = rr[key]
            rr[key] = (v + 1) % n
            return v

        def mm(out, lhsT, rhs, start, stop, reads, writes, sgc=False):
            if sgc:
                return S.op("pe", lambda e: e.matmul(out, lhsT, rhs, start=start, stop=stop, skip_group_check=True),
                            reads, writes)
            return S.op("pe", lambda e: e.matmul(out, lhsT, rhs, start=start, stop=stop), reads, writes)

        def tr(out, in_, ident, reads, writes):
            return S.op("pe", lambda e: e.transpose(out, in_, ident), reads, writes)

        def act(out, in_, func, reads, writes, bias=None, scale=None, accum=None):
            kw = {}
            if bias is not None:
                kw["bias"] = bias
            if scale is not None:
                kw["scale"] = scale
            if accum is not None:
                kw["accum_out"] = accum
            return S.op("act", lambda e: e.activation(out, in_, func, **kw), reads, writes)

        def tt(eng, out, a, b, op, reads, writes):
            return S.op(eng, lambda e: e.tensor_tensor(out, a, b, op), reads, writes)

        def ts(eng, out, a, s1, op0, reads, writes, s2=None, op1=None):
            if s2 is None:
                return S.op(eng, lambda e: e.tensor_scalar(out, a, s1, None, op0), reads, writes)
            return S.op(eng, lambda e: e.tensor_scalar(out, a, s1, s2, op0, op1), reads, writes)

        def stt(eng, out, a, s, b, op0, op1, reads, writes):
            return S.op(eng, lambda e: e.scalar_tensor_tensor(out, a, s, b, op0, op1), reads, writes)

        def cp(eng, out, in_, reads, writes):
            if eng == "act":
                return S.op("act", lambda e: e.copy(out, in_), reads, writes)
            return S.op(eng, lambda e: e.tensor_copy(out, in_), reads, writes)

        def recip(out, in_, reads, writes):
            return S.op("dve", lambda e: e.reciprocal(out, in_), reads, writes)

        def memset(eng, ap, v, writes):
            return S.op(eng, lambda e: e.memset(ap, v), (), writes)

        def w_desc(key):
            kind = key[0]
            if kind == "ada":
                _, l, g = key
                return (w_ada[l][:, g * 512:(g + 1) * 512].rearrange("(k p) c -> p k c", p=128), 8192, (16, 512))
            if kind == "in":
                _, l, pa, c0 = key
                n = 64 if c0 == O_BKR else 512
                return (w_in[l][:, c0:c0 + n].rearrange("(k p) c -> p k c", p=128), 16 * n, (16, n))
            if kind == "out":
                _, l, pa, cg = key
                return (w_out[l][:, cg * 512:(cg + 1) * 512].rearrange("(k p) c -> p k c", p=128), 8192, (16, 512))
            if kind == "up":
                _, l, pa, fg = key
                return (w_up[l][:, fg * 512:(fg + 1) * 512].rearrange("(k p) c -> p k c", p=128), 8192, (16, 512))
            _, l, pa, fg = key
            return (w_down[l][fg * 512:(fg + 1) * 512, :].rearrange("(f p) d -> p f d", p=128), 8192, (4, D))

        wkeys = list(plan) if plan is not None else []
        wseq = [w_desc(k) for k in wkeys]
        wstate = {"next_issue": 0, "next_get": 0}

        def w_issue_upto(n):
            while wstate["next_issue"] < min(n, len(wseq)):
                i = wstate["next_issue"]
                dram_ap, nel, shp = wseq[i]
                slot = i % 3
                dst = WRt[:, slot, 0:nel].rearrange("p (a b) -> p a b", a=shp[0])
                S.dma("pool", dst, dram_ap, (), (R_WR[slot],))
                wstate["next_issue"] += 1

        def w_get(key):
            i = wstate["next_get"]
            wstate["next_get"] += 1
            if plan is None:
                wkeys.append(key)
                dram_ap, nel, shp = w_desc(key)
            else:
                assert wkeys[i] == key, (i, wkeys[i], key)
                w_issue_upto(i + 3)
                dram_ap, nel, shp = wseq[i]
            slot = i % 3
            return WRt[:, slot, 0:nel].rearrange("p (a b) -> p a b", a=shp[0]), R_WR[slot]

        S.dma("sp", identF[:], ident_d, (), (R_const,))
        S.dma("pool", identB[:], ident_d, (), (R_const,))
        S.dma("pool", maskB[:], masks_d, (), (R_const,))
        S.dma("sp", rc32[:], rc32_d, (), (R_const,)); S.dma("sp", rs32[:], rs32_d, (), (R_const,))
        S.dma("sp", rc16[:], rc16_d, (), (R_const,)); S.dma("sp", rs16[:], rs16_d, (), (R_const,))
        S.dma("sp", cTs[:], cT_d, (), (R_const,))
        S.dma("sp", gTs[:], gT_d, (), (R_const,))
        S.dma("sp", bTs[:], bT_d, (), (R_const,))
        memset("dve", onesF[:], 1.0, (R_const,))
        memset("dve", onesB[:], 1.0, (R_const,))
        memset("dve", epsT[:], EPS, (R_const,))
        act(sTs[:], cTs[:], AF.Silu, (R_const,), (R_const,))
        if plan is not None:
            w_issue_upto(3)

        def stats_rstd(get_chunk, width, scale_inv):
            for c in range(KC):
                src, rds = get_chunk(c)
                i = c % 2
                act(SQbf[i], src, AF.Square, rds, (R_SQ[i],))
                for th in range(2):
                    mm(Zp[th][:], onesB[:], SQbf[i][:, th * 512:(th + 1) * 512], c == 0, c == KC - 1,
                       (R_SQ[i], R_const), (R_Z[th],))
            for th in range(2):
                act(RSt[:, th * 512:(th + 1) * 512], Zp[th][:], AF.Sqrt, (R_Z[th], R_const), (R_RS,),
                    bias=epsT[:, 0:1], scale=scale_inv)
            recip(RSt[:], RSt[:], (R_RS,), (R_RS,))

        def load_xc(pa, c):
            i = nxt("xc", 2)
            S.dma("sp", XC[i], xscr[pa][c], (XSC[pa][c],), (R_XC[i],))
            return i

        def apply_mod(pa, gi, si, c, src, rds):
            for th in range(2):
                i = nxt("sq", 2)
                hs = slice(th * 512, (th + 1) * 512)
                tt("dve", SQh[i], src[:, hs], RSt[:, hs], ALU.mult, tuple(rds) + (R_RS,), (R_SQ[i],))
                act(Ht[:, c, hs], SQh[i], AF.Identity, (R_SQ[i], R_par), (R_H,),
                    bias=PAR[:, pa, si, c:c + 1], scale=PAR[:, pa, gi, c:c + 1])

        def norm_from_scratch(pa, gi, si):
            def gc(c):
                i = load_xc(pa, c)
                return XC[i], (R_XC[i],)
            stats_rstd(gc, T, 1.0 / D)
            for c in range(KC):
                i = load_xc(pa, c)
                apply_mod(pa, gi, si, c, XC[i], (R_XC[i],))

        def post_residual(pa, ggi, to_scratch):
            def gc(c):
                return BIGt[:, c, :], (R_BIG[c],)
            stats_rstd(gc, T, 1.0 / D)
            for c in range(KC):
                i = load_xc(pa, c)
                for th in range(2):
                    q = nxt("sq", 2)
                    hs = slice(th * 512, (th + 1) * 512)
                    tt("dve", SQh[q], BIGt[:, c, hs], RSt[:, hs], ALU.mult, (R_BIG[c], R_RS), (R_SQ[q],))
                    stt("dve", BIGt[:, c, hs], SQh[q], PAR[:, pa, ggi, c:c + 1], XC[i][:, hs], ALU.mult, ALU.add,
                        (R_SQ[q], R_par, R_XC[i]), (R_BIG[c],))
                if to_scratch:
                    S.dma("sp", xscr[pa][c], BIGt[:, c, :], (R_BIG[c],), (XSC[pa][c],))

        def norm_from_big(pa, gi, si):
            def gc(c):
                return BIGt[:, c, :], (R_BIG[c],)
            stats_rstd(gc, T, 1.0 / D)
            for c in range(KC):
                apply_mod(pa, gi, si, c, BIGt[:, c, :], (R_BIG[c],))

        def phase_xt(pa):
            XIN = [BIGf[:, i * 2048:(i + 1) * 2048] for i in range(2)]
            XTS = [BIGf[:, 4096 + i * 2048: 4096 + (i + 1) * 2048].rearrange("p (c t) -> p c t", c=KC) for i in range(2)]
            R_XIN = [S.res("XIN%d" % i) for i in range(2)]
            R_XTS = [S.res("XTS%d" % i) for i in range(2)]
            for t_ in range(NT):
                b = t_ % 2
                S.dma("sp", XIN[b], x_in[pa][t_ * 128:(t_ + 1) * 128, :], (), (R_XIN[b],))
                for cg in range(4):
                    z = nxt("z", 2)
                    for j in range(4):
                        c = cg * 4 + j
                        tr(Zp[z][:, j * 128:(j + 1) * 128], XIN[b][:, c * 128:(c + 1) * 128], identF[:],
                           (R_XIN[b], R_const), (R_Z[z],))
                    cp("dve" if cg % 2 == 0 else "act", XTS[b][:, cg * 4:(cg + 1) * 4, :],
                       Zp[z][:].rearrange("p (j t) -> p j t", j=4), (R_Z[z],), (R_XTS[b],))
                S.dma("sp", xscr[pa][:, :, t_ * 128:(t_ + 1) * 128].rearrange("c p t -> p c t"), XTS[b],
                      (R_XTS[b],), tuple(XSC[pa]))
            S.barrier()

        def ada_group(l, g):
            wv, rw = w_get(("ada", l, g))
            for j in range(4):
                for k in range(KC):
                    mm(Mp[:, j * 2:(j + 1) * 2], wv[:, k, j * 128:(j + 1) * 128], sTs[:, k, :], k == 0, k == KC - 1,
                       (rw, R_const), (R_M,))
            for j in range(4):
                m = g * 4 + j
                ts("dve", modT[:, l, m, :], Mp[:, j * 2:(j + 1) * 2], bTs[:, l, m:m + 1], ALU.add,
                   (R_M, R_const), (R_mod,))

        ada_pending = []

        def ada_pump(n=1):
            for _ in range(n):
                if ada_pending:
                    la, g = ada_pending.pop(0)
                    ada_group(la, g)

        def ada_slots(l, pa, pi):
            if l == 0 and pa == 0:
                return [(0, 8 + 2 * pi), (0, 9 + 2 * pi)] if pi < 8 else []
            if l == 0 and pa == 1:
                if pi < 4:
                    return [(1, 3 * pi + i) for i in range(3)]
                return [(1, 12 + 2 * (pi - 4) + i) for i in range(2)]
            return []

        def par_rows(l, rows):
            for pa in range(2):
                def mch(i):
                    return modT[:, l, i * 16:(i + 1) * 16, pa]
                rds = (R_mod, R_const)
                if 0 in rows:
                    stt("dve", PAR[:, pa, 0, :], mch(1), 1.0, gTs[:, l, 0, :], ALU.add, ALU.mult, rds, (R_par,))
                if 1 in rows:
                    cp("dve", PAR[:, pa, 1, :], mch(0), rds, (R_par,))
                if 2 in rows:
                    tt("dve", PAR[:, pa, 2, :], mch(2), gTs[:, l, 1, :], ALU.mult, rds, (R_par,))
                if 3 in rows:
                    stt("dve", PAR[:, pa, 3, :], mch(4), 1.0, gTs[:, l, 2, :], ALU.add, ALU.mult, rds, (R_par,))
                if 4 in rows:
                    cp("dve", PAR[:, pa, 4, :], mch(3), rds, (R_par,))
                if 5 in rows:
                    tt("dve", PAR[:, pa, 5, :], mch(5), gTs[:, l, 3, :], ALU.mult, rds, (R_par,))

        def phase_ada(l):
            if l == 0:
                for g in range(8):
                    ada_group(0, g)
                par_rows(0, (0, 1))
            else:
                par_rows(1, (0, 1, 2, 3, 4, 5))
            S.dma("sp", smallS[:], small_d[:, l, :], (), (R_small,))
            S.dma("pool", wukS[:], wuk_d[l].rearrange("(c p) n -> p c n", p=128), (), (R_wu,))
            S.dma("pool", wuvS[:], wuv_d[l].rearrange("(c p) n -> p c n", p=128), (), (R_wu,))
            act(SC[:, 0:4], smallS[:, 0:4], AF.Exp, (R_small,), (R_sc,))
            lam_init = 0.8 - 0.6 * math.exp(-0.3 * l)
            dl = smallS[:, 644:900]
            tt("dve", LAMt[:, 0:64], dl[:, 0:64], dl[:, 64:128], ALU.mult, (R_small,), (R_lam,))
            tt("dve", LAMt[:, 64:128], dl[:, 128:192], dl[:, 192:256], ALU.mult, (R_small,), (R_lam,))
            S.op("dve", lambda e: e.tensor_reduce(SC[:, 8:10], LAMt[:].rearrange("p (a b) -> p a b", a=2), AX.X, ALU.add),
                 (R_lam,), (R_sc,))
            act(SC[:, 8:10], SC[:, 8:10], AF.Exp, (R_sc,), (R_sc,))
            tt("dve", SC[:, 10:11], SC[:, 9:10], SC[:, 8:9], ALU.subtract, (R_sc,), (R_sc,))
            ts("dve", SC[:, 10:11], SC[:, 10:11], -lam_init, ALU.add, (R_sc,), (R_sc,))
            ts("dve", DGt[:], smallS[:, 516:644], 1.0 - lam_init, ALU.mult, (R_small,), (R_dg,))
            S.barrier()

        Opv = Op[:].rearrange("p b (j c) -> p (b j) c", j=2)

        def Oj(j):
            return Opv[:, j, 0:129]

        def phase_attn(l, pa):
            isS = (pa == 1)
            koff = 2 if isS else 0
            nkt = 10 if isS else 8
            if isS:
                seqs = [(list(range(8)), list(range(10)))]
            else:
                seqs = [([2 * s_, 2 * s_ + 1], [2 * s_, 2 * s_ + 1]) for s_ in range(4)]
            memset("dve", VA[:, :, :, 128:129], 1.0, (R_VA,))
            if l == 0 and pa == 0:
                ada_pending.extend((0, g) for g in range(8, 24))
            if l == 0 and pa == 1:
                ada_pending.extend((1, g) for g in range(24))

            def proj(c0, n, handler):
                wv, rw = w_get(("in", l, pa, c0))
                pend2 = []
                for t_ in range(NT):
                    zb_, rz_ = bank3()
                    for k in range(KC):
                        mm(zb_[:, 0:n], Ht[:, k, t_ * 128:(t_ + 1) * 128], wv[:, k, :], k == 0, k == KC - 1,
                           (R_H, rw), (rz_,))
                    st2 = []
                    handler(t_, zb_, rz_, st2)
                    for f_ in pend2:
                        f_()
                    pend2 = st2
                for f_ in pend2:
                    f_()
                ada_pump(1)

            pidx = [0]

            def zs_load(zp, rz, n):
                zi = nxt("zs", 3)
                cp("act", ZS[zi][:, 0:n], zp[:, 0:n], (rz,), (R_ZS[zi],))
                return zi

            def rope(t_, src, dst, W, hw, rsrc, wdst):
                G = W // (4 * hw)
                ctab = (rc32 if hw == 32 else rc16)[:, t_]
                stab = (rs32 if hw == 32 else rs16)[:, t_]
                cb = ctab.unsqueeze(1).broadcast_to([128, G, 2, hw])
                sb_ = stab.unsqueeze(1).broadcast_to([128, G, 2, hw])
                s5 = src.rearrange("p (g b h f) -> p g b h f", g=G, b=2, h=2)
                d5 = dst.rearrange("p (g b h f) -> p g b h f", g=G, b=2, h=2)
                x1 = s5[:, :, :, 0, :]
                x2 = s5[:, :, :, 1, :]
                en, ta, tb, ra, rb_ = "dve", ZT[0], ZT[1], R_ZT[0], R_ZT[1]
                t1 = ta[:, 0:W // 2].rearrange("p (g b f) -> p g b f", g=G, b=2)
                t2 = tb[:, 0:W // 2].rearrange("p (g b f) -> p g b f", g=G, b=2)
                rd = tuple(rsrc) + (R_const,)
                tt(en, t1, x1, cb, ALU.mult, rd, (ra,))
                tt(en, t2, x2, sb_, ALU.mult, rd, (rb_,))
                tt(en, d5[:, :, :, 0, :], t1, t2, ALU.subtract, (ra, rb_), wdst)
                tt(en, t1, x2, cb, ALU.mult, rd, (ra,))
                tt(en, t2, x1, sb_, ALU.mult, rd, (rb_,))
                tt(en, d5[:, :, :, 1, :], t1, t2, ALU.add, (ra, rb_), wdst)

            cpe = [0]

            def cpalt():
                cpe[0] ^= 1
                return "act" if cpe[0] else "dve"

            def tr_to(zb, n, dst3, wres):
                for j in range(n):
                    tr(TRp[:, j * 128:(j + 1) * 128], ZB[zb][:, j * 128:(j + 1) * 128], identB[:],
                       (R_ZB[zb], R_const), (R_TR,))
                cp(cpalt(), dst3, TRp[:, 0:n * 128].rearrange("p (j t) -> p j t", j=n), (R_TR,), (wres,))

            def kq_path(t_, zi, c0, w, hw, dst3, wres, st2):
                zb = nxt("zb", 4)
                src = ZS[zi][:, c0:c0 + w]
                if isS and hw is not None:
                    rope(t_, src, ZB[zb][:, 0:w], w, hw, (R_ZS[zi],), (R_ZB[zb],))
                else:
                    cp("dve", ZB[zb][:, 0:w], src, (R_ZS[zi],), (R_ZB[zb],))
                st2.append(lambda: tr_to(zb, w // 128, dst3, wres))

            def v_path(t_, zi, c0, G):
                cp("dve", VA[:, koff + t_, 0:G, 0:128], ZS[zi][:, c0:c0 + G * 128].rearrange("p (g d) -> p g d", g=G),
                   (R_ZS[zi],), (R_VA,))

            def out_heads(t_, zi, c0, G, dram):
                s_ = t_ // 2
                r0 = (t_ % 2) * 128
                S.dma("sp", dram[s_, l, :, r0:r0 + 128, :].rearrange("g t d -> t g d"),
                      ZS[zi][:, c0:c0 + G * 128].rearrange("p (g d) -> p g d", g=G), (R_ZS[zi],), (), is_out=True)

            def tok(t_):
                return slice(t_ * 128, (t_ + 1) * 128)

            def ktok(t_):
                return slice((koff + t_) * 128, (koff + t_ + 1) * 128)

            def ctx_k_heads(dram_k, G):
                for kt in range(2):
                    S.dma("pool", CST[:, kt, 0:G, :], dram_k[l][:, kt * 128:(kt + 1) * 128, :].rearrange("g p d -> p g d"),
                          (), (R_CST,))
                for kt in range(2):
                    for g in range(G):
                        tr(TRp[:, (kt * G + g) * 128:(kt * G + g + 1) * 128], CST[:, kt, g, :], identB[:],
                           (R_CST, R_const), (R_TR,))
                cp(cpalt(), KA[:, 0:G, 0:256].rearrange("p g (k t) -> p k g t", k=2),
                   TRp[:, 0:2 * G * 128].rearrange("p (k g t) -> p k g t", k=2, g=G), (R_TR,), (R_KA,))

            def ctx_v_heads(dram_v, G):
                for kt in range(2):
                    S.dma("pool", VA[:, kt, 0:G, 0:128], dram_v[l][:, kt * 128:(kt + 1) * 128, :].rearrange("g p d -> p g d"),
                          (), (R_VA,))

            fin_pend = [None]

            def run_attn(hlist, parts, vap, scale, fin, banded=False):
                for (qt_list, kt_list) in seqs:
                    if banded:
                        chunks = [[q] for q in qt_list]
                    else:
                        chunks = [qt_list[i:i + 4] for i in range(0, len(qt_list), 4)]
                    for hh in hlist:
                        for ch in chunks:
                            q0 = ch[0] * 128
                            w = len(ch) * 128
                            if banded:
                                i = ch[0]
                                kts = [(0, None), (1, None)]
                                for j in (i - 1, i, i + 1):
                                    if 0 <= j < 8:
                                        kts.append((2 + j, 0 if j == i - 1 else (1 if j == i + 1 else None)))
                            else:
                                kts = [(k, None) for k in kt_list]
                            def pv(n_, kt, p_, nk=len(kts), nch=len(ch), hh=hh):
                                for j in range(nch):
                                    mm(Oj(j), PT[p_][:, j * 128:(j + 1) * 128], vap(hh, kt), n_ == 0 and j % 2 == 0,
                                       n_ == nk - 1, (R_PT[p_], R_VA), (R_O,), sgc=True)
                            pend = None
                            for n_, (kt, mk) in enumerate(kts):
                                s_ = nxt("st", 2)
                                pl = parts(hh, kt, q0, w)
                                for pi, (lh, rh) in enumerate(pl):
                                    mm(STp[s_][:, 0:w], lh, rh, pi == 0, pi == len(pl) - 1,
                                       (R_KA, R_KB, R_QA, R_QB), (R_ST[s_],))
                                p_ = nxt("pt", 3)
                                act(PT[p_][:, 0:w], STp[s_][:, 0:w], AF.Exp, (R_ST[s_],), (R_PT[p_],), scale=scale)
                                if mk is not None:
                                    tt("dve", PT[p_][:, 0:128], PT[p_][:, 0:128], maskB[:, mk * 128:(mk + 1) * 128],
                                       ALU.mult, (R_PT[p_], R_const), (R_PT[p_],))
                                if n_ == 1 and fin_pend[0] is not None:
                                    fin_pend[0]()
                                    fin_pend[0] = None
                                if pend is not None:
                                    pv(*pend)
                                pend = (n_, kt, p_)
                            pv(*pend)
                            if fin_pend[0] is not None:
                                fin_pend[0]()
                            fin_pend[0] = fin(hh, ch)
                        ada_pump(1)
                if fin_pend[0] is not None:
                    fin_pend[0]()
                    fin_pend[0] = None

            def fin_std(e_of, sink):
                def f(hh, ch):
                    n = len(ch)
                    q0 = ch[0] * 128
                    w = n * 128
                    ob = nxt("ob", 2)
                    if sink:
                        ts("dve", SC[:, 32:32 + n], Opv[:, 0:n, 128], SC[:, hh:hh + 1], ALU.add, (R_O, R_sc), (R_sc,))
                    else:
                        cp("dve", SC[:, 32:32 + n], Opv[:, 0:n, 128], (R_O,), (R_sc,))
                    recip(SC[:, 32:32 + n], SC[:, 32:32 + n], (R_sc,), (R_sc,))
                    tt("dve", OB[ob][:, 0:n, :], Opv[:, 0:n, 0:128],
                       SC[:, 32:32 + n].unsqueeze(2).broadcast_to([128, n, 128]), ALU.mult, (R_O, R_sc), (R_OB[ob],))
                    def s2():
                        for j in range(n):
                            tr(TRp[:, j * 128:(j + 1) * 128], OB[ob][:, j, :], identB[:], (R_OB[ob], R_const), (R_TR,))
                        cp(cpalt(), OTt[:, e_of(hh), q0:q0 + w], TRp[:, 0:w], (R_TR,), (R_OT,))
                    return s2
                return f

            def hA1(t_, zp, rz, st2):
                zi = zs_load(zp, rz, 512)
                kq_path(t_, zi, 0, 512, 32, QA[:, 0:4, tok(t_)], R_QA, st2)

            def hA2(t_, zp, rz, st2):
                zi = zs_load(zp, rz, 512)
                if not isS:
                    out_heads(t_, zi, 0, 2, nak)
                    out_heads(t_, zi, 256, 2, nav)
                kq_path(t_, zi, 0, 256, 32, KA[:, 0:2, ktok(t_)], R_KA, st2)
                v_path(t_, zi, 256, 2)

            mark('A_proj_%d%d' % (l, pa))
            proj(O_AQ, 512, hA1)
            proj(O_AKV, 512, hA2)
            if isS:
                ctx_k_heads(cak, 2)
                ctx_v_heads(cav, 2)
            mark('A_attn_%d%d' % (l, pa))
            run_attn(list(range(4)),
                     lambda h, kt, q0, w: [(KA[:, h // 2, kt * 128:(kt + 1) * 128], QA[:, h, q0:q0 + w])],
                     lambda h, kt: VA[:, kt, h // 2, 0:129], 128.0 ** -0.5, fin_std(lambda h: h, True), banded=isS)

            def hB1(t_, zp, rz, st2):
                zi = zs_load(zp, rz, 512)
                kq_path(t_, zi, 0, 512, None, QA[:, 0:4, tok(t_)], R_QA, st2)

            def rms_rows(srcv, G, width, ssl, gain_bc, rsrc, wres):
                act(SQ2[:, 0:G * width].rearrange("p (g d) -> p g d", g=G), srcv, AF.Square, rsrc, (R_SQ2,))
                S.op("dve", lambda e: e.tensor_reduce(SC[:, ssl], SQ2[:, 0:G * width].rearrange("p (g d) -> p g d", g=G),
                                                      AX.X, ALU.add), (R_SQ2,), (R_sc,))
                act(SC[:, ssl], SC[:, ssl], AF.Sqrt, (R_sc, R_const), (R_sc,), bias=epsT[:, 0:1], scale=1.0 / width)
                recip(SC[:, ssl], SC[:, ssl], (R_sc,), (R_sc,))
                tt("dve", srcv, srcv, SC[:, ssl].unsqueeze(2).broadcast_to([128, G, width]), ALU.mult,
                   tuple(rsrc) + (R_sc,), wres)
                tt("dve", srcv, srcv, gain_bc, ALU.mult, tuple(rsrc) + (R_small,), wres)

            def hB2(t_, zp, rz, st2):
                zi = zs_load(zp, rz, 512)
                kq_path(t_, zi, 0, 256, 16, QB[:, 0:2, tok(t_)], R_QB, st2)
                cv = ZS[zi][:, 256:512].rearrange("p (g d) -> p g d", g=1)
                rms_rows(cv, 1, 256, slice(20, 21), smallS[:, 4:260].rearrange("p (g d) -> p g d", g=1),
                         (R_ZS[zi],), (R_ZS[zi],))
                if not isS:
                    s_ = t_ // 2
                    r0 = (t_ % 2) * 128
                    S.dma("sp", nmla[s_, l, r0:r0 + 128, 0:256], ZS[zi][:, 256:512], (R_ZS[zi],), (), is_out=True)
                zb = nxt("zb", 4)
                cp("dve", ZB[zb][:, 0:256], ZS[zi][:, 256:512], (R_ZS[zi],), (R_ZB[zb],))
                st2.append(lambda: tr_to(zb, 2, CL[:, 0:2, ktok(t_)], R_CL))

            def hB3(t_, zp, rz, st2):
                zi = zs_load(zp, rz, 64)
                if not isS:
                    s_ = t_ // 2
                    r0 = (t_ % 2) * 128
                    S.dma("sp", nmla[s_, l, r0:r0 + 128, 256:320], ZS[zi][:, 0:64], (R_ZS[zi],), (), is_out=True)
                zb = nxt("zb", 4)
                if isS:
                    rope(t_, ZS[zi][:, 0:64], ZB[zb][:, 0:64], 64, 16, (R_ZS[zi],), (R_ZB[zb],))
                else:
                    cp("dve", ZB[zb][:, 0:64], ZS[zi][:, 0:64], (R_ZS[zi],), (R_ZB[zb],))
                cp("dve", ZB[zb][:, 64:128], ZB[zb][:, 0:64], (R_ZB[zb],), (R_ZB[zb],))

                def s2():
                    tr(TRp[:, 0:128], ZB[zb][:, 0:128], identB[:], (R_ZB[zb], R_const), (R_TR,))
                    cp(cpalt(), KB[:, ktok(t_)], TRp[:, 0:128], (R_TR,), (R_KB,))
                st2.append(s2)

            mark('B_proj_%d%d' % (l, pa))
            proj(O_BQN, 512, hB1)
            proj(O_BQRC, 512, hB2)
            proj(O_BKR, 64, hB3)
            if isS:
                S.dma("pool", CM[:], cmla[l].rearrange("(k p) c -> p k c", p=128), (), (R_CM,))
                for kt in range(2):
                    for rc in range(2):
                        tr(TRp[:, (kt * 2 + rc) * 128:(kt * 2 + rc + 1) * 128], CM[:, kt, rc * 128:(rc + 1) * 128],
                           identB[:], (R_CM, R_const), (R_TR,))
                cp(cpalt(), CL[:, 0:2, 0:256].rearrange("p r (k t) -> p k r t", k=2),
                   TRp[:, 0:512].rearrange("p (k r t) -> p k r t", k=2, r=2), (R_TR,), (R_CL,))
                for kt in range(2):
                    zb = nxt("zb", 4)
                    cp("dve", ZB[zb][:, 0:64], CM[:, kt, 256:320], (R_CM,), (R_ZB[zb],))
                    cp("dve", ZB[zb][:, 64:128], CM[:, kt, 256:320], (R_CM,), (R_ZB[zb],))
                    tr(TRp[:, 0:128], ZB[zb][:, 0:128], identB[:], (R_ZB[zb], R_const), (R_TR,))
                    cp(cpalt(), KB[:, kt * 128:(kt + 1) * 128], TRp[:, 0:128], (R_TR,), (R_KB,))
            mark('B_expand_%d%d' % (l, pa))
            nkeys = nkt * 128
            for h in range(4):
                k0 = 0
                while k0 < nkeys:
                    w = min(512, nkeys - k0)
                    z = nxt("z", 2)
                    for rc in range(2):
                        mm(Zp[z][:, 0:w], wukS[:, rc, h * 128:(h + 1) * 128], CL[:, rc, k0:k0 + w], rc == 0, rc == 1,
                           (R_wu, R_CL), (R_Z[z],))
                    cp(cpalt(), KA[:, h, k0:k0 + w], Zp[z][:, 0:w], (R_Z[z],), (R_KA,))
                    k0 += w
            for kt in range(nkt):
                z = nxt("z", 2)
                for rc in range(2):
                    mm(Zp[z][:], CL[:, rc, kt * 128:(kt + 1) * 128], wuvS[:, rc, :], rc == 0, rc == 1,
                       (R_wu, R_CL), (R_Z[z],))
                cp(cpalt(), VA[:, kt, 0:4, 0:128], Zp[z][:].rearrange("p (g d) -> p g d", g=4), (R_Z[z],), (R_VA,))

            def partsB(h, kt, q0, w):
                hh_ = h % 2
                return [(KA[:, h, kt * 128:(kt + 1) * 128], QA[:, h, q0:q0 + w]),
                        (KB[64 * hh_:64 * hh_ + 64, kt * 128:(kt + 1) * 128], QB[64 * hh_:64 * hh_ + 64, h // 2, q0:q0 + w])]

            mark('B_attn_%d%d' % (l, pa))
            run_attn(list(range(4)), partsB, lambda h, kt: VA[:, kt, h, 0:129], 192.0 ** -0.5,
                     fin_std(lambda h: 4 + h, False))

            def hC1(t_, zp, rz, st2):
                zi = zs_load(zp, rz, 512)
                rms_rows(ZS[zi][:, 0:512].rearrange("p (g d) -> p g d", g=4), 4, 128, slice(24, 28),
                         smallS[:, 260:388].unsqueeze(1).broadcast_to([128, 4, 128]), (R_ZS[zi],), (R_ZS[zi],))
                kq_path(t_, zi, 0, 512, 32, QA[:, 0:4, tok(t_)], R_QA, st2)

            def hC2(t_, zp, rz, st2):
                zi = zs_load(zp, rz, 512)
                rms_rows(ZS[zi][:, 0:256].rearrange("p (g d) -> p g d", g=2), 2, 128, slice(24, 26),
                         smallS[:, 388:516].unsqueeze(1).broadcast_to([128, 2, 128]), (R_ZS[zi],), (R_ZS[zi],))
                if not isS:
                    out_heads(t_, zi, 0, 2, nck)
                    out_heads(t_, zi, 256, 2, ncv)
                kq_path(t_, zi, 0, 256, 32, KA[:, 0:2, ktok(t_)], R_KA, st2)
                v_path(t_, zi, 256, 2)

            mark('C_proj_%d%d' % (l, pa))
            proj(O_CQ, 512, hC1)
            proj(O_CKV, 512, hC2)
            if isS:
                ctx_k_heads(cck, 2)
                ctx_v_heads(ccv, 2)
            mark('C_attn_%d%d' % (l, pa))
            run_attn(list(range(4)),
                     lambda h, kt, q0, w: [(KA[:, h // 2, kt * 128:(kt + 1) * 128], QA[:, h, q0:q0 + w])],
                     lambda h, kt: VA[:, kt, h // 2, 0:129], 128.0 ** -0.5, fin_std(lambda h: 8 + h, False))

            def hD1(t_, zp, rz, st2):
                zi = zs_load(zp, rz, 512)
                kq_path(t_, zi, 0, 512, 16, QA[:, 0:4, tok(t_)], R_QA, st2)

            def hD2(t_, zp, rz, st2):
                zi = zs_load(zp, rz, 512)
                if not isS:
                    out_heads(t_, zi, 0, 4, ndk)
                kq_path(t_, zi, 0, 512, 16, KA[:, 0:4, ktok(t_)], R_KA, st2)

            def hD3(t_, zp, rz, st2):
                zi = zs_load(zp, rz, 512)
                if not isS:
                    out_heads(t_, zi, 0, 4, ndv)
                v_path(t_, zi, 0, 4)

            mark('D_proj_%d%d' % (l, pa))
            proj(O_DQ, 512, hD1)
            proj(O_DK, 512, hD2)
            proj(O_DV, 512, hD3)
            if isS:
                ctx_k_heads(cdk, 4)
                ctx_v_heads(cdv, 4)

            def partsD(hm, kt, q0, w):
                h, m = hm
                return [(KA[64 * m:64 * m + 64, h, kt * 128:(kt + 1) * 128], QA[64 * m:64 * m + 64, h, q0:q0 + w])]

            def finD(hm, ch):
                h, m = hm
                n = len(ch)
                q0 = ch[0] * 128
                w = n * 128
                cp("dve", SC[:, 32:32 + n], Opv[:, 0:n, 128], (R_O,), (R_sc,))
                recip(SC[:, 32:32 + n], SC[:, 32:32 + n], (R_sc,), (R_sc,))
                rb = SC[:, 32:32 + n].unsqueeze(2).broadcast_to([128, n, 128])
                if m == 0:
                    tt("dve", O1S[:, 0:n, :], Opv[:, 0:n, 0:128], rb, ALU.mult, (R_O, R_sc), (R_O1S,))
                    return None
                dv_ = SQ2[:, 0:w].rearrange("p (j d) -> p j d", j=n)
                tt("dve", dv_, Opv[:, 0:n, 0:128], rb, ALU.mult, (R_O, R_sc), (R_SQ2,))
                stt("dve", dv_, dv_, SC[:, 10:11], O1S[:, 0:n, :], ALU.mult, ALU.add, (R_SQ2, R_sc, R_O1S), (R_SQ2,))
                sqv = ZS[0][:, 0:w].rearrange("p (j d) -> p j d", j=n)
                act(sqv, dv_, AF.Square, (R_SQ2,), (R_ZS[0],))
                S.op("dve", lambda e: e.tensor_reduce(SC[:, 40:40 + n], sqv, AX.X, ALU.add), (R_ZS[0],), (R_sc,))
                act(SC[:, 40:40 + n], SC[:, 40:40 + n], AF.Sqrt, (R_sc, R_const), (R_sc,), bias=epsT[:, 0:1], scale=1.0 / 128)
                recip(SC[:, 40:40 + n], SC[:, 40:40 + n], (R_sc,), (R_sc,))
                tt("dve", dv_, dv_, SC[:, 40:40 + n].unsqueeze(2).broadcast_to([128, n, 128]), ALU.mult,
                   (R_SQ2, R_sc), (R_SQ2,))
                ob = nxt("ob", 2)
                tt("dve", OB[ob][:, 0:n, :], dv_, DGt[:].unsqueeze(1).broadcast_to([128, n, 128]), ALU.mult,
                   (R_SQ2, R_dg), (R_OB[ob],))
                def s2():
                    for j in range(n):
                        tr(TRp[:, j * 128:(j + 1) * 128], OB[ob][:, j, :], identB[:], (R_OB[ob], R_const), (R_TR,))
                    cp(cpalt(), OTt[:, 12 + h, q0:q0 + w], TRp[:, 0:w], (R_TR,), (R_OT,))
                return s2

            def run_attn_D():
                for (qt_list, kt_list) in seqs:
                    chunks = [qt_list[i:i + 4] for i in range(0, len(qt_list), 4)]
                    for h in range(4):
                        for ch in chunks:
                            for m in range(2):
                                q0 = ch[0] * 128
                                w = len(ch) * 128
                                def pv(n_, kt, p_, nk=len(kt_list), nch=len(ch), h=h):
                                    for j in range(nch):
                                        mm(Oj(j), PT[p_][:, j * 128:(j + 1) * 128], VA[:, kt, h, 0:129],
                                           n_ == 0 and j % 2 == 0, n_ == nk - 1, (R_PT[p_], R_VA), (R_O,), sgc=True)
                                pend = None
                                for n_, kt in enumerate(kt_list):
                                    s_ = nxt("st", 2)
                                    (lh, rh), = partsD((h, m), kt, q0, w)
                                    mm(STp[s_][:, 0:w], lh, rh, True, True, (R_KA, R_QA), (R_ST[s_],))
                                    p_ = nxt("pt", 3)
                                    act(PT[p_][:, 0:w], STp[s_][:, 0:w], AF.Exp, (R_ST[s_],), (R_PT[p_],), scale=64.0 ** -0.5)
                                    if n_ == 1 and fin_pend[0] is not None:
                                        fin_pend[0]()
                                        fin_pend[0] = None
                                    if pend is not None:
                                        pv(*pend)
                                    pend = (n_, kt, p_)
                                pv(*pend)
                                if fin_pend[0] is not None:
                                    fin_pend[0]()
                                fin_pend[0] = finD((h, m), ch)
                        ada_pump(1)
                if fin_pend[0] is not None:
                    fin_pend[0]()
                    fin_pend[0] = None

            mark('D_attn_%d%d' % (l, pa))
            run_attn_D()
            ada_pump(len(ada_pending))
            S.barrier()

        def phase_wout(l, pa):
            ce = 0
            for cg in range(4):
                wv, rw = w_get(("out", l, pa, cg))
                for j in range(4):
                    dc = cg * 4 + j
                    for th in range(2):
                        zb_, rz_ = bank6()
                        for e_ in range(KC):
                            mm(zb_[:], wv[:, e_, j * 128:(j + 1) * 128], OTt[:, e_, th * 512:(th + 1) * 512],
                               e_ == 0, e_ == KC - 1, (rw, R_OT), (rz_,))
                        ce ^= 1
                        cp("act" if ce else "dve", BIGt[:, dc, th * 512:(th + 1) * 512], zb_[:], (rz_,), (R_BIG[dc],))
            S.barrier()

        def phase_mlp(l, pa):
            rt = [0]

            def up(fg):
                ab = fg % 2
                wu, ru = w_get(("up", l, pa, fg))
                for fc in range(4):
                    for th in range(2):
                        zb_, rz_ = bank6()
                        for k in range(KC):
                            mm(zb_[:], wu[:, k, fc * 128:(fc + 1) * 128], Ht[:, k, th * 512:(th + 1) * 512],
                               k == 0, k == KC - 1, (ru, R_H), (rz_,))
                        rt[0] ^= 1
                        r_ = rt[0]
                        act(RT[r_], zb_[:], AF.Relu, (rz_,), (R_RT[r_],))
                        tt("dve", ACTT[ab][:, fc, th * 512:(th + 1) * 512], RT[r_], RT[r_], ALU.mult,
                           (R_RT[r_],), (R_ACTT[ab],))

            def down(fg):
                ab = fg % 2
                wd, rd = w_get(("down", l, pa, fg))
                for dc in range(KC):
                    for th in range(2):
                        zb_, rz_ = bank6()
                        for fc in range(4):
                            mm(zb_[:], wd[:, fc, dc * 128:(dc + 1) * 128], ACTT[ab][:, fc, th * 512:(th + 1) * 512],
                               fc == 0, fc == 3, (rd, R_ACTT[ab]), (rz_,))
                        dst = BIGt[:, dc, th * 512:(th + 1) * 512]
                        if fg == 0:
                            cp("dve", dst, zb_[:], (rz_,), (R_BIG[dc],))
                        else:
                            tt("dve", dst, dst, zb_[:], ALU.add, (rz_, R_BIG[dc]), (R_BIG[dc],))

            up(0)
            for fg in range(16):
                if fg + 1 < 16:
                    up(fg + 1)
                down(fg)

        def phase_final(pa):
            rt = 0
            for t_ in range(NT):
                for cg in range(4):
                    z = nxt("z", 2)
                    for j in range(4):
                        c = cg * 4 + j
                        tr(Zp[z][:, j * 128:(j + 1) * 128], BIGt[:, c, t_ * 128:(t_ + 1) * 128], identF[:],
                           (R_BIG[c], R_const), (R_Z[z],))
                    rt ^= 1
                    cp("act" if rt else "dve", RT[rt], Zp[z][:], (R_Z[z],), (R_RT[rt],))
                    S.dma("sp", y_out[pa][t_ * 128:(t_ + 1) * 128, cg * 512:(cg + 1) * 512], RT[rt], (R_RT[rt],), (),
                          is_out=True)

        def mark(name):
            PHASES.append((name, dict(S.cnt)))
        mark('xt')
        phase_xt(0)
        phase_xt(1)
        for l in range(DEPTH):
            mark('ada%d' % l)
            phase_ada(l)
            for pa in range(2):
                mark('norm1_%d%d' % (l, pa))
                norm_from_scratch(pa, 0, 1)
                S.barrier()
                mark('attn_%d%d' % (l, pa))
                phase_attn(l, pa)
                if l == 0 and pa == 0:
                    par_rows(0, (2, 3, 4, 5))
                mark('wout_%d%d' % (l, pa))
                phase_wout(l, pa)
                mark('post1_%d%d' % (l, pa))
                post_residual(pa, 2, True)
                norm_from_big(pa, 3, 4)
                mark('mlp_%d%d' % (l, pa))
                phase_mlp(l, pa)
                mark('post2_%d%d' % (l, pa))
                post_residual(pa, 5, l == 0)
                if l == DEPTH - 1:
                    phase_final(pa)
                S.barrier()
        mark('end')
        if plan is None:
            return wkeys
        assert wstate["next_get"] == len(wseq), (wstate, len(wseq))

        S.plan_signals()
        nep = {e: (S.nsig[e] + EPOCH - 1) // EPOCH for e in ENGS}
        sems = {}
        for e in ENGS:
            for ep in range(max(1, nep[e])):
                sems[("e", e, ep)] = es.enter_context(nc.semaphore("s_%s_%d" % (e, ep)))
        for j in range(NDSEM):
            sems[("d", j)] = es.enter_context(nc.semaphore("d_%d" % j))
        block = es.enter_context(nc.Block())

        @block.tensor
        def _(e):
            S.emit("pe", e, sems)

        @block.scalar
        def _(e):
            S.emit("act", e, sems)

        @block.vector
        def _(e):
            S.emit("dve", e, sems)

        @block.gpsimd
        def _(e):
            S.emit("pool", e, sems)

        @block.sync
        def _(e):
            S.emit("sp", e, sems)
            for j in range(NDSEM):
                if S.dma_uses[j] > 0:
                    e.wait_ge(sems[("d", j)], 16 * S.dma_uses[j])
    return nc


_CACHE = {}


def kernel(x_prompt, x_sample, cache_a_k, cache_a_v, cache_mla, cache_c_k, cache_c_v, cache_d_k, cache_d_v,
           c, c_ctx, w_ada, b_ada, g_attn_pre, g_attn_post, g_mlp_pre, g_mlp_post, w_in, a_sink,
           b_g_kv, b_w_uk, b_w_uv, c_gq, c_gk, d_lam, d_g_out, w_out, w_up, w_down):
    f = lambda a: np.ascontiguousarray(np.asarray(a, dtype=np.float32))
    x_prompt, x_sample = f(x_prompt), f(x_sample)
    if "nc" not in _CACHE:
        _CACHE["nc"] = build()
    nc = _CACHE["nc"]
    c32, s32, c16, s16 = _rope_tables()
    ident = np.eye(128, dtype=np.float32)
    masks = np.concatenate([np.tril(np.ones((128, 128), np.float32)), np.triu(np.ones((128, 128), np.float32))], axis=1)
    gT = np.stack([f(g_attn_pre), f(g_attn_post), f(g_mlp_pre), f(g_mlp_post)], axis=1)
    gT = np.ascontiguousarray(gT.reshape(2, 4, KC, 128).transpose(3, 0, 1, 2))
    bT = np.ascontiguousarray(f(b_ada).reshape(2, 96, 128).transpose(2, 0, 1))
    small = np.concatenate([f(a_sink), f(b_g_kv), f(c_gq), f(c_gk), f(d_g_out), f(d_lam).reshape(2, 256)], axis=1)
    small = np.ascontiguousarray(np.broadcast_to(small[None], (128, 2, 900)))
    shared = {
        "w_ada": f(w_ada), "w_in": f(w_in), "w_out": f(w_out), "w_up": f(w_up), "w_down": f(w_down),
        "wuk": f(b_w_uk).reshape(2, 256, 512), "wuv": f(b_w_uv).reshape(2, 256, 512),
        "gT": gT, "bT": bT, "small": small, "ident": ident, "masks": masks,
        "rc32": c32, "rs32": s32, "rc16": c16, "rs16": s16,
    }
    cc, cctx = f(c), f(c_ctx)
    in_maps = []
    for b in range(NCORES):
        cT = np.stack([cctx, cc[b]], axis=-1).reshape(KC, 128, 2).transpose(1, 0, 2)
        m = dict(shared)
        m.update({
            "xp": x_prompt[4 * b:4 * b + 4].reshape(T, D), "xs": x_sample[b],
            "cak": f(cache_a_k[b]), "cav": f(cache_a_v[b]), "cmla": f(cache_mla[b]),
            "cck": f(cache_c_k[b]), "ccv": f(cache_c_v[b]), "cdk": f(cache_d_k[b]), "cdv": f(cache_d_v[b]),
            "cT": np.ascontiguousarray(cT),
        })
        in_maps.append(m)
    res = run_bass_kernel_spmd(nc, in_maps, core_ids=list(range(NCORES)))
    R = res.results
    cat = lambda k: np.concatenate([np.asarray(r[k], dtype=np.float32) for r in R], axis=0)
    y_prompt = cat("yp").reshape(32, 256, D)
    y_sample = np.stack([np.asarray(r["ys"], dtype=np.float32) for r in R], axis=0)
    return (y_prompt, y_sample, cat("nak"), cat("nav"), cat("nmla"), cat("nck"), cat("ncv"), cat("ndk"), cat("ndv"))
```

```python
import math
from contextlib import ExitStack
import numpy as np
import concourse.bass as bass
import concourse.mybir as mybir
from concourse.bass_utils import run_bass_kernel_spmd

F32 = mybir.dt.float32
BF16 = mybir.dt.bfloat16
AF = mybir.ActivationFunctionType
ALU = mybir.AluOpType
AX = mybir.AxisListType

D = 2048
KC = 16
T = 1024
NT = 8
DEPTH = 2
DFF = 8192
INW = 4672
EPS = 1e-6
NCORES = 8
O_AQ, O_AKV, O_BQN, O_BQRC, O_BKR, O_CQ, O_CKV, O_DQ, O_DK, O_DV = 0, 512, 1024, 1536, 2048, 2112, 2624, 3136, 3648, 4160

PHASES = []
EPOCH = 12000
NDSEM = 16
NSW = 5
ENGS = ("pe", "act", "dve", "pool", "sp")
SAME_SYNC = {"pe": False, "act": True, "dve": True, "pool": True, "sp": False}


class Res:
    __slots__ = ("name", "w", "rs", "persist")

    def __init__(self, name, persist=False):
        self.name = name
        self.w = None
        self.rs = {}
        self.persist = persist


class Sched:
    def __init__(self):
        self.q = {e: [] for e in ENGS}
        self.cnt = {e: 0 for e in ENGS}
        self.nop = {e: 0 for e in ENGS}
        self.dma_uses = [0] * NDSEM
        self.dma_rr = 0
        self.dma_rr_sw = 0
        self.all_res = []
        self.pend = {e: {} for e in ENGS}
        self.out_events = []
        self.rank = {}
        self.nsig = {e: 0 for e in ENGS}

    def res(self, name, persist=False):
        r = Res(name, persist)
        self.all_res.append(r)
        return r

    @staticmethod
    def _need(d, ev):
        if ev is None:
            return
        if d.get(ev[0], 0) < ev[1]:
            d[ev[0]] = ev[1]

    def _deps(self, eng, reads, writes):
        allw, raw = {}, {}
        for r in reads:
            self._need(allw, r.w)
            self._need(raw, r.w)
        for r in writes:
            self._need(allw, r.w)
            for k, v in r.rs.items():
                self._need(allw, (k, v))
        for k, v in self.pend[eng].items():
            if allw.get(k, 0) < v:
                allw[k] = v
        self.pend[eng] = {}
        return allw, raw

    def _commit(self, ev, reads, writes):
        for r in reads:
            if r.rs.get(ev[0], 0) < ev[1]:
                r.rs[ev[0]] = ev[1]
        for r in writes:
            r.w = ev
            r.rs = {}

    def op(self, eng, fn, reads=(), writes=()):
        allw, raw = self._deps(eng, reads, writes)
        self.cnt[eng] += 1
        self.nop[eng] += 1
        ev = (("e", eng), self.nop[eng])
        self.q[eng].append([fn, allw, raw, ev])
        self._commit(ev, reads, writes)
        return ev

    def dma(self, eng, out, in_, reads=(), writes=(), is_out=False):
        allw, raw = self._deps(eng, reads, writes)
        if eng == "pool":
            j = self.dma_rr_sw
            self.dma_rr_sw = (j + 1) % NSW
        else:
            j = NSW + self.dma_rr
            self.dma_rr = (self.dma_rr + 1) % (NDSEM - NSW)
        prev = self.dma_uses[j]
        self.dma_uses[j] += 1
        key = ("d", j)
        if prev > 0 and allw.get(key, 0) < 16 * prev:
            allw[key] = 16 * prev
        ev = (key, 16 * (prev + 1))
        self.cnt[eng] += 1
        self.q[eng].append([lambda e: e.dma_start(out=out, in_=in_), allw, raw, ev])
        self._commit(ev, reads, writes)
        if is_out:
            self.out_events.append(ev)
        return ev

    def barrier(self):
        waits = {}
        for r in self.all_res:
            if r.persist:
                continue
            evs = list(r.rs.items())
            if r.w is not None:
                evs.append(r.w)
            for k, v in evs:
                if waits.get(k, 0) < v:
                    waits[k] = v
            r.w = None
            r.rs = {}
        for e in ENGS:
            for k, v in waits.items():
                if self.pend[e].get(k, 0) < v:
                    self.pend[e][k] = v

    def plan_signals(self):
        needed = {e: set() for e in ENGS}
        for eng in ENGS:
            seen = {}
            for ent in self.q[eng]:
                fn, allw, raw, ev = ent
                eff = {}
                for k, v in allw.items():
                    if k[0] == "e" and k[1] == eng and not SAME_SYNC[eng]:
                        continue
                    if seen.get(k, 0) >= v:
                        continue
                    seen[k] = v
                    eff[k] = v
                    if k[0] == "e":
                        needed[k[1]].add(v)
                ent.append(eff)
        for e in ENGS:
            for r, idx in enumerate(sorted(needed[e])):
                self.rank[(e, idx)] = r
            self.nsig[e] = len(needed[e])

    def emit(self, eng, e, sems):
        for fn, allw, raw, ev, eff in self.q[eng]:
            for k, v in eff.items():
                if k[0] == "e":
                    r = self.rank[(k[1], v)]
                    e.wait_ge(sems[("e", k[1], r // EPOCH)], r % EPOCH + 1)
                else:
                    e.wait_ge(sems[k], v)
            ins = fn(e)
            if ev[0][0] == "d":
                ins.then_inc(sems[ev[0]], 16)
            else:
                r = self.rank.get((eng, ev[1]))
                if r is not None:
                    ins.then_inc(sems[("e", eng, r // EPOCH)], 1)


def _rope_tables():
    t = np.arange(T)
    row = (t // 64).astype(np.float32)
    col = (t % 64).astype(np.float32)

    def tab(half):
        inv = (np.float32(10000.0) ** (-(np.arange(half, dtype=np.float32) / np.float32(half)))).astype(np.float32)
        ar = (row[:, None] * inv[None, :]).astype(np.float32)
        ac = (col[:, None] * inv[None, :]).astype(np.float32)
        c = np.stack([np.cos(ar), np.cos(ac)], axis=1).astype(np.float32)
        s = np.stack([np.sin(ar), np.sin(ac)], axis=1).astype(np.float32)
        c = c.reshape(NT, 128, 2, half).transpose(1, 0, 2, 3)
        s = s.reshape(NT, 128, 2, half).transpose(1, 0, 2, 3)
        return np.ascontiguousarray(c), np.ascontiguousarray(s)

    c32, s32 = tab(32)
    c16, s16 = tab(16)
    return c32, s32, c16, s16


def build():
    plan = _build(None)
    return _build(plan)


def _build(plan):
    nc = bass.Bass("TRN2", target_bir_lowering=False)
    del PHASES[:]
    S = Sched()

    def din(name, shape):
        return nc.dram_tensor(name, list(shape), F32, kind="ExternalInput").ap()

    def dout(name, shape):
        return nc.dram_tensor(name, list(shape), F32, kind="ExternalOutput").ap()

    x_in = [din("xp", [T, D]), din("xs", [T, D])]
    cak = din("cak", [2, 2, 256, 128]); cav = din("cav", [2, 2, 256, 128])
    cmla = din("cmla", [2, 256, 320])
    cck = din("cck", [2, 2, 256, 128]); ccv = din("ccv", [2, 2, 256, 128])
    cdk = din("cdk", [2, 4, 256, 128]); cdv = din("cdv", [2, 4, 256, 128])
    cT_d = din("cT", [128, KC, 2])
    gT_d = din("gT", [128, 2, 4, KC])
    bT_d = din("bT", [128, 2, 96])
    w_ada = din("w_ada", [2, D, 6 * D])
    w_in = din("w_in", [2, D, INW])
    w_out = din("w_out", [2, D, D])
    w_up = din("w_up", [2, D, DFF])
    w_down = din("w_down", [2, DFF, D])
    wuk_d = din("wuk", [2, 256, 512]); wuv_d = din("wuv", [2, 256, 512])
    small_d = din("small", [128, 2, 900])
    ident_d = din("ident", [128, 128])
    masks_d = din("masks", [128, 256])
    rc32_d = din("rc32", [128, NT, 2, 32]); rs32_d = din("rs32", [128, NT, 2, 32])
    rc16_d = din("rc16", [128, NT, 2, 16]); rs16_d = din("rs16", [128, NT, 2, 16])

    y_out = [dout("yp", [T, D]), dout("ys", [T, D])]
    nak = dout("nak", [4, 2, 2, 256, 128]); nav = dout("nav", [4, 2, 2, 256, 128])
    nmla = dout("nmla", [4, 2, 256, 320])
    nck = dout("nck", [4, 2, 2, 256, 128]); ncv = dout("ncv", [4, 2, 2, 256, 128])
    ndk = dout("ndk", [4, 2, 4, 256, 128]); ndv = dout("ndv", [4, 2, 4, 256, 128])
    xscr = [nc.dram_tensor("xscr%d" % i, [KC, 128, T], F32, kind="Internal").ap() for i in range(2)]
    XSC = [[S.res("xscr%d_%d" % (i, c)) for c in range(KC)] for i in range(2)]

    es = ExitStack()
    with es:
        def sb(name, shape, dt):
            return es.enter_context(nc.sbuf_tensor(name, list(shape), dt))

        def ps(name, shape, dt):
            return es.enter_context(nc.psum_tensor(name, list(shape), dt))

        es.enter_context(nc.allow_low_precision("bf16 matmul operands, fp32 accumulation"))
        WRt = sb("WR", [128, 3, 8192], BF16)
        Ht = sb("H", [128, KC, T], BF16)
        OTt = sb("OT", [128, KC, T], BF16)
        BIGt = sb("BIG", [128, KC, T], F32)
        RSt = sb("RS", [128, T], F32)
        identF = sb("identF", [128, 128], F32)
        identB = sb("identB", [128, 128], BF16)
        onesF = sb("onesF", [128, 128], F32)
        onesB = sb("onesB", [128, 128], BF16)
        maskB = sb("maskB", [128, 256], BF16)
        rc32 = sb("rc32s", [128, NT, 2, 32], F32); rs32 = sb("rs32s", [128, NT, 2, 32], F32)
        rc16 = sb("rc16s", [128, NT, 2, 16], F32); rs16 = sb("rs16s", [128, NT, 2, 16], F32)
        cTs = sb("cTs", [128, KC, 2], F32)
        sTs = sb("sTs", [128, KC, 2], BF16)
        gTs = sb("gTs", [128, 2, 4, KC], F32)
        bTs = sb("bTs", [128, 2, 96], F32)
        modT = sb("modT", [128, 2, 96, 2], F32)
        PAR = sb("PAR", [128, 2, 6, KC], F32)
        smallS = sb("smallS", [128, 900], F32)
        wukS = sb("wukS", [128, 2, 512], BF16); wuvS = sb("wuvS", [128, 2, 512], BF16)
        SC = sb("SC", [128, 64], F32)
        epsT = sb("epsT", [128, 1], F32)
        DGt = sb("DG", [128, 128], F32)
        LAMt = sb("LAMt", [128, 128], F32)

        STp = [ps("ST%d" % i, [128, 512], F32) for i in range(2)]
        Op = ps("Oacc", [128, 2, 512], F32)
        Zp = [ps("Z%d" % i, [128, 512], F32) for i in range(2)]
        TRp = ps("TRB", [128, 1024], BF16)
        Mp = ps("MISC", [128, 512], F32)
        R_ST = [S.res("ST%d" % i) for i in range(2)]
        R_O = S.res("O")
        R_Z = [S.res("Z%d" % i) for i in range(2)]
        R_TR = S.res("TR")
        R_M = S.res("M")

        BANKS6 = [(Zp[0], R_Z[0]), (Zp[1], R_Z[1]), (STp[0], R_ST[0]), (STp[1], R_ST[1]), (Mp, R_M)]
        BANKS3 = [(Zp[0], R_Z[0]), (Zp[1], R_Z[1]), (Mp, R_M)]

        def bank6():
            return BANKS6[nxt("bk6", len(BANKS6))]

        def bank3():
            return BANKS3[nxt("bk3", len(BANKS3))]

        R_WR = [S.res("WR%d" % i, persist=True) for i in range(3)]
        R_H = S.res("H"); R_OT = S.res("OT"); R_RS = S.res("RS")
        R_BIG = [S.res("BIG%d" % c) for c in range(KC)]
        R_const = S.res("const"); R_mod = S.res("mod"); R_par = S.res("par")
        R_lam = S.res("lam"); R_small = S.res("small"); R_wu = S.res("wu"); R_sc = S.res("sc"); R_dg = S.res("dg")

        OTf = OTt[:].bitcast(F32)
        OTflat = OTt[:].rearrange("p c t -> p (c t)").bitcast(F32)
        XC = [OTflat[:, i * 1024:(i + 1) * 1024] for i in range(2)]
        SQh = [OTflat[:, 2048 + i * 512: 2048 + (i + 1) * 512] for i in range(2)]
        SQbf = [SQh[i].bitcast(BF16) for i in range(2)]
        R_XC = [S.res("XC%d" % i) for i in range(2)]
        R_SQ = [S.res("SQ%d" % i) for i in range(2)]
        OTb = OTt[:].rearrange("p c t -> p (c t)")
        ACTT = [OTb[:, 6144 + i * 4096: 6144 + (i + 1) * 4096].rearrange("p (f t) -> p f t", f=4) for i in range(2)]
        R_ACTT = [S.res("ACTT%d" % i) for i in range(2)]
        RT = [OTflat[:, 7168 + i * 512: 7168 + (i + 1) * 512] for i in range(2)]
        R_RT = [S.res("RT%d" % i) for i in range(2)]

        BIGf = BIGt[:].rearrange("p c t -> p (c t)")
        BIGb = BIGf.bitcast(BF16)
        off = [0]

        def carve_b(n):
            a = BIGb[:, off[0]: off[0] + n]
            off[0] += n
            return a

        QA = carve_b(4096).rearrange("p (h t) -> p h t", h=4)
        QB = carve_b(2048).rearrange("p (h t) -> p h t", h=2)
        KA = carve_b(5120).rearrange("p (h t) -> p h t", h=4)
        KB = carve_b(1280)
        CL = carve_b(2560).rearrange("p (h t) -> p h t", h=2)
        VA = carve_b(5200).rearrange("p (k g d) -> p k g d", k=10, g=4)
        PT = [carve_b(512) for _ in range(3)]
        ZB = [carve_b(512) for _ in range(4)]
        OB = [carve_b(512).rearrange("p (j d) -> p j d", j=4) for _ in range(2)]
        CST = carve_b(1024).rearrange("p (k g d) -> p k g d", k=2, g=4)
        CM = carve_b(640).rearrange("p (k c) -> p k c", k=2)
        assert off[0] % 2 == 0
        foff = [off[0] // 2]

        def carve_f(n):
            a = BIGf[:, foff[0]: foff[0] + n]
            foff[0] += n
            return a

        ZS = [carve_f(512) for _ in range(3)]
        ZT = [carve_f(256) for _ in range(2)]
        O1S = carve_f(512).rearrange("p (j d) -> p j d", j=4)
        SQ2 = carve_f(512)
        assert foff[0] <= 16384, foff[0]
        R_QA = S.res("QA"); R_QB = S.res("QB"); R_KA = S.res("KA"); R_KB = S.res("KB"); R_CL = S.res("CL")
        R_VA = S.res("VA")
        R_PT = [S.res("PT%d" % i) for i in range(3)]
        R_ZB = [S.res("ZB%d" % i) for i in range(4)]
        R_OB = [S.res("OB%d" % i) for i in range(2)]
        R_CST = S.res("CST"); R_CM = S.res("CM")
        R_ZS = [S.res("ZS%d" % i) for i in range(3)]
        R_ZT = [S.res("ZT%d" % i) for i in range(2)]
        R_ZTP = [S.res("ZTP%d" % i) for i in range(2)]
        R_O1S = S.res("O1S"); R_SQ2 = S.res("SQ2")
        rr = {"zs": 0, "zb": 0, "z": 0, "pt": 0, "st": 0, "ob": 0, "xc": 0, "sq": 0, "bk6": 0, "bk3": 0}

        def nxt(key, n):
            v = rr[key]
            rr[key] = (v + 1) % n
            return v

        def mm(out, lhsT, rhs, start, stop, reads, writes, sgc=False):
            if sgc:
                return S.op("pe", lambda e: e.matmul(out, lhsT, rhs, start=start, stop=stop, skip_group_check=True),
                            reads, writes)
            return S.op("pe", lambda e: e.matmul(out, lhsT, rhs, start=start, stop=stop), reads, writes)

        def tr(out, in_, ident, reads, writes):
            return S.op("pe", lambda e: e.transpose(out, in_, ident), reads, writes)

        def act(out, in_, func, reads, writes, bias=None, scale=None, accum=None):
            kw = {}
            if bias is not None:
                kw["bias"] = bias
            if scale is not None:
                kw["scale"] = scale
            if accum is not None:
                kw["accum_out"] = accum
            return S.op("act", lambda e: e.activation(out, in_, func, **kw), reads, writes)

        def tt(eng, out, a, b, op, reads, writes):
            return S.op(eng, lambda e: e.tensor_tensor(out, a, b, op), reads, writes)

        def ts(eng, out, a, s1, op0, reads, writes, s2=None, op1=None):
            if s2 is None:
                return S.op(eng, lambda e: e.tensor_scalar(out, a, s1, None, op0), reads, writes)
            return S.op(eng, lambda e: e.tensor_scalar(out, a, s1, s2, op0, op1), reads, writes)

        def stt(eng, out, a, s, b, op0, op1, reads, writes):
            return S.op(eng, lambda e: e.scalar_tensor_tensor(out, a, s, b, op0, op1), reads, writes)

        def cp(eng, out, in_, reads, writes):
            if eng == "act":
                return S.op("act", lambda e: e.copy(out, in_), reads, writes)
            return S.op(eng, lambda e: e.tensor_copy(out, in_), reads, writes)

        def recip(out, in_, reads, writes):
            return S.op("dve", lambda e: e.reciprocal(out, in_), reads, writes)

        def memset(eng, ap, v, writes):
            return S.op(eng, lambda e: e.memset(ap, v), (), writes)

        def w_desc(key):
            kind = key[0]
            if kind == "ada":
                _, l, g = key
                return (w_ada[l][:, g * 512:(g + 1) * 512].rearrange("(k p) c -> p k c", p=128), 8192, (16, 512))
            if kind == "in":
                _, l, pa, c0 = key
                n = 64 if c0 == O_BKR else 512
                return (w_in[l][:, c0:c0 + n].rearrange("(k p) c -> p k c", p=128), 16 * n, (16, n))
            if kind == "out":
                _, l, pa, cg = key
                return (w_out[l][:, cg * 512:(cg + 1) * 512].rearrange("(k p) c -> p k c", p=128), 8192, (16, 512))
            if kind == "up":
                _, l, pa, fg = key
                return (w_up[l][:, fg * 512:(fg + 1) * 512].rearrange("(k p) c -> p k c", p=128), 8192, (16, 512))
            _, l, pa, fg = key
            return (w_down[l][fg * 512:(fg + 1) * 512, :].rearrange("(f p) d -> p f d", p=128), 8192, (4, D))

        wkeys = list(plan) if plan is not None else []
        wseq = [w_desc(k) for k in wkeys]
        wstate = {"next_issue": 0, "next_get": 0}

        def w_issue_upto(n):
            while wstate["next_issue"] < min(n, len(wseq)):
                i = wstate["next_issue"]
                dram_ap, nel, shp = wseq[i]
                slot = i % 3
                dst = WRt[:, slot, 0:nel].rearrange("p (a b) -> p a b", a=shp[0])
                S.dma("pool", dst, dram_ap, (), (R_WR[slot],))
                wstate["next_issue"] += 1

        def w_get(key):
            i = wstate["next_get"]
            wstate["next_get"] += 1
            if plan is None:
                wkeys.append(key)
                dram_ap, nel, shp = w_desc(key)
            else:
                assert wkeys[i] == key, (i, wkeys[i], key)
                w_issue_upto(i + 3)
                dram_ap, nel, shp = wseq[i]
            slot = i % 3
            return WRt[:, slot, 0:nel].rearrange("p (a b) -> p a b", a=shp[0]), R_WR[slot]

        S.dma("sp", identF[:], ident_d, (), (R_const,))
        S.dma("pool", identB[:], ident_d, (), (R_const,))
        S.dma("pool", maskB[:], masks_d, (), (R_const,))
        S.dma("sp", rc32[:], rc32_d, (), (R_const,)); S.dma("sp", rs32[:], rs32_d, (), (R_const,))
        S.dma("sp", rc16[:], rc16_d, (), (R_const,)); S.dma("sp", rs16[:], rs16_d, (), (R_const,))
        S.dma("sp", cTs[:], cT_d, (), (R_const,))
        S.dma("sp", gTs[:], gT_d, (), (R_const,))
        S.dma("sp", bTs[:], bT_d, (), (R_const,))
        memset("dve", onesF[:], 1.0, (R_const,))
        memset("dve", onesB[:], 1.0, (R_const,))
        memset("dve", epsT[:], EPS, (R_const,))
        act(sTs[:], cTs[:], AF.Silu, (R_const,), (R_const,))
        if plan is not None:
            w_issue_upto(3)

        def stats_rstd(get_chunk, width, scale_inv):
            for c in range(KC):
                src, rds = get_chunk(c)
                i = c % 2
                act(SQbf[i], src, AF.Square, rds, (R_SQ[i],))
                for th in range(2):
                    mm(Zp[th][:], onesB[:], SQbf[i][:, th * 512:(th + 1) * 512], c == 0, c == KC - 1,
                       (R_SQ[i], R_const), (R_Z[th],))
            for th in range(2):
                act(RSt[:, th * 512:(th + 1) * 512], Zp[th][:], AF.Sqrt, (R_Z[th], R_const), (R_RS,),
                    bias=epsT[:, 0:1], scale=scale_inv)
            recip(RSt[:], RSt[:], (R_RS,), (R_RS,))

        def load_xc(pa, c):
            i = nxt("xc", 2)
            S.dma("sp", XC[i], xscr[pa][c], (XSC[pa][c],), (R_XC[i],))
            return i

        def apply_mod(pa, gi, si, c, src, rds):
            for th in range(2):
                i = nxt("sq", 2)
                hs = slice(th * 512, (th + 1) * 512)
                tt("dve", SQh[i], src[:, hs], RSt[:, hs], ALU.mult, tuple(rds) + (R_RS,), (R_SQ[i],))
                act(Ht[:, c, hs], SQh[i], AF.Identity, (R_SQ[i], R_par), (R_H,),
                    bias=PAR[:, pa, si, c:c + 1], scale=PAR[:, pa, gi, c:c + 1])

        def norm_from_scratch(pa, gi, si):
            def gc(c):
                i = load_xc(pa, c)
                return XC[i], (R_XC[i],)
            stats_rstd(gc, T, 1.0 / D)
            for c in range(KC):
                i = load_xc(pa, c)
                apply_mod(pa, gi, si, c, XC[i], (R_XC[i],))

        def post_residual(pa, ggi, to_scratch):
            def gc(c):
                return BIGt[:, c, :], (R_BIG[c],)
            stats_rstd(gc, T, 1.0 / D)
            for c in range(KC):
                i = load_xc(pa, c)
                for th in range(2):
                    q = nxt("sq", 2)
                    hs = slice(th * 512, (th + 1) * 512)
                    tt("dve", SQh[q], BIGt[:, c, hs], RSt[:, hs], ALU.mult, (R_BIG[c], R_RS), (R_SQ[q],))
                    stt("dve", BIGt[:, c, hs], SQh[q], PAR[:, pa, ggi, c:c + 1], XC[i][:, hs], ALU.mult, ALU.add,
                        (R_SQ[q], R_par, R_XC[i]), (R_BIG[c],))
                if to_scratch:
                    S.dma("sp", xscr[pa][c], BIGt[:, c, :], (R_BIG[c],), (XSC[pa][c],))

        def norm_from_big(pa, gi, si):
            def gc(c):
                return BIGt[:, c, :], (R_BIG[c],)
            stats_rstd(gc, T, 1.0 / D)
            for c in range(KC):
                apply_mod(pa, gi, si, c, BIGt[:, c, :], (R_BIG[c],))

        def phase_xt(pa):
            XIN = [BIGf[:, i * 2048:(i + 1) * 2048] for i in range(2)]
            XTS = [BIGf[:, 4096 + i * 2048: 4096 + (i + 1) * 2048].rearrange("p (c t) -> p c t", c=KC) for i in range(2)]
            R_XIN = [S.res("XIN%d" % i) for i in range(2)]
            R_XTS = [S.res("XTS%d" % i) for i in range(2)]
            for t_ in range(NT):
                b = t_ % 2
                S.dma("sp", XIN[b], x_in[pa][t_ * 128:(t_ + 1) * 128, :], (), (R_XIN[b],))
                for cg in range(4):
                    z = nxt("z", 2)
                    for j in range(4):
                        c = cg * 4 + j
                        tr(Zp[z][:, j * 128:(j + 1) * 128], XIN[b][:, c * 128:(c + 1) * 128], identF[:],
                           (R_XIN[b], R_const), (R_Z[z],))
                    cp("dve" if cg % 2 == 0 else "act", XTS[b][:, cg * 4:(cg + 1) * 4, :],
                       Zp[z][:].rearrange("p (j t) -> p j t", j=4), (R_Z[z],), (R_XTS[b],))
                S.dma("sp", xscr[pa][:, :, t_ * 128:(t_ + 1) * 128].rearrange("c p t -> p c t"), XTS[b],
                      (R_XTS[b],), tuple(XSC[pa]))
            S.barrier()

        def ada_group(l, g):
            wv, rw = w_get(("ada", l, g))
            for j in range(4):
                for k in range(KC):
                    mm(Mp[:, j * 2:(j + 1) * 2], wv[:, k, j * 128:(j + 1) * 128], sTs[:, k, :], k == 0, k == KC - 1,
                       (rw, R_const), (R_M,))
            for j in range(4):
                m = g * 4 + j
                ts("dve", modT[:, l, m, :], Mp[:, j * 2:(j + 1) * 2], bTs[:, l, m:m + 1], ALU.add,
                   (R_M, R_const), (R_mod,))

        ada_pending = []

        def ada_pump(n=1):
            for _ in range(n):
                if ada_pending:
                    la, g = ada_pending.pop(0)
                    ada_group(la, g)

        def ada_slots(l, pa, pi):
            if l == 0 and pa == 0:
                return [(0, 8 + 2 * pi), (0, 9 + 2 * pi)] if pi < 8 else []
            if l == 0 and pa == 1:
                if pi < 4:
                    return [(1, 3 * pi + i) for i in range(3)]
                return [(1, 12 + 2 * (pi - 4) + i) for i in range(2)]
            return []

        def par_rows(l, rows):
            for pa in range(2):
                def mch(i):
                    return modT[:, l, i * 16:(i + 1) * 16, pa]
                rds = (R_mod, R_const)
                if 0 in rows:
                    stt("dve", PAR[:, pa, 0, :], mch(1), 1.0, gTs[:, l, 0, :], ALU.add, ALU.mult, rds, (R_par,))
                if 1 in rows:
                    cp("dve", PAR[:, pa, 1, :], mch(0), rds, (R_par,))
                if 2 in rows:
                    tt("dve", PAR[:, pa, 2, :], mch(2), gTs[:, l, 1, :], ALU.mult, rds, (R_par,))
                if 3 in rows:
                    stt("dve", PAR[:, pa, 3, :], mch(4), 1.0, gTs[:, l, 2, :], ALU.add, ALU.mult, rds, (R_par,))
                if 4 in rows:
                    cp("dve", PAR[:, pa, 4, :], mch(3), rds, (R_par,))
                if 5 in rows:
                    tt("dve", PAR[:, pa, 5, :], mch(5), gTs[:, l, 3, :], ALU.mult, rds, (R_par,))

        def phase_ada(l):
            if l == 0:
                for g in range(8):
                    ada_group(0, g)
                par_rows(0, (0, 1))
            else:
                par_rows(1, (0, 1, 2, 3, 4, 5))
            S.dma("sp", smallS[:], small_d[:, l, :], (), (R_small,))
            S.dma("pool", wukS[:], wuk_d[l].rearrange("(c p) n -> p c n", p=128), (), (R_wu,))
            S.dma("pool", wuvS[:], wuv_d[l].rearrange("(c p) n -> p c n", p=128), (), (R_wu,))
            act(SC[:, 0:4], smallS[:, 0:4], AF.Exp, (R_small,), (R_sc,))
            lam_init = 0.8 - 0.6 * math.exp(-0.3 * l)
            dl = smallS[:, 644:900]
            tt("dve", LAMt[:, 0:64], dl[:, 0:64], dl[:, 64:128], ALU.mult, (R_small,), (R_lam,))
            tt("dve", LAMt[:, 64:128], dl[:, 128:192], dl[:, 192:256], ALU.mult, (R_small,), (R_lam,))
            S.op("dve", lambda e: e.tensor_reduce(SC[:, 8:10], LAMt[:].rearrange("p (a b) -> p a b", a=2), AX.X, ALU.add),
                 (R_lam,), (R_sc,))
            act(SC[:, 8:10], SC[:, 8:10], AF.Exp, (R_sc,), (R_sc,))
            tt("dve", SC[:, 10:11], SC[:, 9:10], SC[:, 8:9], ALU.subtract, (R_sc,), (R_sc,))
            ts("dve", SC[:, 10:11], SC[:, 10:11], -lam_init, ALU.add, (R_sc,), (R_sc,))
            ts("dve", DGt[:], smallS[:, 516:644], 1.0 - lam_init, ALU.mult, (R_small,), (R_dg,))
            S.barrier()

        Opv = Op[:].rearrange("p b (j c) -> p (b j) c", j=2)

        def Oj(j):
            return Opv[:, j, 0:129]

        def phase_attn(l, pa):
            isS = (pa == 1)
            koff = 2 if isS else 0
            nkt = 10 if isS else 8
            if isS:
                seqs = [(list(range(8)), list(range(10)))]
            else:
                seqs = [([2 * s_, 2 * s_ + 1], [2 * s_, 2 * s_ + 1]) for s_ in range(4)]
            memset("dve", VA[:, :, :, 128:129], 1.0, (R_VA,))
            if l == 0 and pa == 0:
                ada_pending.extend((0, g) for g in range(8, 24))
            if l == 0 and pa == 1:
                ada_pending.extend((1, g) for g in range(24))

            def proj(c0, n, handler):
                wv, rw = w_get(("in", l, pa, c0))
                pend2 = carry[0]
                carry[0] = []
                for t_ in range(NT):
                    zb_, rz_ = bank3()
                    for k in range(KC):
                        mm(zb_[:, 0:n], Ht[:, k, t_ * 128:(t_ + 1) * 128], wv[:, k, :], k == 0, k == KC - 1,
                           (R_H, rw), (rz_,))
                    st2 = []
                    handler(t_, zb_, rz_, st2)
                    for f_ in pend2:
                        f_()
                    pend2 = st2
                carry[0] = pend2
                ada_pump(1)

            carry = [[]]

            def flush_carry():
                for f_ in carry[0]:
                    f_()
                carry[0] = []

            pidx = [0]

            def zs_load(zp, rz, n):
                zi = nxt("zs", 3)
                cp("act", ZS[zi][:, 0:n], zp[:, 0:n], (rz,), (R_ZS[zi],))
                return zi

            def rope(t_, src, dst, W, hw, rsrc, wdst):
                G = W // (4 * hw)
                ctab = (rc32 if hw == 32 else rc16)[:, t_]
                stab = (rs32 if hw == 32 else rs16)[:, t_]
                cb = ctab.unsqueeze(1).broadcast_to([128, G, 2, hw])
                sb_ = stab.unsqueeze(1).broadcast_to([128, G, 2, hw])
                s5 = src.rearrange("p (g b h f) -> p g b h f", g=G, b=2, h=2)
                d5 = dst.rearrange("p (g b h f) -> p g b h f", g=G, b=2, h=2)
                x1 = s5[:, :, :, 0, :]
                x2 = s5[:, :, :, 1, :]
                en, ta, tb, ra, rb_ = "dve", ZT[0], ZT[1], R_ZT[0], R_ZT[1]
                t1 = ta[:, 0:W // 2].rearrange("p (g b f) -> p g b f", g=G, b=2)
                t2 = tb[:, 0:W // 2].rearrange("p (g b f) -> p g b f", g=G, b=2)
                rd = tuple(rsrc) + (R_const,)
                tt(en, t1, x1, cb, ALU.mult, rd, (ra,))
                tt(en, t2, x2, sb_, ALU.mult, rd, (rb_,))
                tt(en, d5[:, :, :, 0, :], t1, t2, ALU.subtract, (ra, rb_), wdst)
                tt(en, t1, x2, cb, ALU.mult, rd, (ra,))
                tt(en, t2, x1, sb_, ALU.mult, rd, (rb_,))
                tt(en, d5[:, :, :, 1, :], t1, t2, ALU.add, (ra, rb_), wdst)

            cpe = [0]

            def cpalt():
                cpe[0] ^= 1
                return "act" if cpe[0] else "dve"

            def tr_to(zb, n, dst3, wres):
                for j in range(n):
                    tr(TRp[:, j * 128:(j + 1) * 128], ZB[zb][:, j * 128:(j + 1) * 128], identB[:],
                       (R_ZB[zb], R_const), (R_TR,))
                cp(cpalt(), dst3, TRp[:, 0:n * 128].rearrange("p (j t) -> p j t", j=n), (R_TR,), (wres,))

            def kq_path(t_, zi, c0, w, hw, dst3, wres, st2):
                zb = nxt("zb", 4)
                src = ZS[zi][:, c0:c0 + w]
                if isS and hw is not None:
                    rope(t_, src, ZB[zb][:, 0:w], w, hw, (R_ZS[zi],), (R_ZB[zb],))
                else:
                    cp("dve", ZB[zb][:, 0:w], src, (R_ZS[zi],), (R_ZB[zb],))
                st2.append(lambda: tr_to(zb, w // 128, dst3, wres))

            def v_path(t_, zi, c0, G):
                cp("dve", VA[:, koff + t_, 0:G, 0:128], ZS[zi][:, c0:c0 + G * 128].rearrange("p (g d) -> p g d", g=G),
                   (R_ZS[zi],), (R_VA,))

            def out_heads(t_, zi, c0, G, dram):
                s_ = t_ // 2
                r0 = (t_ % 2) * 128
                S.dma("sp", dram[s_, l, :, r0:r0 + 128, :].rearrange("g t d -> t g d"),
                      ZS[zi][:, c0:c0 + G * 128].rearrange("p (g d) -> p g d", g=G), (R_ZS[zi],), (), is_out=True)

            def tok(t_):
                return slice(t_ * 128, (t_ + 1) * 128)

            def ktok(t_):
                return slice((koff + t_) * 128, (koff + t_ + 1) * 128)

            def ctx_k_heads(dram_k, G):
                for kt in range(2):
                    S.dma("pool", CST[:, kt, 0:G, :], dram_k[l][:, kt * 128:(kt + 1) * 128, :].rearrange("g p d -> p g d"),
                          (), (R_CST,))
                for kt in range(2):
                    for g in range(G):
                        tr(TRp[:, (kt * G + g) * 128:(kt * G + g + 1) * 128], CST[:, kt, g, :], identB[:],
                           (R_CST, R_const), (R_TR,))
                cp(cpalt(), KA[:, 0:G, 0:256].rearrange("p g (k t) -> p k g t", k=2),
                   TRp[:, 0:2 * G * 128].rearrange("p (k g t) -> p k g t", k=2, g=G), (R_TR,), (R_KA,))

            def ctx_v_heads(dram_v, G):
                for kt in range(2):
                    S.dma("pool", VA[:, kt, 0:G, 0:128], dram_v[l][:, kt * 128:(kt + 1) * 128, :].rearrange("g p d -> p g d"),
                          (), (R_VA,))

            fin_pend = [None]

            def run_attn(hlist, parts, vap, scale, fin, banded=False):
                for (qt_list, kt_list) in seqs:
                    if banded:
                        chunks = [[q] for q in qt_list]
                    else:
                        chunks = [qt_list[i:i + 4] for i in range(0, len(qt_list), 4)]
                    for hh in hlist:
                        for ch in chunks:
                            q0 = ch[0] * 128
                            w = len(ch) * 128
                            if banded:
                                i = ch[0]
                                kts = [(0, None), (1, None)]
                                for j in (i - 1, i, i + 1):
                                    if 0 <= j < 8:
                                        kts.append((2 + j, 0 if j == i - 1 else (1 if j == i + 1 else None)))
                            else:
                                kts = [(k, None) for k in kt_list]
                            def pv(n_, kt, p_, nk=len(kts), nch=len(ch), hh=hh):
                                for j in range(nch):
                                    mm(Oj(j), PT[p_][:, j * 128:(j + 1) * 128], vap(hh, kt), n_ == 0 and j % 2 == 0,
                                       n_ == nk - 1, (R_PT[p_], R_VA), (R_O,), sgc=True)
                            pend = None
                            for n_, (kt, mk) in enumerate(kts):
                                s_ = nxt("st", 2)
                                pl = parts(hh, kt, q0, w)
                                for pi, (lh, rh) in enumerate(pl):
                                    mm(STp[s_][:, 0:w], lh, rh, pi == 0, pi == len(pl) - 1,
                                       (R_KA, R_KB, R_QA, R_QB), (R_ST[s_],))
                                p_ = nxt("pt", 3)
                                act(PT[p_][:, 0:w], STp[s_][:, 0:w], AF.Exp, (R_ST[s_],), (R_PT[p_],), scale=scale)
                                if mk is not None:
                                    tt("dve", PT[p_][:, 0:128], PT[p_][:, 0:128], maskB[:, mk * 128:(mk + 1) * 128],
                                       ALU.mult, (R_PT[p_], R_const), (R_PT[p_],))
                                if n_ == 1 and fin_pend[0] is not None:
                                    fin_pend[0]()
                                    fin_pend[0] = None
                                if pend is not None:
                                    pv(*pend)
                                pend = (n_, kt, p_)
                            pv(*pend)
                            if fin_pend[0] is not None:
                                fin_pend[0]()
                            fin_pend[0] = fin(hh, ch)
                        ada_pump(1)
                if fin_pend[0] is not None:
                    fin_pend[0]()
                    fin_pend[0] = None

            def fin_std(e_of, sink):
                def f(hh, ch):
                    n = len(ch)
                    q0 = ch[0] * 128
                    w = n * 128
                    ob = nxt("ob", 2)
                    if sink:
                        ts("dve", SC[:, 32:32 + n], Opv[:, 0:n, 128], SC[:, hh:hh + 1], ALU.add, (R_O, R_sc), (R_sc,))
                    else:
                        cp("dve", SC[:, 32:32 + n], Opv[:, 0:n, 128], (R_O,), (R_sc,))
                    recip(SC[:, 32:32 + n], SC[:, 32:32 + n], (R_sc,), (R_sc,))
                    tt("dve", OB[ob][:, 0:n, :], Opv[:, 0:n, 0:128],
                       SC[:, 32:32 + n].unsqueeze(2).broadcast_to([128, n, 128]), ALU.mult, (R_O, R_sc), (R_OB[ob],))
                    def s2():
                        for j in range(n):
                            tr(TRp[:, j * 128:(j + 1) * 128], OB[ob][:, j, :], identB[:], (R_OB[ob], R_const), (R_TR,))
                        cp(cpalt(), OTt[:, e_of(hh), q0:q0 + w], TRp[:, 0:w], (R_TR,), (R_OT,))
                    return s2
                return f

            def hA1(t_, zp, rz, st2):
                zi = zs_load(zp, rz, 512)
                kq_path(t_, zi, 0, 512, 32, QA[:, 0:4, tok(t_)], R_QA, st2)

            def hA2(t_, zp, rz, st2):
                zi = zs_load(zp, rz, 512)
                if not isS:
                    out_heads(t_, zi, 0, 2, nak)
                    out_heads(t_, zi, 256, 2, nav)
                kq_path(t_, zi, 0, 256, 32, KA[:, 0:2, ktok(t_)], R_KA, st2)
                v_path(t_, zi, 256, 2)

            mark('A_proj_%d%d' % (l, pa))
            proj(O_AQ, 512, hA1)
            proj(O_AKV, 512, hA2)
            flush_carry()
            if isS:
                ctx_k_heads(cak, 2)
                ctx_v_heads(cav, 2)
            mark('A_attn_%d%d' % (l, pa))
            run_attn(list(range(4)),
                     lambda h, kt, q0, w: [(KA[:, h // 2, kt * 128:(kt + 1) * 128], QA[:, h, q0:q0 + w])],
                     lambda h, kt: VA[:, kt, h // 2, 0:129], 128.0 ** -0.5, fin_std(lambda h: h, True), banded=isS)

            def hB1(t_, zp, rz, st2):
                zi = zs_load(zp, rz, 512)
                kq_path(t_, zi, 0, 512, None, QA[:, 0:4, tok(t_)], R_QA, st2)

            def rms_rows(srcv, G, width, ssl, gain_bc, rsrc, wres):
                act(SQ2[:, 0:G * width].rearrange("p (g d) -> p g d", g=G), srcv, AF.Square, rsrc, (R_SQ2,))
                S.op("dve", lambda e: e.tensor_reduce(SC[:, ssl], SQ2[:, 0:G * width].rearrange("p (g d) -> p g d", g=G),
                                                      AX.X, ALU.add), (R_SQ2,), (R_sc,))
                act(SC[:, ssl], SC[:, ssl], AF.Sqrt, (R_sc, R_const), (R_sc,), bias=epsT[:, 0:1], scale=1.0 / width)
                recip(SC[:, ssl], SC[:, ssl], (R_sc,), (R_sc,))
                tt("dve", srcv, srcv, SC[:, ssl].unsqueeze(2).broadcast_to([128, G, width]), ALU.mult,
                   tuple(rsrc) + (R_sc,), wres)
                tt("dve", srcv, srcv, gain_bc, ALU.mult, tuple(rsrc) + (R_small,), wres)

            def hB2(t_, zp, rz, st2):
                zi = zs_load(zp, rz, 512)
                kq_path(t_, zi, 0, 256, 16, QB[:, 0:2, tok(t_)], R_QB, st2)
                cv = ZS[zi][:, 256:512].rearrange("p (g d) -> p g d", g=1)
                rms_rows(cv, 1, 256, slice(20, 21), smallS[:, 4:260].rearrange("p (g d) -> p g d", g=1),
                         (R_ZS[zi],), (R_ZS[zi],))
                if not isS:
                    s_ = t_ // 2
                    r0 = (t_ % 2) * 128
                    S.dma("sp", nmla[s_, l, r0:r0 + 128, 0:256], ZS[zi][:, 256:512], (R_ZS[zi],), (), is_out=True)
                zb = nxt("zb", 4)
                cp("dve", ZB[zb][:, 0:256], ZS[zi][:, 256:512], (R_ZS[zi],), (R_ZB[zb],))
                st2.append(lambda: tr_to(zb, 2, CL[:, 0:2, ktok(t_)], R_CL))

            def hB3(t_, zp, rz, st2):
                zi = zs_load(zp, rz, 64)
                if not isS:
                    s_ = t_ // 2
                    r0 = (t_ % 2) * 128
                    S.dma("sp", nmla[s_, l, r0:r0 + 128, 256:320], ZS[zi][:, 0:64], (R_ZS[zi],), (), is_out=True)
                zb = nxt("zb", 4)
                if isS:
                    rope(t_, ZS[zi][:, 0:64], ZB[zb][:, 0:64], 64, 16, (R_ZS[zi],), (R_ZB[zb],))
                else:
                    cp("dve", ZB[zb][:, 0:64], ZS[zi][:, 0:64], (R_ZS[zi],), (R_ZB[zb],))
                cp("dve", ZB[zb][:, 64:128], ZB[zb][:, 0:64], (R_ZB[zb],), (R_ZB[zb],))

                def s2():
                    tr(TRp[:, 0:128], ZB[zb][:, 0:128], identB[:], (R_ZB[zb], R_const), (R_TR,))
                    cp(cpalt(), KB[:, ktok(t_)], TRp[:, 0:128], (R_TR,), (R_KB,))
                st2.append(s2)

            mark('B_proj_%d%d' % (l, pa))
            proj(O_BQN, 512, hB1)
            proj(O_BQRC, 512, hB2)
            proj(O_BKR, 64, hB3)
            flush_carry()
            if isS:
                S.dma("pool", CM[:], cmla[l].rearrange("(k p) c -> p k c", p=128), (), (R_CM,))
                for kt in range(2):
                    for rc in range(2):
                        tr(TRp[:, (kt * 2 + rc) * 128:(kt * 2 + rc + 1) * 128], CM[:, kt, rc * 128:(rc + 1) * 128],
                           identB[:], (R_CM, R_const), (R_TR,))
                cp(cpalt(), CL[:, 0:2, 0:256].rearrange("p r (k t) -> p k r t", k=2),
                   TRp[:, 0:512].rearrange("p (k r t) -> p k r t", k=2, r=2), (R_TR,), (R_CL,))
                for kt in range(2):
                    zb = nxt("zb", 4)
                    cp("dve", ZB[zb][:, 0:64], CM[:, kt, 256:320], (R_CM,), (R_ZB[zb],))
                    cp("dve", ZB[zb][:, 64:128], CM[:, kt, 256:320], (R_CM,), (R_ZB[zb],))
                    tr(TRp[:, 0:128], ZB[zb][:, 0:128], identB[:], (R_ZB[zb], R_const), (R_TR,))
                    cp(cpalt(), KB[:, kt * 128:(kt + 1) * 128], TRp[:, 0:128], (R_TR,), (R_KB,))
            mark('B_expand_%d%d' % (l, pa))
            nkeys = nkt * 128
            for h in range(4):
                k0 = 0
                while k0 < nkeys:
                    w = min(512, nkeys - k0)
                    z = nxt("z", 2)
                    for rc in range(2):
                        mm(Zp[z][:, 0:w], wukS[:, rc, h * 128:(h + 1) * 128], CL[:, rc, k0:k0 + w], rc == 0, rc == 1,
                           (R_wu, R_CL), (R_Z[z],))
                    cp(cpalt(), KA[:, h, k0:k0 + w], Zp[z][:, 0:w], (R_Z[z],), (R_KA,))
                    k0 += w
            for kt in range(nkt):
                z = nxt("z", 2)
                for rc in range(2):
                    mm(Zp[z][:], CL[:, rc, kt * 128:(kt + 1) * 128], wuvS[:, rc, :], rc == 0, rc == 1,
                       (R_wu, R_CL), (R_Z[z],))
                cp(cpalt(), VA[:, kt, 0:4, 0:128], Zp[z][:].rearrange("p (g d) -> p g d", g=4), (R_Z[z],), (R_VA,))

            def partsB(h, kt, q0, w):
                hh_ = h % 2
                return [(KA[:, h, kt * 128:(kt + 1) * 128], QA[:, h, q0:q0 + w]),
                        (KB[64 * hh_:64 * hh_ + 64, kt * 128:(kt + 1) * 128], QB[64 * hh_:64 * hh_ + 64, h // 2, q0:q0 + w])]

            mark('B_attn_%d%d' % (l, pa))
            run_attn(list(range(4)), partsB, lambda h, kt: VA[:, kt, h, 0:129], 192.0 ** -0.5,
                     fin_std(lambda h: 4 + h, False))

            def hC1(t_, zp, rz, st2):
                zi = zs_load(zp, rz, 512)
                rms_rows(ZS[zi][:, 0:512].rearrange("p (g d) -> p g d", g=4), 4, 128, slice(24, 28),
                         smallS[:, 260:388].unsqueeze(1).broadcast_to([128, 4, 128]), (R_ZS[zi],), (R_ZS[zi],))
                kq_path(t_, zi, 0, 512, 32, QA[:, 0:4, tok(t_)], R_QA, st2)

            def hC2(t_, zp, rz, st2):
                zi = zs_load(zp, rz, 512)
                rms_rows(ZS[zi][:, 0:256].rearrange("p (g d) -> p g d", g=2), 2, 128, slice(24, 26),
                         smallS[:, 388:516].unsqueeze(1).broadcast_to([128, 2, 128]), (R_ZS[zi],), (R_ZS[zi],))
                if not isS:
                    out_heads(t_, zi, 0, 2, nck)
                    out_heads(t_, zi, 256, 2, ncv)
                kq_path(t_, zi, 0, 256, 32, KA[:, 0:2, ktok(t_)], R_KA, st2)
                v_path(t_, zi, 256, 2)

            mark('C_proj_%d%d' % (l, pa))
            proj(O_CQ, 512, hC1)
            proj(O_CKV, 512, hC2)
            flush_carry()
            if isS:
                ctx_k_heads(cck, 2)
                ctx_v_heads(ccv, 2)
            mark('C_attn_%d%d' % (l, pa))
            run_attn(list(range(4)),
                     lambda h, kt, q0, w: [(KA[:, h // 2, kt * 128:(kt + 1) * 128], QA[:, h, q0:q0 + w])],
                     lambda h, kt: VA[:, kt, h // 2, 0:129], 128.0 ** -0.5, fin_std(lambda h: 8 + h, False))

            def hD1(t_, zp, rz, st2):
                zi = zs_load(zp, rz, 512)
                kq_path(t_, zi, 0, 512, 16, QA[:, 0:4, tok(t_)], R_QA, st2)

            def hD2(t_, zp, rz, st2):
                zi = zs_load(zp, rz, 512)
                if not isS:
                    out_heads(t_, zi, 0, 4, ndk)
                kq_path(t_, zi, 0, 512, 16, KA[:, 0:4, ktok(t_)], R_KA, st2)

            def hD3(t_, zp, rz, st2):
                zi = zs_load(zp, rz, 512)
                if not isS:
                    out_heads(t_, zi, 0, 4, ndv)
                v_path(t_, zi, 0, 4)

            mark('D_proj_%d%d' % (l, pa))
            proj(O_DQ, 512, hD1)
            proj(O_DK, 512, hD2)
            proj(O_DV, 512, hD3)
            flush_carry()
            if isS:
                ctx_k_heads(cdk, 4)
                ctx_v_heads(cdv, 4)

            def partsD(hm, kt, q0, w):
                h, m = hm
                return [(KA[64 * m:64 * m + 64, h, kt * 128:(kt + 1) * 128], QA[64 * m:64 * m + 64, h, q0:q0 + w])]

            def finD(hm, ch):
                h, m = hm
                n = len(ch)
                q0 = ch[0] * 128
                w = n * 128
                cp("dve", SC[:, 32:32 + n], Opv[:, 0:n, 128], (R_O,), (R_sc,))
                recip(SC[:, 32:32 + n], SC[:, 32:32 + n], (R_sc,), (R_sc,))
                rb = SC[:, 32:32 + n].unsqueeze(2).broadcast_to([128, n, 128])
                if m == 0:
                    tt("dve", O1S[:, 0:n, :], Opv[:, 0:n, 0:128], rb, ALU.mult, (R_O, R_sc), (R_O1S,))
                    return None
                dv_ = SQ2[:, 0:w].rearrange("p (j d) -> p j d", j=n)
                tt("dve", dv_, Opv[:, 0:n, 0:128], rb, ALU.mult, (R_O, R_sc), (R_SQ2,))
                stt("dve", dv_, dv_, SC[:, 10:11], O1S[:, 0:n, :], ALU.mult, ALU.add, (R_SQ2, R_sc, R_O1S), (R_SQ2,))
                sqv = ZS[0][:, 0:w].rearrange("p (j d) -> p j d", j=n)
                act(sqv, dv_, AF.Square, (R_SQ2,), (R_ZS[0],))
                S.op("dve", lambda e: e.tensor_reduce(SC[:, 40:40 + n], sqv, AX.X, ALU.add), (R_ZS[0],), (R_sc,))
                act(SC[:, 40:40 + n], SC[:, 40:40 + n], AF.Sqrt, (R_sc, R_const), (R_sc,), bias=epsT[:, 0:1], scale=1.0 / 128)
                recip(SC[:, 40:40 + n], SC[:, 40:40 + n], (R_sc,), (R_sc,))
                tt("dve", dv_, dv_, SC[:, 40:40 + n].unsqueeze(2).broadcast_to([128, n, 128]), ALU.mult,
                   (R_SQ2, R_sc), (R_SQ2,))
                ob = nxt("ob", 2)
                tt("dve", OB[ob][:, 0:n, :], dv_, DGt[:].unsqueeze(1).broadcast_to([128, n, 128]), ALU.mult,
                   (R_SQ2, R_dg), (R_OB[ob],))
                def s2():
                    for j in range(n):
                        tr(TRp[:, j * 128:(j + 1) * 128], OB[ob][:, j, :], identB[:], (R_OB[ob], R_const), (R_TR,))
                    cp(cpalt(), OTt[:, 12 + h, q0:q0 + w], TRp[:, 0:w], (R_TR,), (R_OT,))
                return s2

            def run_attn_D():
                for (qt_list, kt_list) in seqs:
                    chunks = [qt_list[i:i + 4] for i in range(0, len(qt_list), 4)]
                    for h in range(4):
                        for ch in chunks:
                            for m in range(2):
                                q0 = ch[0] * 128
                                w = len(ch) * 128
                                def pv(n_, kt, p_, nk=len(kt_list), nch=len(ch), h=h):
                                    for j in range(nch):
                                        mm(Oj(j), PT[p_][:, j * 128:(j + 1) * 128], VA[:, kt, h, 0:129],
                                           n_ == 0 and j % 2 == 0, n_ == nk - 1, (R_PT[p_], R_VA), (R_O,), sgc=True)
                                pend = None
                                for n_, kt in enumerate(kt_list):
                                    s_ = nxt("st", 2)
                                    (lh, rh), = partsD((h, m), kt, q0, w)
                                    mm(STp[s_][:, 0:w], lh, rh, True, True, (R_KA, R_QA), (R_ST[s_],))
                                    p_ = nxt("pt", 3)
                                    act(PT[p_][:, 0:w], STp[s_][:, 0:w], AF.Exp, (R_ST[s_],), (R_PT[p_],), scale=64.0 ** -0.5)
                                    if n_ == 1 and fin_pend[0] is not None:
                                        fin_pend[0]()
                                        fin_pend[0] = None
                                    if pend is not None:
                                        pv(*pend)
                                    pend = (n_, kt, p_)
                                pv(*pend)
                                if fin_pend[0] is not None:
                                    fin_pend[0]()
                                fin_pend[0] = finD((h, m), ch)
                        ada_pump(1)
                if fin_pend[0] is not None:
                    fin_pend[0]()
                    fin_pend[0] = None

            mark('D_attn_%d%d' % (l, pa))
            run_attn_D()
            ada_pump(len(ada_pending))
            S.barrier()

        def phase_wout(l, pa):
            ce = 0
            for cg in range(4):
                wv, rw = w_get(("out", l, pa, cg))
                for j in range(4):
                    dc = cg * 4 + j
                    for th in range(2):
                        zb_, rz_ = bank6()
                        for e_ in range(KC):
                            mm(zb_[:], wv[:, e_, j * 128:(j + 1) * 128], OTt[:, e_, th * 512:(th + 1) * 512],
                               e_ == 0, e_ == KC - 1, (rw, R_OT), (rz_,))
                        ce ^= 1
                        cp("act" if ce else "dve", BIGt[:, dc, th * 512:(th + 1) * 512], zb_[:], (rz_,), (R_BIG[dc],))
            S.barrier()

        def phase_mlp(l, pa):
            rt = [0]

            def up(fg):
                ab = fg % 2
                wu, ru = w_get(("up", l, pa, fg))
                for fc in range(4):
                    for th in range(2):
                        zb_, rz_ = bank6()
                        for k in range(KC):
                            mm(zb_[:], wu[:, k, fc * 128:(fc + 1) * 128], Ht[:, k, th * 512:(th + 1) * 512],
                               k == 0, k == KC - 1, (ru, R_H), (rz_,))
                        rt[0] ^= 1
                        r_ = rt[0]
                        act(RT[r_], zb_[:], AF.Relu, (rz_,), (R_RT[r_],))
                        tt("dve", ACTT[ab][:, fc, th * 512:(th + 1) * 512], RT[r_], RT[r_], ALU.mult,
                           (R_RT[r_],), (R_ACTT[ab],))

            def down(fg):
                ab = fg % 2
                wd, rd = w_get(("down", l, pa, fg))
                for dc in range(KC):
                    for th in range(2):
                        zb_, rz_ = bank6()
                        for fc in range(4):
                            mm(zb_[:], wd[:, fc, dc * 128:(dc + 1) * 128], ACTT[ab][:, fc, th * 512:(th + 1) * 512],
                               fc == 0, fc == 3, (rd, R_ACTT[ab]), (rz_,))
                        dst = BIGt[:, dc, th * 512:(th + 1) * 512]
                        if fg == 0:
                            cp("dve", dst, zb_[:], (rz_,), (R_BIG[dc],))
                        else:
                            tt("dve", dst, dst, zb_[:], ALU.add, (rz_, R_BIG[dc]), (R_BIG[dc],))

            up(0)
            for fg in range(16):
                if fg + 1 < 16:
                    up(fg + 1)
                down(fg)

        def phase_final(pa):
            rt = 0
            for t_ in range(NT):
                for cg in range(4):
                    z = nxt("z", 2)
                    for j in range(4):
                        c = cg * 4 + j
                        tr(Zp[z][:, j * 128:(j + 1) * 128], BIGt[:, c, t_ * 128:(t_ + 1) * 128], identF[:],
                           (R_BIG[c], R_const), (R_Z[z],))
                    rt ^= 1
                    cp("act" if rt else "dve", RT[rt], Zp[z][:], (R_Z[z],), (R_RT[rt],))
                    S.dma("sp", y_out[pa][t_ * 128:(t_ + 1) * 128, cg * 512:(cg + 1) * 512], RT[rt], (R_RT[rt],), (),
                          is_out=True)

        def mark(name):
            PHASES.append((name, dict(S.cnt)))
        mark('xt')
        phase_xt(0)
        phase_xt(1)
        for l in range(DEPTH):
            mark('ada%d' % l)
            phase_ada(l)
            for pa in range(2):
                mark('norm1_%d%d' % (l, pa))
                norm_from_scratch(pa, 0, 1)
                S.barrier()
                mark('attn_%d%d' % (l, pa))
                phase_attn(l, pa)
                if l == 0 and pa == 0:
                    par_rows(0, (2, 3, 4, 5))
                mark('wout_%d%d' % (l, pa))
                phase_wout(l, pa)
                mark('post1_%d%d' % (l, pa))
                post_residual(pa, 2, True)
                norm_from_big(pa, 3, 4)
                mark('mlp_%d%d' % (l, pa))
                phase_mlp(l, pa)
                mark('post2_%d%d' % (l, pa))
                post_residual(pa, 5, l == 0)
                if l == DEPTH - 1:
                    phase_final(pa)
                S.barrier()
        mark('end')
        if plan is None:
            return wkeys
        assert wstate["next_get"] == len(wseq), (wstate, len(wseq))

        S.plan_signals()
        nep = {e: (S.nsig[e] + EPOCH - 1) // EPOCH for e in ENGS}
        sems = {}
        for e in ENGS:
            for ep in range(max(1, nep[e])):
                sems[("e", e, ep)] = es.enter_context(nc.semaphore("s_%s_%d" % (e, ep)))
        for j in range(NDSEM):
            sems[("d", j)] = es.enter_context(nc.semaphore("d_%d" % j))
        block = es.enter_context(nc.Block())

        @block.tensor
        def _(e):
            S.emit("pe", e, sems)

        @block.scalar
        def _(e):
            S.emit("act", e, sems)

        @block.vector
        def _(e):
            S.emit("dve", e, sems)

        @block.gpsimd
        def _(e):
            S.emit("pool", e, sems)

        @block.sync
        def _(e):
            S.emit("sp", e, sems)
            for j in range(NDSEM):
                if S.dma_uses[j] > 0:
                    e.wait_ge(sems[("d", j)], 16 * S.dma_uses[j])
    return nc


_CACHE = {}


def kernel(x_prompt, x_sample, cache_a_k, cache_a_v, cache_mla, cache_c_k, cache_c_v, cache_d_k, cache_d_v,
           c, c_ctx, w_ada, b_ada, g_attn_pre, g_attn_post, g_mlp_pre, g_mlp_post, w_in, a_sink,
           b_g_kv, b_w_uk, b_w_uv, c_gq, c_gk, d_lam, d_g_out, w_out, w_up, w_down):
    f = lambda a: np.ascontiguousarray(np.asarray(a, dtype=np.float32))
    x_prompt, x_sample = f(x_prompt), f(x_sample)
    if "nc" not in _CACHE:
        _CACHE["nc"] = build()
    nc = _CACHE["nc"]
    c32, s32, c16, s16 = _rope_tables()
    ident = np.eye(128, dtype=np.float32)
    masks = np.concatenate([np.tril(np.ones((128, 128), np.float32)), np.triu(np.ones((128, 128), np.float32))], axis=1)
    gT = np.stack([f(g_attn_pre), f(g_attn_post), f(g_mlp_pre), f(g_mlp_post)], axis=1)
    gT = np.ascontiguousarray(gT.reshape(2, 4, KC, 128).transpose(3, 0, 1, 2))
    bT = np.ascontiguousarray(f(b_ada).reshape(2, 96, 128).transpose(2, 0, 1))
    small = np.concatenate([f(a_sink), f(b_g_kv), f(c_gq), f(c_gk), f(d_g_out), f(d_lam).reshape(2, 256)], axis=1)
    small = np.ascontiguousarray(np.broadcast_to(small[None], (128, 2, 900)))
    shared = {
        "w_ada": f(w_ada), "w_in": f(w_in), "w_out": f(w_out), "w_up": f(w_up), "w_down": f(w_down),
        "wuk": f(b_w_uk).reshape(2, 256, 512), "wuv": f(b_w_uv).reshape(2, 256, 512),
        "gT": gT, "bT": bT, "small": small, "ident": ident, "masks": masks,
        "rc32": c32, "rs32": s32, "rc16": c16, "rs16": s16,
    }
    cc, cctx = f(c), f(c_ctx)
    in_maps = []
    for b in range(NCORES):
        cT = np.stack([cctx, cc[b]], axis=-1).reshape(KC, 128, 2).transpose(1, 0, 2)
        m = dict(shared)
        m.update({
            "xp": x_prompt[4 * b:4 * b + 4].reshape(T, D), "xs": x_sample[b],
            "cak": f(cache_a_k[b]), "cav": f(cache_a_v[b]), "cmla": f(cache_mla[b]),
            "cck": f(cache_c_k[b]), "ccv": f(cache_c_v[b]), "cdk": f(cache_d_k[b]), "cdv": f(cache_d_v[b]),
            "cT": np.ascontiguousarray(cT),
        })
        in_maps.append(m)
    res = run_bass_kernel_spmd(nc, in_maps, core_ids=list(range(NCORES)))
    R = res.results
    cat = lambda k: np.concatenate([np.asarray(r[k], dtype=np.float32) for r in R], axis=0)
    y_prompt = cat("yp").reshape(32, 256, D)
    y_sample = np.stack([np.asarray(r["ys"], dtype=np.float32) for r in R], axis=0)
    return (y_prompt, y_sample, cat("nak"), cat("nav"), cat("nmla"), cat("nck"), cat("ncv"), cat("ndk"), cat("ndv"))
```
